# Optimizing a Trainium2 kernel written in Bass

```python
import jax, jax.numpy as jnp
from jax import lax
import numpy as np

D_MODEL = 1024
BATCH = 4
SEQ = 8192
DEPTH = 1
DEC_BATCH = 8
DEC_SEQ = 32
PAST_LEN = 1024

CHUNK = 64
D_POOL = D_MODEL // 2
POOL_WINDOWS = (2, 4, 8, 16)
N_POOL_GROUPS = len(POOL_WINDOWS)
POOL_GROUP = D_POOL // N_POOL_GROUPS
POOL_STATE = max(POOL_WINDOWS) - 1
SB_HEADS = 8
SB_HEAD_DIM = 64
D_SB = SB_HEADS * SB_HEAD_DIM
Q_BLOCK = 128
D_FF = 4 * D_MODEL
RMS_EPS = 1e-6
D_IN = D_POOL + 3 * D_SB + 2 * D_MODEL
SPLITS = [D_POOL, D_POOL + D_SB, D_POOL + 2 * D_SB, D_POOL + 3 * D_SB, D_POOL + 3 * D_SB + D_MODEL]

kernel_name = "hybrid_pool_stickbreaking_streaming_step"


def rmsnorm(x, g):
    xf = x.astype(jnp.float32)
    xf = xf * lax.rsqrt(jnp.mean(xf * xf, axis=-1, keepdims=True) + RMS_EPS)
    return xf.astype(x.dtype) * g


def swiglu(x, w_gate, w_up, w_down):
    return (jax.nn.silu(x @ w_gate) * (x @ w_up)) @ w_down


def pool_mix(u, prefix, pos, w_grp, scale):
    b, t, _ = u.shape
    ext = jnp.concatenate([prefix, u], axis=1).astype(jnp.float32)
    cs = jnp.concatenate([jnp.zeros((b, 1, D_POOL), jnp.float32), jnp.cumsum(ext, axis=1)], axis=1)
    end = cs[:, POOL_STATE + 1:]
    outs = []
    for gi, w in enumerate(POOL_WINDOWS):
        lo, hi = gi * POOL_GROUP, (gi + 1) * POOL_GROUP
        start = cs[:, POOL_STATE + 1 - w: POOL_STATE + 1 - w + t, lo:hi]
        cnt = jnp.minimum(w, pos + 1).astype(jnp.float32)[None, :, None]
        outs.append((end[..., lo:hi] - start) / cnt)
    mean = jnp.concatenate(outs, axis=-1)
    d = (mean - u.astype(jnp.float32)).astype(u.dtype).reshape(b, t, N_POOL_GROUPS, POOL_GROUP)
    y = jnp.einsum('btgc,gce->btge', d, w_grp).reshape(b, t, D_POOL)
    return y * scale


def stick_breaking(q, k, v, q_pos, k_pos):
    z = jnp.einsum('bhqd,bhkd->bhqk', q, k).astype(jnp.float32) * (SB_HEAD_DIM ** -0.5)
    mask = k_pos[None, :] < q_pos[:, None]
    log_surv = jnp.where(mask, jax.nn.log_sigmoid(-z), 0.0)
    after = lax.cumsum(log_surv, axis=3, reverse=True) - log_surv
    w = jnp.where(mask, jnp.exp(jax.nn.log_sigmoid(z) + after), 0.0)
    return jnp.einsum('bhqk,bhkd->bhqd', w, v.astype(jnp.float32)).astype(v.dtype)


def stick_breaking_prompt(q, k, v):
    b, h, s, hd = q.shape
    k_pos = jnp.arange(s)

    def one_block(i):
        s0 = i * Q_BLOCK
        qb = lax.dynamic_slice_in_dim(q, s0, Q_BLOCK, axis=2)
        return stick_breaking(qb, k, v, s0 + jnp.arange(Q_BLOCK), k_pos)

    o = lax.map(one_block, jnp.arange(s // Q_BLOCK))
    return o.transpose(1, 2, 0, 3, 4).reshape(b, h, s, hd)


def heads(t):
    b, n, _ = t.shape
    return t.reshape(b, n, SB_HEADS, SB_HEAD_DIM).transpose(0, 2, 1, 3)


def layer_forward(x, pos, pool_prefix, k_past, v_past,
                  ffn1_norm, ffn1_gate, ffn1_up, ffn1_down, mix_norm, w_in, pool_w, pool_scale,
                  w_branch_pool, w_branch_sb, w_out, ffn2_norm, ffn2_gate, ffn2_up, ffn2_down):
    b, t, _ = x.shape
    x = x + 0.5 * swiglu(rmsnorm(x, ffn1_norm), ffn1_gate, ffn1_up, ffn1_down)
    h = rmsnorm(x, mix_norm)
    u, q, k, v, g_a, g_b = jnp.split(h @ w_in, SPLITS, axis=-1)
    a = pool_mix(u, pool_prefix, pos, pool_w, pool_scale)
    qh, kh, vh = heads(q), heads(k), heads(v)
    if k_past is None:
        o = stick_breaking_prompt(qh, kh, vh)
    else:
        k_all = jnp.concatenate([k_past, kh], axis=2)
        v_all = jnp.concatenate([v_past, vh], axis=2)
        o = stick_breaking(qh, k_all, v_all, pos, jnp.arange(k_all.shape[2]))
    o = o.transpose(0, 2, 1, 3).reshape(b, t, D_SB)
    merged = jax.nn.sigmoid(g_a) * (a @ w_branch_pool) + jax.nn.sigmoid(g_b) * (o @ w_branch_sb)
    x = x + merged @ w_out
    x = x + 0.5 * swiglu(rmsnorm(x, ffn2_norm), ffn2_gate, ffn2_up, ffn2_down)
    new_pool = jnp.concatenate([pool_prefix, u], axis=1)[:, -POOL_STATE:]
    return x, kh, vh, new_pool


def setup_inputs(seed: int = 0) -> dict:
    key = jax.random.key(seed)
    ks = jax.random.split(key, 24)
    f32 = jnp.float32

    def nrm(k, shape, fan_in):
        return jax.random.normal(k, shape, f32) * (fan_in ** -0.5)

    def gain(k, shape):
        return 1.0 + 0.02 * jax.random.normal(k, shape, f32)

    L = DEPTH
    return {
        "x_prompt": jax.random.normal(ks[0], (BATCH, SEQ, D_MODEL), f32),
        "x_sample": jax.random.normal(ks[1], (DEC_BATCH, DEC_SEQ, D_MODEL), f32),
        "cache_k": jax.random.normal(ks[2], (L, DEC_BATCH, SB_HEADS, PAST_LEN, SB_HEAD_DIM), f32),
        "cache_v": jax.random.normal(ks[3], (L, DEC_BATCH, SB_HEADS, PAST_LEN, SB_HEAD_DIM), f32),
        "state_pool": jax.random.normal(ks[4], (L, DEC_BATCH, POOL_STATE, D_POOL), f32),
        "ffn1_norm": gain(ks[5], (L, D_MODEL)),
        "ffn1_gate": nrm(ks[6], (L, D_MODEL, D_FF), D_MODEL),
        "ffn1_up": nrm(ks[7], (L, D_MODEL, D_FF), D_MODEL),
        "ffn1_down": nrm(ks[8], (L, D_FF, D_MODEL), D_FF),
        "mix_norm": gain(ks[9], (L, D_MODEL)),
        "w_in": nrm(ks[10], (L, D_MODEL, D_IN), D_MODEL),
        "pool_w": nrm(ks[11], (L, N_POOL_GROUPS, POOL_GROUP, POOL_GROUP), POOL_GROUP),
        "pool_scale": 1.0 + 0.1 * jax.random.normal(ks[12], (L, D_POOL), f32),
        "w_branch_pool": nrm(ks[13], (L, D_POOL, D_MODEL), D_POOL),
        "w_branch_sb": nrm(ks[14], (L, D_SB, D_MODEL), D_SB),
        "w_out": nrm(ks[15], (L, D_MODEL, D_MODEL), D_MODEL),
        "ffn2_norm": gain(ks[16], (L, D_MODEL)),
        "ffn2_gate": nrm(ks[17], (L, D_MODEL, D_FF), D_MODEL),
        "ffn2_up": nrm(ks[18], (L, D_MODEL, D_FF), D_MODEL),
        "ffn2_down": nrm(ks[19], (L, D_FF, D_MODEL), D_FF),
        "final_norm": gain(ks[20], (D_MODEL,)),
    }


def reference(x_prompt, x_sample, cache_k, cache_v, state_pool,
              ffn1_norm, ffn1_gate, ffn1_up, ffn1_down, mix_norm, w_in, pool_w, pool_scale,
              w_branch_pool, w_branch_sb, w_out, ffn2_norm, ffn2_gate, ffn2_up, ffn2_down, final_norm):
    b_p, s_p, _ = x_prompt.shape
    s_d = x_sample.shape[1]
    pos_p = jnp.arange(s_p)
    pos_d = PAST_LEN + jnp.arange(s_d)
    xp, xd = x_prompt, x_sample
    kp_l, vp_l, pp_l, kd_l, vd_l, pd_l = [], [], [], [], [], []
    for l in range(DEPTH):
        w = (ffn1_norm[l], ffn1_gate[l], ffn1_up[l], ffn1_down[l], mix_norm[l], w_in[l], pool_w[l],
             pool_scale[l], w_branch_pool[l], w_branch_sb[l], w_out[l], ffn2_norm[l], ffn2_gate[l],
             ffn2_up[l], ffn2_down[l])
        zero_prefix = jnp.zeros((b_p, POOL_STATE, D_POOL), xp.dtype)
        xp, kp, vp, pp = layer_forward(xp, pos_p, zero_prefix, None, None, *w)
        xd, kd, vd, pd = layer_forward(xd, pos_d, state_pool[l], cache_k[l], cache_v[l], *w)
        kp_l.append(kp); vp_l.append(vp); pp_l.append(pp)
        kd_l.append(kd); vd_l.append(vd); pd_l.append(pd)
    y_prompt = rmsnorm(xp, final_norm)
    y_sample = rmsnorm(xd, final_norm)
    new_k_prompt = jnp.stack(kp_l, axis=0)
    new_v_prompt = jnp.stack(vp_l, axis=0)
    new_pool_prompt = jnp.stack(pp_l, axis=0)
    new_k_sample = jnp.stack(kd_l, axis=0)
    new_v_sample = jnp.stack(vd_l, axis=0)
    new_pool_sample = jnp.stack(pd_l, axis=0)
    return (y_prompt, y_sample, new_k_prompt, new_v_prompt, new_pool_prompt, new_k_sample, new_v_sample, new_pool_sample)
```

```python
import numpy as np
import ml_dtypes
from contextlib import ExitStack
import concourse.bass as bass
import concourse.mybir as mybir
from concourse.bass_utils import run_bass_kernel_spmd

F32 = mybir.dt.float32
BF16 = mybir.dt.bfloat16
AF = mybir.ActivationFunctionType
ALU = mybir.AluOpType

D = 1024
DFF = 4096
NB = 65
OWN = [0] + [x for m in range(16) for x in (4 * m + 3, 4 * m + 4)]
SLOT = {b: i for i, b in enumerate(OWN)}
NS = len(OWN)
NQC = NS * 128 + 32
DEC0 = NS * 128
NPAGE = 8
LOOKAHEAD = NPAGE - 1
MASKV = -30000.0
WINS = (2, 4, 8, 16)


class Buf:
    __slots__ = ("w", "rc", "rd", "name")

    def __init__(self, name=""):
        self.w = None
        self.rc = {}
        self.rd = []
        self.name = name


class DSem:
    def __init__(self, h):
        self.h = h
        self.v = 0
        self.buf = Buf("dsem")


class Op:
    __slots__ = ("eng", "fn", "deps", "needed", "val", "dsem", "phase")


class Sched:
    ENG = ("pe", "act", "dve", "pool", "sp")
    COMPUTE = ("pe", "act", "dve", "pool")

    def __init__(self, nc, csem):
        self.nc = nc
        self.csem = csem
        self.q = {e: [] for e in self.ENG}
        self.cnt = {e: 0 for e in self.COMPUTE}
        self.known = {}
        self.phase = 0
        self.dsems = []

    def dsem(self, h):
        d = DSem(h)
        self.dsems.append(d)
        return d

    def op(self, eng, fn, reads=(), writes=(), dsem=None):
        o = Op()
        o.eng = eng
        o.fn = fn
        o.needed = False
        o.val = None
        o.dsem = dsem
        o.phase = self.phase
        deps = []
        if dsem is not None:
            writes = list(writes) + [dsem.buf]
        for b in reads:
            if b.w is not None:
                deps.append(b.w)
        for b in writes:
            if b.w is not None:
                deps.append(b.w)
            deps.extend(b.rc.values())
            deps.extend(b.rd)
        ph = self.phase
        deps = [d for d in deps if d[-1] == ph]
        o.deps = deps
        for d in deps:
            if d[0] == "c":
                d[1].needed = True
        if dsem is not None:
            dsem.v += 16
            tok = ("d", dsem, dsem.v, ph)
        else:
            tok = ("c", o, ph)
        for b in reads:
            if tok[0] == "c":
                b.rc[eng] = tok
            else:
                b.rd.append(tok)
        for b in writes:
            b.w = tok
            b.rc = {}
            b.rd = []
        self.q[eng].append(o)
        return tok

    def _emit(self, e, eng):
        known = self.known
        for o in self.q[eng]:
            need = {}
            for d in o.deps:
                if d[0] == "c":
                    src = d[1]
                    if src.eng == eng and eng == "pe":
                        continue
                    key = ("c", src.eng)
                    val = src.val
                    sem = self.csem[src.eng]
                else:
                    key = ("d", id(d[1]))
                    val = d[2]
                    sem = d[1].h
                if known.get((eng, key), 0) >= val:
                    continue
                if key not in need or need[key][1] < val:
                    need[key] = (sem, val)
            for key, (sem, val) in need.items():
                e.wait_ge(sem, val)
                known[(eng, key)] = val
            ins = o.fn(e)
            if o.dsem is not None:
                ins.then_inc(o.dsem.h, 16)
            elif o.needed:
                ins.then_inc(self.csem[eng], 1)

    def flush(self):
        nc = self.nc
        for eng in self.COMPUTE:
            for o in self.q[eng]:
                if o.needed:
                    self.cnt[eng] += 1
                    o.val = self.cnt[eng]
        with nc.Block() as block:
            @block.tensor
            def _(e):
                self._emit(e, "pe")

            @block.scalar
            def _(e):
                self._emit(e, "act")

            @block.vector
            def _(e):
                self._emit(e, "dve")

            @block.gpsimd
            def _(e):
                self._emit(e, "pool")

            @block.sync
            def _(e):
                self._emit(e, "sp")
                for d in self.dsems:
                    if d.v > 0 and self.known.get(("sp", ("d", id(d))), 0) < d.v:
                        e.wait_ge(d.h, d.v)
                        self.known[("sp", ("d", id(d)))] = d.v
        self.q = {e: [] for e in self.ENG}
        self.phase += 1


class Ring:
    def __init__(self, tiles):
        self.t = tiles
        self.b = [Buf() for _ in tiles]
        self.i = -1

    def next(self):
        self.i = (self.i + 1) % len(self.t)
        return self.t[self.i], self.b[self.i]

    def cur(self):
        return self.t[self.i], self.b[self.i]


class Builder:
    def __init__(self, dbg_tiles=None, phases=(1, 2, 3)):
        self.nc = bass.Bass("TRN2", target_bir_lowering=False)
        self.es = ExitStack()
        self.dbg_tiles = dbg_tiles
        self.phases = phases
        self.skip = set()
        self.dbg_stage = 99
        self.dbg_slots = [0, 1]
        self.dbg_tiles3 = [0, 8]

    def dram(self, name, shape, dt, kind):
        return self.nc.dram_tensor(name, list(shape), dt, kind=kind).ap()

    def sb(self, st, name, shape, dt):
        return st.enter_context(self.nc.sbuf_tensor(f"{name}_p{self.S.phase}", list(shape), dt))

    def ps(self, st, name, shape, dt):
        return st.enter_context(self.nc.psum_tensor(f"{name}_p{self.S.phase}", list(shape), dt))

    def sem(self, name):
        return self.es.enter_context(self.nc.semaphore(name))

    def build(self):
        nc = self.nc
        A = self.dram
        self.xv = A("xv", [NB * 128, D], F32, "ExternalInput")
        self.xs = A("xs", [32, D], F32, "ExternalInput")
        self.ck = A("ck", [8, 1024, 64], F32, "ExternalInput")
        self.cv = A("cv", [8, 1024, 64], F32, "ExternalInput")
        self.spool = A("spool", [15, 512], F32, "ExternalInput")
        self.gains = A("gains", [4, D], F32, "ExternalInput")
        self.pscale = A("pscale", [128, 4], F32, "ExternalInput")
        self.poolw = A("poolw", [4, 128, 128], F32, "ExternalInput")
        self.w_f32 = {
            "g1": A("w_g1", [D, DFF], F32, "ExternalInput"),
            "u1": A("w_u1", [D, DFF], F32, "ExternalInput"),
            "d1": A("w_d1", [DFF, D], F32, "ExternalInput"),
            "in": A("w_in", [D, DFF], F32, "ExternalInput"),
            "bp": A("w_bp", [512, D], F32, "ExternalInput"),
            "sb": A("w_sb", [512, D], F32, "ExternalInput"),
            "o": A("w_o", [D, D], F32, "ExternalInput"),
            "g2": A("w_g2", [D, DFF], F32, "ExternalInput"),
            "u2": A("w_u2", [D, DFF], F32, "ExternalInput"),
            "d2": A("w_d2", [DFF, D], F32, "ExternalInput"),
        }
        self.c_bf = A("c_bf", [128, 128 * 3 + 32], BF16, "ExternalInput")
        self.c_f32 = A("c_f32", [128, 3 * 512 + 2 * 128], F32, "ExternalInput")
        self.y_own = A("y_own", [NQC, D], F32, "ExternalOutput")
        self.nk_own = A("nk_own", [8, NQC, 64], F32, "ExternalOutput")
        self.nv_own = A("nv_own", [8, NQC, 64], F32, "ExternalOutput")
        self.pool_p = A("pool_p", [15, 512], F32, "ExternalOutput")
        self.pool_s = A("pool_s", [15, 512], F32, "ExternalOutput")
        self.w_bf = {k: A("s_" + k, v.shape, BF16, "Internal") for k, v in self.w_f32.items()}
        self.w_cbuf = {k: [] for k in self.w_f32}
        self.x1s = A("s_x1", [NQC, D], F32, "Internal")
        self.hT2s = A("s_hT2", [128, 8, NQC], BF16, "Internal")
        self.QTs = A("s_QT", [4, 128, NQC], BF16, "Internal")
        self.KTs = A("s_KT", [4, 128, NB * 128], BF16, "Internal")
        self.VRs = A("s_VR", [4, 128, NB, 128], BF16, "Internal")
        self.KTDs = A("s_KTD", [4, 128, 32], BF16, "Internal")
        self.VRDs = A("s_VRD", [32, 512], BF16, "Internal")
        self.aTs = A("s_aT", [128, 4, NQC], BF16, "Internal")
        self.OTs = A("s_OT", [4, 128, NQC], BF16, "Internal")

        csem = {e: self.sem("c_" + e) for e in Sched.COMPUTE}
        self.S = Sched(nc, csem)
        self.page_sems = [self.S.dsem(self.sem(f"pg{i}")) for i in range(NPAGE)]
        self.ld_sems = [self.S.dsem(self.sem(f"ld{i}")) for i in range(16)]
        self.st_sems = [self.S.dsem(self.sem(f"st{i}")) for i in range(8)]
        self.cast_sems2 = [self.S.dsem(self.sem(f"cs{i}")) for i in range(2)]
        self._cast_i = 0
        self._ld_i = 0
        self._st_i = 0

        if 1 in self.phases:
            self.phase1()
            self.S.flush()
        if 2 in self.phases:
            self.phase2()
            self.S.flush()
        if 3 in self.phases:
            self.phase3()
            self.S.flush()
        self.es.close()
        return nc

    def load(self, out, in_, wbufs, rbufs=(), eng="sp"):
        d = self.ld_sems[self._ld_i % len(self.ld_sems)]
        self._ld_i += 1
        return self.S.op(eng, lambda e: e.dma_start(out=out, in_=in_), reads=rbufs, writes=wbufs, dsem=d)

    def store(self, out, in_, rbufs, wbufs=(), eng="sp"):
        d = self.st_sems[self._st_i % len(self.st_sems)]
        self._st_i += 1
        return self.S.op(eng, lambda e: e.dma_start(out=out, in_=in_), reads=rbufs, writes=wbufs, dsem=d)

    def mm(self, out, lhsT, rhs, start, stop, reads, writes):
        self.S.op("pe", lambda e: e.matmul(out, lhsT=lhsT, rhs=rhs, start=start, stop=stop), reads, writes)

    def tr(self, out, in_, ident, reads, writes):
        self.S.op("pe", lambda e: e.transpose(out, in_, ident), reads, writes)

    def act(self, out, in_, func, reads, writes, scale=1.0, bias=0.0, accum_out=None):
        if accum_out is None:
            self.S.op("act", lambda e: e.activation(out=out, in_=in_, func=func, scale=scale, bias=bias),
                      reads, writes)
        else:
            self.S.op("act", lambda e: e.activation(out=out, in_=in_, func=func, scale=scale, bias=bias,
                                                    accum_out=accum_out), reads, writes)

    def copy(self, eng, out, in_, reads, writes):
        if eng == "act":
            self.S.op("act", lambda e: e.activation(out=out, in_=in_, func=AF.Copy), reads, writes)
        else:
            self.S.op(eng, lambda e: e.tensor_copy(out=out, in_=in_), reads, writes)

    def tt(self, eng, out, in0, in1, op, reads, writes):
        self.S.op(eng, lambda e: e.tensor_tensor(out=out, in0=in0, in1=in1, op=op), reads, writes)

    def stt(self, eng, out, in0, scalar, in1, op0, op1, reads, writes):
        self.S.op(eng, lambda e: e.scalar_tensor_tensor(out=out, in0=in0, scalar=scalar, in1=in1,
                                                        op0=op0, op1=op1), reads, writes)

    def wstream_init(self, st, plan):
        self.pages = [self.sb(st, f"page{i}", [128, 4096], BF16) for i in range(NPAGE)]
        self.page_bufs = [Buf(f"page{i}") for i in range(NPAGE)]
        self.wplan = plan
        self.w_issued = 0
        self.w_used = 0
        self.w_rel = 0

    def _issue(self, upto):
        while self.w_issued < min(upto, len(self.wplan), self.w_rel + NPAGE):
            i = self.w_issued
            key, src, shp, rng = self.wplan[i]
            rbufs = [b_ for (r0, r1, c0, c1, b_) in self.w_cbuf[key]
                     if r0 < rng[1] and rng[0] < r1 and c0 < rng[3] and rng[2] < c1]
            pg = i % NPAGE
            dst = self.pages[pg][:, 0:shp[1] * shp[2]].rearrange("p (a b) -> p a b", a=shp[1])
            if shp[0] < 128:
                dst = self.pages[pg][0:shp[0], 0:shp[1] * shp[2]].rearrange("p (a b) -> p a b", a=shp[1])
            self.S.op("sp", lambda e, dst=dst, src=src: e.dma_start(out=dst, in_=src),
                      reads=rbufs, writes=[self.page_bufs[pg]], dsem=self.page_sems[pg])
            self.w_issued += 1

    def wget(self, key):
        i = self.w_used
        k, src, shp, _ = self.wplan[i]
        assert k == key, (k, key, i)
        self._issue(i + 1 + LOOKAHEAD)
        self.w_used += 1
        pg = i % NPAGE
        v = self.pages[pg][0:shp[0], 0:shp[1] * shp[2]].rearrange("p (a b) -> p a b", a=shp[1])
        return v, self.page_bufs[pg]

    def wrel(self):
        self.w_rel = self.w_used

    def wsrc_cols(self, key, c0, ncol=512):
        w = self.w_bf[key]
        return (key, w[:, c0:c0 + ncol].rearrange("(kc p) n -> p kc n", p=128), (128, 8, ncol),
                (0, 1024, c0, c0 + ncol))

    def wsrc_rows(self, key, r0, nrow, c0, ncol):
        w = self.w_bf[key]
        return (key, w[r0:r0 + nrow, c0:c0 + ncol].rearrange("(kc p) n -> p kc n", p=128),
                (128, nrow // 128, ncol), (r0, r0 + nrow, c0, c0 + ncol))

    def cast_weights(self, keys):
        if "cast" in self.skip:
            return
        work = []
        for k in keys:
            if ("cast_" + k) in self.skip:
                continue
            src = self.w_f32[k]
            dst = self.w_bf[k]
            rows, cols = src.shape
            if rows == 1024 and cols == 4096:
                chunks = [(slice(0, rows), slice(c, c + 512)) for c in range(0, cols, 512)]
            else:
                step = max(1, min(256, (1 << 20) // cols))
                chunks = [(slice(r, r + step), slice(0, cols)) for r in range(0, rows, step)]
            item = []
            for rs, cs in chunks:
                b_ = Buf()
                self.w_cbuf[k].append((rs.start, rs.stop, cs.start, cs.stop, b_))
                item.append((b_, dst[rs, cs], src[rs, cs]))
            work.append(item)
        order = []
        if len(work) >= 2 and len(work[0]) == len(work[1]):
            for a, b in zip(work[0], work[1]):
                order += [a, b]
            work = work[2:]
        for w_ in work:
            order += w_
        for b_, o, s_ in order:
            d = self.cast_sems2[self._cast_i % 2]
            self._cast_i += 1
            self.S.op("pool", lambda e, o=o, s_=s_: e.dma_start(out=o, in_=s_), reads=[],
                      writes=[b_], dsem=d)

    def common_alloc(self, st, gidx, with_cf=True):
        nc = self.nc
        self.cbf = self.sb(st, "cbf", [128, 128 * 3 + 32], BF16)
        self.cbf_b = Buf("cbf")
        if with_cf:
            self.cf = self.sb(st, "cf", [128, 3 * 512 + 256], F32)
        self.cf_b = Buf("cf")
        self.gb = self.sb(st, "gb", [128, 2, D], F32)
        self.gb_b = Buf("gb")
        self.load(self.cbf[:], self.c_bf, [self.cbf_b])
        if with_cf:
            self.load(self.cf[:], self.c_f32, [self.cf_b])
        for i, gi in enumerate(gidx):
            if "gb" in self.skip:
                break
            self.load(self.gb[:, i, :], self.gains[gi:gi + 1, :].partition_broadcast(128), [self.gb_b])
        self.ident = self.cbf[:, 0:128]
        self.J = self.cbf[:, 128:256]
        self.xt = Ring([self.sb(st, f"xt{i}", [128, 4, D], F32) for i in range(2)])
        self.xt_sb = [[Buf() for _ in range(4)] for _ in range(2)]
        self.hb = Ring([self.sb(st, f"hb{i}", [128, D], BF16) for i in range(4)])
        self.sq = self.sb(st, "sq", [128, D], BF16)
        self.stat = self.sb(st, "stat", [128, 16], F32)
        self.stat_b = [Buf() for _ in range(8)]
        self.stat_i = 0
        self.hT_ring = [self.sb(st, f"hT{i}", [128, 8, 512], BF16) for i in range(2)]
        self.hT_bufs = [[Buf() for _ in range(4)] for _ in range(2)]
        self.hT_i = 0
        self.hT = self.hT_ring[0]
        self.hT_b = self.hT_bufs[0]
        self.actT = self.sb(st, "actT", [128, 32, 512], BF16)
        self.actT_b = [Buf() for _ in range(32)]
        self.sgate = Ring([self.sb(st, f"sgate{i}", [128, 512], F32) for i in range(2)])
        self.pmm = Ring([self.ps(st, f"pmm{i}", [128, 512], F32) for i in range(4)])
        self.pacc = Ring([self.ps(st, f"pacc{i}", [128, 512], F32) for i in range(2)])
        self.pmisc = Ring([self.ps(st, f"pmisc{i}", [128, 512], F32) for i in range(2)])
        self.pmisc_bf = [t[:].bitcast(BF16) for t in self.pmisc.t] if False else None

    def rmsnorm_hT(self, xap, np_, col, gi, xbuf, hTbuf, defer=False):
        si = self.stat_i % 8
        self.stat_i += 1
        ss = self.stat[0:np_, 2 * si:2 * si + 1]
        rs = self.stat[0:np_, 2 * si + 1:2 * si + 2]
        sb_ = self.stat_b[si]
        self.act(self.sq[0:np_, :], xap, AF.Square, [xbuf], [sb_], accum_out=ss)
        self.act(ss, ss, AF.Sqrt, [sb_], [sb_], scale=1.0 / D, bias=1e-6)
        self.S.op("dve", lambda e: e.reciprocal(out=rs, in_=ss), [sb_], [sb_])
        hb, hbb = self.hb.next()
        self.stt("dve", hb[0:np_, :], xap, rs, self.gb[0:np_, gi, :], ALU.mult, ALU.mult,
                 [xbuf, sb_, self.gb_b], [hbb])
        if defer:
            return (hb, hbb, np_, col, hTbuf)
        self.rmsnorm_part2((hb, hbb, np_, col, hTbuf))

    def rmsnorm_part2(self, h):
        hb, hbb, np_, col, hTbuf = h
        pm, pmb = self.pmisc.next()
        pv = pm[:].bitcast(BF16)
        for c in range(8):
            self.tr(pv[:, c * np_:(c + 1) * np_], hb[0:np_, c * 128:(c + 1) * 128], self.ident[0:np_, 0:np_],
                    [hbb, self.cbf_b], [pmb])
        self.copy("dve", self.hT[:, :, col:col + np_],
                  pv[:, 0:8 * np_].rearrange("p (c n) -> p c n", c=8), [pmb], [hTbuf])

    def hT_swap(self):
        self.hT_i ^= 1
        self.hT = self.hT_ring[self.hT_i]
        self.hT_b = self.hT_bufs[self.hT_i]

    def ffn(self, kg, ku, kd, subs, ntok, xt, xbufs, mid=None):
        hTr = self.hT_b
        for jg in range(8):
            G, Gb = self.wget(kg)
            U, Ub = self.wget(ku)
            for jj in range(4):
                j = jg * 4 + jj
                pg, pgb = self.pmm.next()
                for kc in range(8):
                    self.mm(pg[:, 0:ntok], G[:, kc, jj * 128:(jj + 1) * 128], self.hT[:, kc, 0:ntok],
                            kc == 0, kc == 7, [Gb] + hTr, [pgb])
                pu, pub = self.pmm.next()
                for kc in range(8):
                    self.mm(pu[:, 0:ntok], U[:, kc, jj * 128:(jj + 1) * 128], self.hT[:, kc, 0:ntok],
                            kc == 0, kc == 7, [Ub] + hTr, [pub])
                sg, sgb = self.sgate.next()
                self.act(sg[:, 0:ntok], pg[:, 0:ntok], AF.Silu, [pgb], [sgb])
                self.tt("dve", self.actT[:, j, 0:ntok], sg[:, 0:ntok], pu[:, 0:ntok], ALU.mult,
                        [sgb, pub], [self.actT_b[j]])
            self.wrel()
        if mid is not None:
            mid()
        for hh in range(2):
            Dp = [self.wget(kd) for _ in range(4)]
            for si, sub in enumerate(subs):
                np_, col = sub["np"], sub["col"]
                pa, pab = self.pacc.next()
                for fg in range(4):
                    Dv, Db = Dp[fg]
                    for fc in range(8):
                        f = fg * 8 + fc
                        self.mm(pa[0:np_, :], self.actT[:, f, col:col + np_], Dv[:, fc, :],
                                f == 0, f == 31, [self.actT_b[f], Db], [pab])
                xs_ = xt[0:np_, si, hh * 512:(hh + 1) * 512]
                self.stt("dve", xs_, pa[0:np_, :], 0.5, xs_, ALU.mult, ALU.add, [pab, xbufs[si]], [xbufs[si]])
            self.wrel()

    def tiles1(self):
        tl = []
        for ti in range(16):
            subs = []
            for s in range(4):
                b = 4 * ti + s
                subs.append(dict(kind="p", blk=b, np=128, col=s * 128, own=b in SLOT, slot=SLOT.get(b)))
            tl.append(dict(subs=subs, ntok=512))
        subs = [dict(kind="p", blk=64, np=128, col=0, own=True, slot=SLOT[64]),
                dict(kind="d", blk=None, np=32, col=128, own=True, slot=None)]
        tl.append(dict(subs=subs, ntok=160))
        return tl

    def phase1(self):
        nc = self.nc
        S = self.S
        tiles = self.tiles1()
        if self.dbg_tiles is not None:
            tiles = [tiles[i] for i in self.dbg_tiles]
        plan = []
        if self.dbg_stage < 4:
            tiles = [dict(t, noproj=True) for t in tiles]
        for t in tiles:
            for jg in range(8):
                plan.append(self.wsrc_cols("g1", jg * 512))
                plan.append(self.wsrc_cols("u1", jg * 512))
            for hh in range(2):
                for fg in range(4):
                    plan.append(self.wsrc_rows("d1", fg * 1024, 1024, hh * 512, 512))
            if self.dbg_stage < 2:
                plan = []
            if t.get("noproj"):
                continue
            for c0 in (0, 1024, 1536, 512):
                plan.append(self.wsrc_cols("in", c0))
        with ExitStack() as st:
            self.cast_weights(["g1", "u1", "d1", "in"])
            self.common_alloc(st, (0, 1))
            self.wstream_init(st, plan)
            uring = Ring([self.sb(st, f"u{i}", [128, 512], F32) for i in range(3)])
            ktm = Ring([self.sb(st, f"ktm{i}", [128, 512], F32) for i in range(1)])
            vtm = Ring([self.sb(st, f"vtm{i}", [128, 512], F32) for i in range(1)])
            kbf = Ring([self.sb(st, f"kbf{i}", [128, 512], BF16) for i in range(2)])
            vbf = Ring([self.sb(st, f"vbf{i}", [128, 512], BF16) for i in range(2)])
            ktr = Ring([self.sb(st, f"ktr{i}", [128, 4, 128], BF16) for i in range(2)])
            vrv = Ring([self.sb(st, f"vrv{i}", [128, 512], BF16) for i in range(2)])
            qts = Ring([self.sb(st, f"qts{i}", [128, 4, 128], BF16) for i in range(2)])
            dts = Ring([self.sb(st, f"dts{i}", [128, 4, 128], BF16) for i in range(2)])
            ats = Ring([self.sb(st, f"ats{i}", [128, 4, 128], BF16) for i in range(2)])
            pwf = self.sb(st, "pwf", [128, 4, 128], F32)
            pwb = self.sb(st, "pwb", [128, 4, 128], BF16)
            pw_b = Buf()
            psc = self.sb(st, "psc", [128, 4], F32)
            psc_b = Buf()
            spl = self.sb(st, "spl", [15, 512], F32)
            spl_b = Buf()
            self.load(pwf[:], self.poolw.rearrange("g c e -> c g e"), [pw_b])
            self.copy("dve", pwb[:], pwf[:], [pw_b], [pw_b])
            self.load(psc[:], self.pscale, [psc_b])
            self.load(spl[:], self.spool, [spl_b])
            cf = self.cf
            bc = lambda g: cf[:, g * 128:(g + 1) * 128]
            bp = lambda g: cf[:, 512 + g * 128:512 + (g + 1) * 128]
            bs = lambda g: cf[:, 1024 + g * 128:1024 + (g + 1) * 128]
            bdc = lambda g: cf[0:32, 1536 + g * 32:1536 + (g + 1) * 32]
            bdp = lambda g: cf[0:15, 1664 + g * 32:1664 + (g + 1) * 32]

            u_prev = None

            def load_x(t):
                xt, _ = self.xt.next()
                xb = self.xt_sb[self.xt.i]
                for si, sub in enumerate(t["subs"]):
                    if sub["kind"] == "p":
                        b = sub["blk"]
                        self.load(xt[:, si, :], self.xv[b * 128:(b + 1) * 128, :], [xb[si]])
                    else:
                        self.load(xt[0:32, si, :], self.xs, [xb[si]])
                return xt, xb

            def norm1_part1(t, xt, xb, hTb):
                return [self.rmsnorm_hT(xt[0:sub["np"], si, :], sub["np"], sub["col"], 0, xb[si], hTb[si], defer=True)
                        for si, sub in enumerate(t["subs"])]

            cur = load_x(tiles[0])
            for h in norm1_part1(tiles[0], cur[0], cur[1], self.hT_b):
                self.rmsnorm_part2(h)
            for ti, t in enumerate(tiles):
                subs, ntok = t["subs"], t["ntok"]
                xt, xb = cur
                nxt = None
                pend = []
                if ti + 1 < len(tiles):
                    nxt = load_x(tiles[ti + 1])
                    pend = norm1_part1(tiles[ti + 1], nxt[0], nxt[1], self.hT_bufs[self.hT_i ^ 1])

                def mid(pend=pend):
                    keep = (self.hT, self.hT_b)
                    self.hT_swap()
                    for h in pend:
                        self.rmsnorm_part2(h)
                    self.hT_swap()
                    assert keep[0] is self.hT

                self.ffn("g1", "u1", "d1", subs, ntok, xt, xb, mid=mid)
                cur = nxt
                if self.dbg_stage < 3:
                    continue
                for si, sub in enumerate(subs):
                    np_ = sub["np"]
                    if sub["own"]:
                        oc = sub["slot"] * 128 if sub["kind"] == "p" else DEC0
                        self.store(self.x1s[oc:oc + np_, :], xt[0:np_, si, :], [xb[si]])
                    self.rmsnorm_hT(xt[0:np_, si, :], np_, sub["col"], 1, xb[si], self.hT_b[si])
                    if sub["own"]:
                        self.store(self.hT2s[:, :, oc:oc + np_], self.hT[:, :, sub["col"]:sub["col"] + np_],
                                   [self.hT_b[si]])
                if self.dbg_stage < 4:
                    continue
                Uw, Uwb = self.wget("in")
                Kw, Kwb = self.wget("in")
                Vw, Vwb = self.wget("in")
                Qw, Qwb = self.wget("in")
                for si, sub in enumerate(subs):
                    np_, col = sub["np"], sub["col"]
                    dec = sub["kind"] == "d"
                    own = sub["own"]
                    oc = (sub["slot"] * 128 if not dec else DEC0) if own else None
                    hTs = [self.hT_b[si]]
                    Jn = self.J[0:np_, 128 - np_:128]
                    pa, pab = self.pacc.next()
                    for kc in range(8):
                        self.mm(pa[0:np_, :], self.hT[:, kc, col:col + np_], Uw[:, kc, :], kc == 0, kc == 7,
                                hTs + [Uwb], [pab])
                    ut, utb = uring.next()
                    self.copy("act", ut[0:np_, :], pa[0:np_, :], [pab], [utb])
                    pa, pab = self.pacc.next()
                    for kc in range(8):
                        self.mm(pa[0:np_, :], self.hT[:, kc, col:col + np_], Kw[:, kc, :], kc == 0, kc == 7,
                                hTs + [Kwb], [pab])
                    kb_, kbb = kbf.next()
                    if own and "nkst" not in self.skip:
                        km, kmb = ktm.next()
                        self.copy("act", km[0:np_, :], pa[0:np_, :], [pab], [kmb])
                        self.copy("pool", kb_[0:np_, :], km[0:np_, :], [kmb], [kbb])
                        for h in range(8):
                            self.store(self.nk_own[h, oc:oc + np_, :], km[0:np_, h * 64:(h + 1) * 64], [kmb])
                    else:
                        self.copy("act", kb_[0:np_, :], pa[0:np_, :], [pab], [kbb])
                    pm, pmb = self.pmisc.next()
                    for j in range(4):
                        self.mm(pm[:, j * np_:(j + 1) * np_], kb_[0:np_, j * 128:(j + 1) * 128], Jn, True, True,
                                [kbb, self.cbf_b], [pmb])
                    kr, krb = ktr.next()
                    self.copy("dve", kr[:, :, 0:np_], pm[:, 0:4 * np_].rearrange("p (j n) -> p j n", j=4),
                              [pmb], [krb])
                    if "ktst" in self.skip:
                        pass
                    elif not dec:
                        rb = 64 - sub["blk"]
                        self.store(self.KTs[:, :, rb * 128:(rb + 1) * 128].rearrange("j p n -> p j n"),
                                   kr[:], [krb])
                    else:
                        self.store(self.KTDs.rearrange("j p n -> p j n"), kr[:, :, 0:32], [krb])
                    pa, pab = self.pacc.next()
                    for kc in range(8):
                        self.mm(pa[0:np_, :], self.hT[:, kc, col:col + np_], Vw[:, kc, :], kc == 0, kc == 7,
                                hTs + [Vwb], [pab])
                    vb_, vbb = vbf.next()
                    if own and "nkst" not in self.skip:
                        vm, vmb = vtm.next()
                        self.copy("act", vm[0:np_, :], pa[0:np_, :], [pab], [vmb])
                        self.copy("pool", vb_[0:np_, :], vm[0:np_, :], [vmb], [vbb])
                        for h in range(8):
                            self.store(self.nv_own[h, oc:oc + np_, :], vm[0:np_, h * 64:(h + 1) * 64], [vmb])
                    else:
                        self.copy("act", vb_[0:np_, :], pa[0:np_, :], [pab], [vbb])
                    pm, pmb = self.pmisc.next()
                    self.mm(pm[0:np_, :], Jn, vb_[0:np_, :], True, True, [vbb, self.cbf_b], [pmb])
                    vr, vrb = vrv.next()
                    self.copy("dve", vr[0:np_, :], pm[0:np_, :], [pmb], [vrb])
                    if "ktst" in self.skip:
                        pass
                    elif not dec:
                        self.store(self.VRs[:, :, 64 - sub["blk"], :].rearrange("j p n -> p j n"),
                                   vr[:].rearrange("p (j n) -> p j n", j=4), [vrb])
                    else:
                        self.store(self.VRDs, vr[0:32, :], [vrb])
                    if own and "q" not in self.skip:
                        pm, pmb = self.pmisc.next()
                        for j in range(4):
                            for kc in range(8):
                                self.mm(pm[:, j * np_:(j + 1) * np_], Qw[:, kc, j * 128:(j + 1) * 128],
                                        self.hT[:, kc, col:col + np_], kc == 0, kc == 7, hTs + [Qwb], [pmb])
                        qt, qtb = qts.next()
                        self.copy("act", qt[:, :, 0:np_], pm[:, 0:4 * np_].rearrange("p (j n) -> p j n", j=4),
                                  [pmb], [qtb])
                        self.store(self.QTs[:, :, oc:oc + np_].rearrange("j p n -> p j n"), qt[:, :, 0:np_], [qtb])
                    if own and "pool" not in self.skip:
                        pm, pmb = self.pmisc.next()
                        for g in range(4):
                            ug = ut[0:np_, g * 128:(g + 1) * 128]
                            if dec:
                                self.mm(pm[:, g * np_:(g + 1) * np_], ug, bdc(g), True, False,
                                        [utb, self.cf_b], [pmb])
                                self.mm(pm[:, g * np_:(g + 1) * np_], spl[0:15, g * 128:(g + 1) * 128], bdp(g),
                                        False, True, [spl_b, self.cf_b], [pmb])
                            elif sub["blk"] == 0:
                                self.mm(pm[:, g * np_:(g + 1) * np_], ug, bs(g), True, True,
                                        [utb, self.cf_b], [pmb])
                            else:
                                upt, upb = u_prev
                                self.mm(pm[:, g * np_:(g + 1) * np_], ug, bc(g), True, False,
                                        [utb, self.cf_b], [pmb])
                                self.mm(pm[:, g * np_:(g + 1) * np_], upt[:, g * 128:(g + 1) * 128], bp(g),
                                        False, True, [upb, self.cf_b], [pmb])
                        dt_, dtb = dts.next()
                        self.copy("dve", dt_[:, :, 0:np_], pm[:, 0:4 * np_].rearrange("p (j n) -> p j n", j=4),
                                  [pmb], [dtb])
                        pm, pmb = self.pmisc.next()
                        for g in range(4):
                            self.mm(pm[:, g * np_:(g + 1) * np_], pwb[:, g, :], dt_[:, g, 0:np_], True, True,
                                    [pw_b, dtb], [pmb])
                        at, atb = ats.next()
                        for g in range(4):
                            self.S.op("dve", lambda e, o=at[:, g, 0:np_], i=pm[:, g * np_:(g + 1) * np_],
                                      s=psc[:, g:g + 1]: e.tensor_scalar(out=o, in0=i, scalar1=s, scalar2=None,
                                                                         op0=ALU.mult),
                                      [pmb, psc_b], [atb])
                        self.store(self.aTs[:, :, oc:oc + np_], at[:, :, 0:np_], [atb])
                    if not dec:
                        if sub["blk"] == 63:
                            self.store(self.pool_p, ut[113:128, :], [utb])
                        u_prev = (ut, utb)
                    else:
                        self.store(self.pool_s, ut[17:32, :], [utb])
                self.wrel()
                self.hT_swap()

    def phase2(self):
        S = self.S
        with ExitStack() as st:
            self.cast_weights(["g2", "u2", "bp", "sb", "o", "d2"])
            cbf = self.sb(st, "cbf2", [128, 128 * 3 + 32], BF16)
            cbf_b = Buf()
            self.load(cbf[:], self.c_bf, [cbf_b])
            ident = cbf[:, 0:128]
            J = cbf[:, 128:256]
            M128 = cbf[:, 256:384]
            M32 = cbf[0:32, 384:416]
            ones = self.sb(st, "ones", [128, 1024], BF16)
            ones_b = Buf()
            S.op("pool", lambda e: e.memset(ones[:], 1.0), [], [ones_b])
            ktp = Ring([self.sb(st, f"ktp{i}", [128, NB * 128], BF16) for i in range(2)])
            vrp = Ring([self.sb(st, f"vrp{i}", [128, NB, 128], BF16) for i in range(2)])
            qtp = Ring([(self.sb(st, f"qA{i}", [128, NQC], BF16), self.sb(st, f"qB{i}", [128, NQC], BF16))
                        for i in range(2)])
            for (qa_, qb_), qbuf_ in zip(qtp.t, qtp.b):
                S.op("pool", lambda e, t=qa_: e.memset(t[64:128, :], 0.0), [], [qbuf_])
                S.op("pool", lambda e, t=qb_: e.memset(t[0:64, :], 0.0), [], [qbuf_])
            ktd = Ring([self.sb(st, f"ktd{i}", [128, 32 + 1024], BF16) for i in range(2)])
            vrd = Ring([self.sb(st, f"vrd{i}", [128, 9, 128], BF16) for i in range(2)])
            ot = Ring([self.sb(st, f"ot{i}", [128, NQC], BF16) for i in range(2)])
            ckf = Ring([self.sb(st, f"ckf{i}", [128, 8, 2, 64], F32) for i in range(2)])
            ckb = Ring([self.sb(st, f"ckb{i}", [128, 8, 2, 64], BF16) for i in range(2)])
            zr = Ring([self.ps(st, f"z{i}", [128, 1024], F32) for i in range(2)])
            wtp = Ring([self.ps(st, f"wtp{i}", [128, 1024], BF16) for i in range(2)])
            oacc = Ring([self.ps(st, f"oacc{i}", [128, 512], F32) for i in range(2)])
            sg = Ring([self.sb(st, f"sg{i}", [128, 1024], F32) for i in range(3)])
            Pb = Ring([self.sb(st, f"Pb{i}", [128, 1025], F32) for i in range(3)])
            wb = Ring([self.sb(st, f"wb{i}", [128, 1024], BF16) for i in range(3)])
            wT = Ring([self.sb(st, f"wT{i}", [128, 1024], BF16) for i in range(3)])
            state = {"prevP": None, "oacc": None}

            def stageA(ch):
                nq, nc_ = ch["nq"], ch["ncols"]
                z, zb = zr.next()
                for off in range(0, nc_, 512):
                    n = min(512, nc_ - off)
                    self.mm(z[0:nq, off:off + n], ch["q"], ch["kT"][:, off:off + n], True,
                            not (ch["first"] and off == 0), ch["rb"], [zb])
                if ch["first"]:
                    M = M128 if nq == 128 else M32
                    self.mm(z[0:nq, 0:nq], ident[0:nq, 0:nq], M[0:nq, 0:nq], False, True, [cbf_b], [zb])
                sgt, sgb = sg.next()
                self.act(sgt[0:nq, 0:nc_], z[0:nq, 0:nc_], AF.Sigmoid, [zb], [sgb], scale=-0.125)
                pt, pb = Pb.next()
                if ch["first"]:
                    S.op("dve", lambda e: e.memset(pt[0:nq, 0:1], 1.0), [], [pb])
                else:
                    ppt, ppb, pn = state["prevP"]
                    self.copy("dve", pt[0:nq, 0:1], ppt[0:nq, pn:pn + 1], [ppb], [pb])
                S.op("dve", lambda e: e.tensor_tensor_scan(out=pt[0:nq, 1:nc_ + 1], data0=sgt[0:nq, 0:nc_],
                                                           data1=ones[0:nq, 0:nc_], initial=pt[0:nq, 0:1],
                                                           op0=ALU.mult, op1=ALU.mult),
                     [sgb, ones_b, pb], [pb])
                state["prevP"] = (pt, pb, nc_)
                wt, wbb = wb.next()
                self.tt("dve", wt[0:nq, 0:nc_], pt[0:nq, 0:nc_], pt[0:nq, 1:nc_ + 1], ALU.subtract, [pb], [wbb])
                ch["wb"] = (wt, wbb)

            def stageB(ch):
                nq = ch["nq"]
                wt, wbb = ch["wb"]
                tp, tpb = wtp.next()
                blocks = ch["blocks"]
                kb = blocks[0][1]
                off = 0
                for bi, (v_ap, kb_) in enumerate(blocks):
                    self.tr(tp[0:kb_, bi * 128:bi * 128 + nq], wt[0:nq, off:off + kb_], ident[0:nq, 0:nq],
                            [wbb, cbf_b], [tpb])
                    off += kb_
                wTt, wTb = wT.next()
                nb = len(blocks)
                self.copy("act", wTt[0:kb, 0:nb * 128].rearrange("p (b n) -> p b n", b=nb)[:, :, 0:nq],
                          tp[0:kb, 0:nb * 128].rearrange("p (b n) -> p b n", b=nb)[:, :, 0:nq], [tpb], [wTb])
                ch["wT"] = (wTt, wTb)

            def stageC(ch):
                nq = ch["nq"]
                blocks = ch["blocks"]
                nb = len(blocks)
                wTt, wTb = ch["wT"]
                if ch["first"]:
                    state["oacc"] = oacc.next()
                oa, oab = state["oacc"]
                pr0 = ch["pr0"]
                for bi, (v_ap, kb_) in enumerate(blocks):
                    self.mm(oa[:, 0:nq], v_ap, wTt[0:kb_, bi * 128:bi * 128 + nq],
                            ch["first"] and bi == 0, ch["last"] and bi == nb - 1, [wTb] + ch["vb"], [oab])
                if ch["last"]:
                    ott, otb = ch["ot"]
                    oc = ch["ocol"]
                    self.copy("dve", ott[pr0:pr0 + 64, oc:oc + nq], oa[pr0:pr0 + 64, 0:nq], [oab], [otb])

            slots = list(range(NS)) if self.dbg_tiles is None else self.dbg_slots
            def prep_loads(j):
                kt, ktb = ktp.next()
                vr, vrb = vrp.next()
                qt, qtb = qtp.next()
                kd, kdb = ktd.next()
                vd, vdb = vrd.next()
                if self.dbg_tiles is None:
                    self.load(kt[:], self.KTs[j], [ktb])
                    self.load(vr[:], self.VRs[j], [vrb])
                    self.load(qt[0][0:64, :], self.QTs[j][0:64, :], [qtb])
                    self.load(qt[1][64:128, :], self.QTs[j][64:128, :], [qtb])
                else:
                    self.load(kt[:, 61 * 128:], self.KTs[j][:, 61 * 128:], [ktb])
                    self.load(vr[:, 61:, :], self.VRs[j][:, 61:, :], [vrb])
                    for hq in range(2):
                        hs = slice(hq * 64, hq * 64 + 64)
                        self.load(qt[hq][hs, 0:256], self.QTs[j][hs, 0:256], [qtb])
                        self.load(qt[hq][hs, DEC0:], self.QTs[j][hs, DEC0:], [qtb])
                self.load(kd[:, 0:32], self.KTDs[j], [kdb])
                self.load(vd[0:32, 0, :], self.VRDs[:, j * 128:(j + 1) * 128], [vdb])
                cfs = []
                for src in (self.ck, self.cv):
                    cf_, cfb = ckf.next()
                    for hh in range(2):
                        self.load(cf_[:, :, hh, :], src[2 * j + hh].rearrange("(b p) d -> p b d", p=128), [cfb])
                    cfs.append((cf_, cfb))
                return dict(kt=kt, ktb=ktb, vr=vr, vrb=vrb, qt=qt, qtb=qtb, kd=kd, kdb=kdb, vd=vd, vdb=vdb, cfs=cfs)

            def prep_compute(P):
                kd, kdb, vd, vdb = P["kd"], P["kdb"], P["vd"], P["vdb"]
                for (cf_, cfb), is_k in zip(P["cfs"], (True, False)):
                    cb_, cbb = ckb.next()
                    self.copy("pool", cb_[:], cf_[:], [cfb], [cbb])
                    z, zb = zr.next()
                    for blk in range(8):
                        rb = 7 - blk
                        cblk = cb_[:, blk, :, :].rearrange("p h d -> p (h d)")
                        if is_k:
                            self.mm(z[:, rb * 128:(rb + 1) * 128], cblk, J, True, True, [cbb, cbf_b], [zb])
                        else:
                            self.mm(z[:, rb * 128:(rb + 1) * 128], J, cblk, True, True, [cbb, cbf_b], [zb])
                    if is_k:
                        self.copy("act", kd[:, 32:1056], z[:, 0:1024], [zb], [kdb])
                    else:
                        self.copy("act", vd[:, 1:9, :], z[:, 0:1024].rearrange("p (b n) -> p b n", b=8), [zb], [vdb])


            nxtP = prep_loads(0)
            prep_compute(nxtP)
            for j in range(4):
                P = nxtP
                kt, ktb, vr, vrb, qt, qtb = P["kt"], P["ktb"], P["vr"], P["vrb"], P["qt"], P["qtb"]
                kd, kdb, vd, vdb = P["kd"], P["kdb"], P["vd"], P["vdb"]
                ott, otb = ot.next()
                if j < 3:
                    nxtP = prep_loads(j + 1)
                for hh in range(2):
                    if hh == 1 and j < 3:
                        prep_compute(nxtP)
                    pr0 = hh * 64
                    chunks = []
                    for k in slots:
                        a = OWN[k]
                        c0 = (64 - a) * 128
                        nblk = a + 1
                        for ci in range(0, nblk, 8):
                            nb = min(8, nblk - ci)
                            chunks.append(dict(
                                nq=128, ncols=nb * 128, first=ci == 0, last=ci + nb == nblk, pr0=pr0,
                                q=qt[hh][:, k * 128:(k + 1) * 128],
                                kT=kt[:, c0 + ci * 128:c0 + (ci + nb) * 128],
                                blocks=[(vr[:, 64 - a + ci + b, :], 128) for b in range(nb)],
                                rb=[qtb, ktb], vb=[vrb], ot=(ott, otb), ocol=k * 128))
                    chunks.append(dict(nq=32, ncols=32, first=True, last=False, pr0=pr0,
                                       q=qt[hh][:, DEC0:DEC0 + 32], kT=kd[:, 0:32],
                                       blocks=[(vd[0:32, 0, :], 32)],
                                       rb=[qtb, kdb], vb=[vdb], ot=(ott, otb), ocol=DEC0))
                    chunks.append(dict(nq=32, ncols=1024, first=False, last=True, pr0=pr0,
                                       q=qt[hh][:, DEC0:DEC0 + 32], kT=kd[:, 32:1056],
                                       blocks=[(vd[:, 1 + b, :], 128) for b in range(8)],
                                       rb=[qtb, kdb], vb=[vdb], ot=(ott, otb), ocol=DEC0))
                    n_ch = len(chunks)
                    for i in range(n_ch + 3):
                        if i < n_ch:
                            stageA(chunks[i])
                        if 0 <= i - 2 < n_ch:
                            stageB(chunks[i - 2])
                        if 0 <= i - 3 < n_ch:
                            stageC(chunks[i - 3])
                if self.dbg_tiles is None:
                    self.store(self.OTs[j], ott[:], [otb])
                else:
                    self.store(self.OTs[j][:, 0:256], ott[:, 0:256], [otb])
                    self.store(self.OTs[j][:, DEC0:], ott[:, DEC0:], [otb])

    def tiles3(self):
        tl = []
        for t in range(8):
            subs = [dict(kind="p", np=128, col=s * 128, slot=4 * t + s, oc=(4 * t + s) * 128) for s in range(4)]
            tl.append(dict(subs=subs, ntok=512, oc=4 * t * 128))
        subs = [dict(kind="p", np=128, col=0, slot=32, oc=32 * 128),
                dict(kind="d", np=32, col=128, slot=None, oc=DEC0)]
        tl.append(dict(subs=subs, ntok=160, oc=32 * 128))
        return tl

    def phase3(self):
        S = self.S
        tiles = self.tiles3()
        if self.dbg_tiles is not None:
            tiles = [dict(subs=tiles[0]["subs"][0:2], ntok=256, oc=0),
                     dict(subs=[dict(kind="d", np=32, col=0, slot=None, oc=DEC0)], ntok=32, oc=DEC0)]
        plan = []
        for t in tiles:
            plan.append(("bp", self.w_bf["bp"].rearrange("(g p) n -> p g n", p=128), (128, 4, 1024),
                         (0, 512, 0, 1024)))
            plan.append(("sb", self.w_bf["sb"].rearrange("(g p) n -> p g n", p=128), (128, 4, 1024),
                         (0, 512, 0, 1024)))
            for half in range(2):
                plan.append(self.wsrc_cols("in", 2048 + half * 512))
                plan.append(self.wsrc_cols("in", 3072 + half * 512))
            for hh in range(2):
                plan.append(self.wsrc_cols("o", hh * 512))
            for jg in range(8):
                plan.append(self.wsrc_cols("g2", jg * 512))
                plan.append(self.wsrc_cols("u2", jg * 512))
            for hh in range(2):
                for fg in range(4):
                    plan.append(self.wsrc_rows("d2", fg * 1024, 1024, hh * 512, 512))
        with ExitStack() as st:
            self.common_alloc(st, (2, 3), with_cf=False)
            self.wstream_init(st, plan)
            aTr = Ring([self.sb(st, f"aT{i}", [128, 4, 512], BF16) for i in range(2)])
            oTr = Ring([self.sb(st, f"oT{i}", [128, 4, 512], BF16) for i in range(2)])
            mg = self.sb(st, "mg", [128, 8, 512], BF16)
            mg_b = [Buf() for _ in range(8)]
            m1r = Ring([self.sb(st, f"m1_{i}", [128, 512], F32) for i in range(2)])
            m2r = Ring([self.sb(st, f"m2_{i}", [128, 512], F32) for i in range(2)])
            ytr = Ring([self.sb(st, f"yt{i}", [128, D], F32) for i in range(2)])
            def p3_loads(t, hT_t, hT_bufs):
                subs, ntok, oc0 = t["subs"], t["ntok"], t["oc"]
                xt, _ = self.xt.next()
                xb = self.xt_sb[self.xt.i]
                for si, sub in enumerate(subs):
                    np_, oc = sub["np"], sub["oc"]
                    self.load(xt[0:np_, si, :], self.x1s[oc:oc + np_, :], [xb[si]])
                self.load(hT_t[:, :, 0:ntok], self.hT2s[:, :, oc0:oc0 + ntok], hT_bufs)
                aT, aT_b = aTr.next()
                oT, oT_b = oTr.next()
                self.load(aT[:, :, 0:ntok], self.aTs[:, :, oc0:oc0 + ntok], [aT_b])
                self.load(oT[:, :, 0:ntok], self.OTs[:, :, oc0:oc0 + ntok].rearrange("j p n -> p j n"), [oT_b])
                return xt, xb, aT, aT_b, oT, oT_b

            cur = p3_loads(tiles[0], self.hT, self.hT_b)
            for ti, t in enumerate(tiles):
                subs, ntok, oc0 = t["subs"], t["ntok"], t["oc"]
                xt, xb, aT, aT_b, oT, oT_b = cur
                nxt_box = [None]

                def mid(ti=ti, nxt_box=nxt_box):
                    if ti + 1 < len(tiles):
                        o = self.hT_i ^ 1
                        nxt_box[0] = p3_loads(tiles[ti + 1], self.hT_ring[o], self.hT_bufs[o])
                BP, BPb = self.wget("bp")
                SBw, SBb = self.wget("sb")
                for half in range(2):
                    GA, GAb = self.wget("in")
                    GB, GBb = self.wget("in")
                    for cc in range(4):
                        c = half * 4 + cc
                        cs = slice(cc * 128, (cc + 1) * 128)
                        cg = slice(c * 128, (c + 1) * 128)
                        pga, pgab = self.pmm.next()
                        for kc in range(8):
                            self.mm(pga[:, 0:ntok], GA[:, kc, cs], self.hT[:, kc, 0:ntok], kc == 0, kc == 7,
                                    [GAb] + self.hT_b, [pgab])
                        pbp, pbpb = self.pmm.next()
                        for g in range(4):
                            self.mm(pbp[:, 0:ntok], BP[:, g, cg], aT[:, g, 0:ntok], g == 0, g == 3,
                                    [BPb, aT_b], [pbpb])
                        sa, sab = self.sgate.next()
                        self.act(sa[:, 0:ntok], pga[:, 0:ntok], AF.Sigmoid, [pgab], [sab])
                        m1, m1b = m1r.next()
                        self.tt("dve", m1[:, 0:ntok], sa[:, 0:ntok], pbp[:, 0:ntok], ALU.mult, [sab, pbpb], [m1b])
                        pgb, pgbb = self.pmm.next()
                        for kc in range(8):
                            self.mm(pgb[:, 0:ntok], GB[:, kc, cs], self.hT[:, kc, 0:ntok], kc == 0, kc == 7,
                                    [GBb] + self.hT_b, [pgbb])
                        psb, psbb = self.pmm.next()
                        for g in range(4):
                            self.mm(psb[:, 0:ntok], SBw[:, g, cg], oT[:, g, 0:ntok], g == 0, g == 3,
                                    [SBb, oT_b], [psbb])
                        sb2, sb2b = self.sgate.next()
                        self.act(sb2[:, 0:ntok], pgb[:, 0:ntok], AF.Sigmoid, [pgbb], [sb2b])
                        m2, m2b = m2r.next()
                        self.tt("dve", m2[:, 0:ntok], sb2[:, 0:ntok], psb[:, 0:ntok], ALU.mult, [sb2b, psbb], [m2b])
                        self.tt("pool", mg[:, c, 0:ntok], m1[:, 0:ntok], m2[:, 0:ntok], ALU.add, [m1b, m2b], [mg_b[c]])
                self.wrel()
                WO = [self.wget("o") for _ in range(2)]
                for si, sub in enumerate(subs):
                    np_, col = sub["np"], sub["col"]
                    for hh in range(2):
                        Wv, Wb = WO[hh]
                        pa, pab = self.pacc.next()
                        for c in range(8):
                            self.mm(pa[0:np_, :], mg[:, c, col:col + np_], Wv[:, c, :], c == 0, c == 7,
                                    [mg_b[c], Wb], [pab])
                        xs_ = xt[0:np_, si, hh * 512:(hh + 1) * 512]
                        self.tt("dve", xs_, pa[0:np_, :], xs_, ALU.add, [pab, xb[si]], [xb[si]])
                self.wrel()
                for si, sub in enumerate(subs):
                    self.rmsnorm_hT(xt[0:sub["np"], si, :], sub["np"], sub["col"], 0, xb[si], self.hT_b[si])
                self.ffn("g2", "u2", "d2", subs, ntok, xt, xb, mid=mid)
                cur = nxt_box[0]
                self.hT_swap()
                for si, sub in enumerate(subs):
                    np_, oc = sub["np"], sub["oc"]
                    xap = xt[0:np_, si, :]
                    k = self.stat_i % 8
                    self.stat_i += 1
                    ss = self.stat[0:np_, 2 * k:2 * k + 1]
                    rs = self.stat[0:np_, 2 * k + 1:2 * k + 2]
                    sb_ = self.stat_b[k]
                    self.act(self.sq[0:np_, :], xap, AF.Square, [xb[si]], [sb_], accum_out=ss)
                    self.act(ss, ss, AF.Sqrt, [sb_], [sb_], scale=1.0 / D, bias=1e-6)
                    S.op("dve", lambda e, rs=rs, ss=ss: e.reciprocal(out=rs, in_=ss), [sb_], [sb_])
                    yt, ytb = ytr.next()
                    self.stt("dve", yt[0:np_, :], xap, rs, self.gb[0:np_, 1, :], ALU.mult, ALU.mult,
                             [xb[si], sb_, self.gb_b], [ytb])
                    self.store(self.y_own[oc:oc + np_, :], yt[0:np_, :], [ytb])

def _consts():
    ident = np.eye(128, dtype=np.float32)
    J = ident[::-1].copy()
    q = np.arange(128)[:, None]
    c = np.arange(128)[None, :]
    M128 = np.where(c <= 127 - q, MASKV, 0.0).astype(np.float32)
    M32 = np.zeros((128, 32), np.float32)
    M32[0:32] = np.where(c[:, 0:32] <= 31 - q[0:32], MASKV, 0.0)
    c_bf = np.concatenate([ident, J, M128, M32], axis=1).astype(ml_dtypes.bfloat16)
    s = np.arange(128)[:, None].astype(np.float64)
    t = np.arange(128)[None, :].astype(np.float64)
    bc, bp, bs = [], [], []
    for w in WINS:
        bc.append(np.where((s > t - w) & (s <= t), 1.0 / w, 0.0) - (s == t))
        bp.append(np.where(s > t - w + 128, 1.0 / w, 0.0))
        cnt = np.minimum(w, t + 1)
        bs.append(np.where((s > t - w) & (s <= t), 1.0 / cnt, 0.0) - (s == t))
    bdc = np.zeros((128, 128))
    bdp = np.zeros((128, 128))
    s3 = np.arange(32)[:, None]
    t3 = np.arange(32)[None, :]
    r3 = np.arange(15)[:, None]
    for g, w in enumerate(WINS):
        bdc[0:32, g * 32:(g + 1) * 32] = np.where((s3 > t3 - w) & (s3 <= t3), 1.0 / w, 0.0) - (s3 == t3)
        bdp[0:15, g * 32:(g + 1) * 32] = np.where(r3 >= 16 + t3 - w, 1.0 / w, 0.0)
    c_f32 = np.concatenate(bc + bp + bs + [bdc, bdp], axis=1).astype(np.float32)
    return np.ascontiguousarray(c_bf), np.ascontiguousarray(c_f32)


_NC_CACHE = {}


def _get_nc():
    if "nc" not in _NC_CACHE:
        _NC_CACHE["nc"] = Builder().build()
    return _NC_CACHE["nc"]


def kernel(x_prompt, x_sample, cache_k, cache_v, state_pool,
           ffn1_norm, ffn1_gate, ffn1_up, ffn1_down, mix_norm, w_in, pool_w, pool_scale,
           w_branch_pool, w_branch_sb, w_out, ffn2_norm, ffn2_gate, ffn2_up, ffn2_down, final_norm):
    f = lambda a: np.ascontiguousarray(np.asarray(a, dtype=np.float32))
    x_prompt, x_sample = f(x_prompt), f(x_sample)
    cache_k, cache_v, state_pool = f(cache_k), f(cache_v), f(state_pool)
    c_bf, c_f32 = _consts()
    gains = np.stack([f(ffn1_norm)[0], f(mix_norm)[0], f(ffn2_norm)[0], f(final_norm)], axis=0)
    pscale = np.ascontiguousarray(f(pool_scale)[0].reshape(4, 128).T)
    shared = {
        "gains": np.ascontiguousarray(gains), "pscale": pscale, "poolw": f(pool_w)[0],
        "w_g1": f(ffn1_gate)[0], "w_u1": f(ffn1_up)[0], "w_d1": f(ffn1_down)[0], "w_in": f(w_in)[0],
        "w_bp": f(w_branch_pool)[0], "w_sb": f(w_branch_sb)[0], "w_o": f(w_out)[0],
        "w_g2": f(ffn2_gate)[0], "w_u2": f(ffn2_up)[0], "w_d2": f(ffn2_down)[0],
        "c_bf": c_bf, "c_f32": c_f32,
    }
    in_maps = []
    for c in range(8):
        b, par = c // 2, c % 2
        xv = np.zeros((NB * 128, D), np.float32)
        if par == 0:
            xv[0:8192] = x_prompt[b]
        else:
            xv[256:NB * 128] = x_prompt[b][0:NB * 128 - 256]
        m = dict(shared)
        m.update({"xv": xv, "xs": x_sample[c], "ck": cache_k[0, c], "cv": cache_v[0, c],
                  "spool": state_pool[0, c]})
        in_maps.append(m)
    nc = _get_nc()
    res = run_bass_kernel_spmd(nc, in_maps, core_ids=list(range(8)))
    R = res.results
    B, SEQ = 4, 8192
    y_prompt = np.zeros((B, SEQ, D), np.float32)
    y_sample = np.zeros((8, 32, D), np.float32)
    nkp = np.zeros((1, B, 8, SEQ, 64), np.float32)
    nvp = np.zeros((1, B, 8, SEQ, 64), np.float32)
    npp = np.zeros((1, B, 15, 512), np.float32)
    nks = np.zeros((1, 8, 8, 32, 64), np.float32)
    nvs = np.zeros((1, 8, 8, 32, 64), np.float32)
    nps = np.zeros((1, 8, 15, 512), np.float32)
    for c in range(8):
        b, par = c // 2, c % 2
        r = R[c]
        for k, vb in enumerate(OWN):
            rb = vb - 2 * par
            if rb < 0 or rb >= 64:
                continue
            y_prompt[b, rb * 128:(rb + 1) * 128] = r["y_own"][k * 128:(k + 1) * 128]
            nkp[0, b, :, rb * 128:(rb + 1) * 128] = r["nk_own"][:, k * 128:(k + 1) * 128]
            nvp[0, b, :, rb * 128:(rb + 1) * 128] = r["nv_own"][:, k * 128:(k + 1) * 128]
        y_sample[c] = r["y_own"][DEC0:DEC0 + 32]
        nks[0, c] = r["nk_own"][:, DEC0:DEC0 + 32]
        nvs[0, c] = r["nv_own"][:, DEC0:DEC0 + 32]
        nps[0, c] = r["pool_s"]
        if par == 0:
            npp[0, b] = r["pool_p"]
    return (y_prompt, y_sample, nkp, nvp, npp, nks, nvs, nps)
```

```python
import numpy as np
import ml_dtypes
from contextlib import ExitStack
import concourse.bass as bass
import concourse.mybir as mybir
from concourse.bass_utils import run_bass_kernel_spmd

F32 = mybir.dt.float32
BF16 = mybir.dt.bfloat16
AF = mybir.ActivationFunctionType
ALU = mybir.AluOpType

D = 1024
DFF = 4096
NB = 65
OWN = [0] + [x for m in range(16) for x in (4 * m + 3, 4 * m + 4)]
SLOT = {b: i for i, b in enumerate(OWN)}
NS = len(OWN)
NQC = NS * 128 + 32
DEC0 = NS * 128
NPAGE = 8
LOOKAHEAD = NPAGE - 1
MASKV = -30000.0
WINS = (2, 4, 8, 16)


class Buf:
    __slots__ = ("w", "rc", "rd", "name")

    def __init__(self, name=""):
        self.w = None
        self.rc = {}
        self.rd = []
        self.name = name


class DSem:
    def __init__(self, h):
        self.h = h
        self.v = 0
        self.buf = Buf("dsem")


class Op:
    __slots__ = ("eng", "fn", "deps", "needed", "val", "dsem", "phase")


class Sched:
    ENG = ("pe", "act", "dve", "pool", "sp")
    COMPUTE = ("pe", "act", "dve", "pool")

    def __init__(self, nc, csem):
        self.nc = nc
        self.csem = csem
        self.q = {e: [] for e in self.ENG}
        self.cnt = {e: 0 for e in self.COMPUTE}
        self.known = {}
        self.phase = 0
        self.dsems = []

    def dsem(self, h):
        d = DSem(h)
        self.dsems.append(d)
        return d

    def op(self, eng, fn, reads=(), writes=(), dsem=None):
        o = Op()
        o.eng = eng
        o.fn = fn
        o.needed = False
        o.val = None
        o.dsem = dsem
        o.phase = self.phase
        deps = []
        if dsem is not None:
            writes = list(writes) + [dsem.buf]
        for b in reads:
            if b.w is not None:
                deps.append(b.w)
        for b in writes:
            if b.w is not None:
                deps.append(b.w)
            deps.extend(b.rc.values())
            deps.extend(b.rd)
        ph = self.phase
        deps = [d for d in deps if d[-1] == ph]
        o.deps = deps
        for d in deps:
            if d[0] == "c":
                d[1].needed = True
        if dsem is not None:
            dsem.v += 16
            tok = ("d", dsem, dsem.v, ph)
        else:
            tok = ("c", o, ph)
        for b in reads:
            if tok[0] == "c":
                b.rc[eng] = tok
            else:
                b.rd.append(tok)
        for b in writes:
            b.w = tok
            b.rc = {}
            b.rd = []
        self.q[eng].append(o)
        return tok

    def _emit(self, e, eng):
        known = self.known
        for o in self.q[eng]:
            need = {}
            for d in o.deps:
                if d[0] == "c":
                    src = d[1]
                    if src.eng == eng and eng == "pe":
                        continue
                    key = ("c", src.eng)
                    val = src.val
                    sem = self.csem[src.eng]
                else:
                    key = ("d", id(d[1]))
                    val = d[2]
                    sem = d[1].h
                if known.get((eng, key), 0) >= val:
                    continue
                if key not in need or need[key][1] < val:
                    need[key] = (sem, val)
            for key, (sem, val) in need.items():
                e.wait_ge(sem, val)
                known[(eng, key)] = val
            ins = o.fn(e)
            if o.dsem is not None:
                ins.then_inc(o.dsem.h, 16)
            elif o.needed:
                ins.then_inc(self.csem[eng], 1)

    def flush(self):
        nc = self.nc
        for eng in self.COMPUTE:
            for o in self.q[eng]:
                if o.needed:
                    self.cnt[eng] += 1
                    o.val = self.cnt[eng]
        with nc.Block() as block:
            @block.tensor
            def _(e):
                self._emit(e, "pe")

            @block.scalar
            def _(e):
                self._emit(e, "act")

            @block.vector
            def _(e):
                self._emit(e, "dve")

            @block.gpsimd
            def _(e):
                self._emit(e, "pool")

            @block.sync
            def _(e):
                self._emit(e, "sp")
                for d in self.dsems:
                    if d.v > 0 and self.known.get(("sp", ("d", id(d))), 0) < d.v:
                        e.wait_ge(d.h, d.v)
                        self.known[("sp", ("d", id(d)))] = d.v
        self.q = {e: [] for e in self.ENG}
        self.phase += 1


class Ring:
    def __init__(self, tiles):
        self.t = tiles
        self.b = [Buf() for _ in tiles]
        self.i = -1

    def next(self):
        self.i = (self.i + 1) % len(self.t)
        return self.t[self.i], self.b[self.i]

    def cur(self):
        return self.t[self.i], self.b[self.i]


class Builder:
    def __init__(self, dbg_tiles=None, phases=(1, 2, 3)):
        self.nc = bass.Bass("TRN2", target_bir_lowering=False)
        self.es = ExitStack()
        self.dbg_tiles = dbg_tiles
        self.phases = phases
        self.skip = set()
        self.dbg_stage = 99
        self.dbg_slots = [0, 1]
        self.dbg_tiles3 = [0, 8]

    def dram(self, name, shape, dt, kind):
        return self.nc.dram_tensor(name, list(shape), dt, kind=kind).ap()

    def sb(self, st, name, shape, dt):
        return st.enter_context(self.nc.sbuf_tensor(f"{name}_p{self.S.phase}", list(shape), dt))

    def ps(self, st, name, shape, dt):
        return st.enter_context(self.nc.psum_tensor(f"{name}_p{self.S.phase}", list(shape), dt))

    def sem(self, name):
        return self.es.enter_context(self.nc.semaphore(name))

    def build(self):
        nc = self.nc
        A = self.dram
        self.xv = A("xv", [NB * 128, D], F32, "ExternalInput")
        self.xs = A("xs", [32, D], F32, "ExternalInput")
        self.ck = A("ck", [8, 1024, 64], F32, "ExternalInput")
        self.cv = A("cv", [8, 1024, 64], F32, "ExternalInput")
        self.spool = A("spool", [15, 512], F32, "ExternalInput")
        self.gains = A("gains", [4, D], F32, "ExternalInput")
        self.pscale = A("pscale", [128, 4], F32, "ExternalInput")
        self.poolw = A("poolw", [4, 128, 128], F32, "ExternalInput")
        self.w_f32 = {
            "g1": A("w_g1", [D, DFF], F32, "ExternalInput"),
            "u1": A("w_u1", [D, DFF], F32, "ExternalInput"),
            "d1": A("w_d1", [DFF, D], F32, "ExternalInput"),
            "in": A("w_in", [D, DFF], F32, "ExternalInput"),
            "bp": A("w_bp", [512, D], F32, "ExternalInput"),
            "sb": A("w_sb", [512, D], F32, "ExternalInput"),
            "o": A("w_o", [D, D], F32, "ExternalInput"),
            "g2": A("w_g2", [D, DFF], F32, "ExternalInput"),
            "u2": A("w_u2", [D, DFF], F32, "ExternalInput"),
            "d2": A("w_d2", [DFF, D], F32, "ExternalInput"),
        }
        self.c_bf = A("c_bf", [128, 128 * 3 + 32], BF16, "ExternalInput")
        self.c_f32 = A("c_f32", [128, 3 * 512 + 2 * 128], F32, "ExternalInput")
        self.y_own = A("y_own", [NQC, D], F32, "ExternalOutput")
        self.nk_own = A("nk_own", [8, NQC, 64], F32, "ExternalOutput")
        self.nv_own = A("nv_own", [8, NQC, 64], F32, "ExternalOutput")
        self.pool_p = A("pool_p", [15, 512], F32, "ExternalOutput")
        self.pool_s = A("pool_s", [15, 512], F32, "ExternalOutput")
        self.w_bf = {k: A("s_" + k, v.shape, BF16, "Internal") for k, v in self.w_f32.items()}
        self.w_cbuf = {k: [] for k in self.w_f32}
        self.x1s = A("s_x1", [NQC, D], F32, "Internal")
        self.hT2s = A("s_hT2", [128, 8, NQC], BF16, "Internal")
        self.QTs = A("s_QT", [4, 128, NQC], BF16, "Internal")
        self.KTs = A("s_KT", [4, 128, NB * 128], BF16, "Internal")
        self.VRs = A("s_VR", [4, 128, NB, 128], BF16, "Internal")
        self.KTDs = A("s_KTD", [4, 128, 32], BF16, "Internal")
        self.VRDs = A("s_VRD", [32, 512], BF16, "Internal")
        self.aTs = A("s_aT", [128, 4, NQC], BF16, "Internal")
        self.OTs = A("s_OT", [4, 128, NQC], BF16, "Internal")

        csem = {e: self.sem("c_" + e) for e in Sched.COMPUTE}
        self.S = Sched(nc, csem)
        self.page_sems = [self.S.dsem(self.sem(f"pg{i}")) for i in range(NPAGE)]
        self.ld_sems = [self.S.dsem(self.sem(f"ld{i}")) for i in range(16)]
        self.st_sems = [self.S.dsem(self.sem(f"st{i}")) for i in range(8)]
        self.cast_sems2 = [self.S.dsem(self.sem(f"cs{i}")) for i in range(2)]
        self._cast_i = 0
        self._ld_i = 0
        self._st_i = 0

        if 1 in self.phases:
            self.phase1()
            self.S.flush()
        if 2 in self.phases:
            self.phase2()
            self.S.flush()
        if 3 in self.phases:
            self.phase3()
            self.S.flush()
        self.es.close()
        return nc

    def load(self, out, in_, wbufs, rbufs=(), eng="sp"):
        d = self.ld_sems[self._ld_i % len(self.ld_sems)]
        self._ld_i += 1
        return self.S.op(eng, lambda e: e.dma_start(out=out, in_=in_), reads=rbufs, writes=wbufs, dsem=d)

    def store(self, out, in_, rbufs, wbufs=(), eng="sp"):
        d = self.st_sems[self._st_i % len(self.st_sems)]
        self._st_i += 1
        return self.S.op(eng, lambda e: e.dma_start(out=out, in_=in_), reads=rbufs, writes=wbufs, dsem=d)

    def mm(self, out, lhsT, rhs, start, stop, reads, writes):
        self.S.op("pe", lambda e: e.matmul(out, lhsT=lhsT, rhs=rhs, start=start, stop=stop), reads, writes)

    def tr(self, out, in_, ident, reads, writes):
        self.S.op("pe", lambda e: e.transpose(out, in_, ident), reads, writes)

    def act(self, out, in_, func, reads, writes, scale=1.0, bias=0.0, accum_out=None):
        if accum_out is None:
            self.S.op("act", lambda e: e.activation(out=out, in_=in_, func=func, scale=scale, bias=bias),
                      reads, writes)
        else:
            self.S.op("act", lambda e: e.activation(out=out, in_=in_, func=func, scale=scale, bias=bias,
                                                    accum_out=accum_out), reads, writes)

    def copy(self, eng, out, in_, reads, writes):
        if eng == "act":
            self.S.op("act", lambda e: e.activation(out=out, in_=in_, func=AF.Copy), reads, writes)
        else:
            self.S.op(eng, lambda e: e.tensor_copy(out=out, in_=in_), reads, writes)

    def tt(self, eng, out, in0, in1, op, reads, writes):
        self.S.op(eng, lambda e: e.tensor_tensor(out=out, in0=in0, in1=in1, op=op), reads, writes)

    def stt(self, eng, out, in0, scalar, in1, op0, op1, reads, writes):
        self.S.op(eng, lambda e: e.scalar_tensor_tensor(out=out, in0=in0, scalar=scalar, in1=in1,
                                                        op0=op0, op1=op1), reads, writes)

    def wstream_init(self, st, plan):
        self.pages = [self.sb(st, f"page{i}", [128, 4096], BF16) for i in range(NPAGE)]
        self.page_bufs = [Buf(f"page{i}") for i in range(NPAGE)]
        self.wplan = plan
        self.w_issued = 0
        self.w_used = 0
        self.w_rel = 0

    def _issue(self, upto):
        while self.w_issued < min(upto, len(self.wplan), self.w_rel + NPAGE):
            i = self.w_issued
            key, src, shp, rng = self.wplan[i]
            rbufs = [b_ for (r0, r1, c0, c1, b_) in self.w_cbuf[key]
                     if r0 < rng[1] and rng[0] < r1 and c0 < rng[3] and rng[2] < c1]
            pg = i % NPAGE
            dst = self.pages[pg][:, 0:shp[1] * shp[2]].rearrange("p (a b) -> p a b", a=shp[1])
            if shp[0] < 128:
                dst = self.pages[pg][0:shp[0], 0:shp[1] * shp[2]].rearrange("p (a b) -> p a b", a=shp[1])
            self.S.op("sp", lambda e, dst=dst, src=src: e.dma_start(out=dst, in_=src),
                      reads=rbufs, writes=[self.page_bufs[pg]], dsem=self.page_sems[pg])
            self.w_issued += 1

    def wget(self, key):
        i = self.w_used
        k, src, shp, _ = self.wplan[i]
        assert k == key, (k, key, i)
        self._issue(i + 1 + LOOKAHEAD)
        self.w_used += 1
        pg = i % NPAGE
        v = self.pages[pg][0:shp[0], 0:shp[1] * shp[2]].rearrange("p (a b) -> p a b", a=shp[1])
        return v, self.page_bufs[pg]

    def wrel(self):
        self.w_rel = self.w_used

    def wsrc_cols(self, key, c0, ncol=512):
        w = self.w_bf[key]
        return (key, w[:, c0:c0 + ncol].rearrange("(kc p) n -> p kc n", p=128), (128, 8, ncol),
                (0, 1024, c0, c0 + ncol))

    def wsrc_rows(self, key, r0, nrow, c0, ncol):
        w = self.w_bf[key]
        return (key, w[r0:r0 + nrow, c0:c0 + ncol].rearrange("(kc p) n -> p kc n", p=128),
                (128, nrow // 128, ncol), (r0, r0 + nrow, c0, c0 + ncol))

    def cast_weights(self, keys):
        if "cast" in self.skip:
            return
        work = []
        for k in keys:
            if ("cast_" + k) in self.skip:
                continue
            src = self.w_f32[k]
            dst = self.w_bf[k]
            rows, cols = src.shape
            if rows == 1024 and cols == 4096:
                chunks = [(slice(0, rows), slice(c, c + 512)) for c in range(0, cols, 512)]
            else:
                step = max(1, min(256, (1 << 20) // cols))
                chunks = [(slice(r, r + step), slice(0, cols)) for r in range(0, rows, step)]
            item = []
            for rs, cs in chunks:
                b_ = Buf()
                self.w_cbuf[k].append((rs.start, rs.stop, cs.start, cs.stop, b_))
                item.append((b_, dst[rs, cs], src[rs, cs]))
            work.append(item)
        order = []
        if len(work) >= 2 and len(work[0]) == len(work[1]):
            for a, b in zip(work[0], work[1]):
                order += [a, b]
            work = work[2:]
        for w_ in work:
            order += w_
        for b_, o, s_ in order:
            d = self.cast_sems2[self._cast_i % 2]
            self._cast_i += 1
            self.S.op("pool", lambda e, o=o, s_=s_: e.dma_start(out=o, in_=s_), reads=[],
                      writes=[b_], dsem=d)

    def common_alloc(self, st, gidx, with_cf=True):
        nc = self.nc
        self.cbf = self.sb(st, "cbf", [128, 128 * 3 + 32], BF16)
        self.cbf_b = Buf("cbf")
        if with_cf:
            self.cf = self.sb(st, "cf", [128, 3 * 512 + 256], F32)
        self.cf_b = Buf("cf")
        self.gb = self.sb(st, "gb", [128, 2, D], F32)
        self.gb_b = Buf("gb")
        self.load(self.cbf[:], self.c_bf, [self.cbf_b])
        if with_cf:
            self.load(self.cf[:], self.c_f32, [self.cf_b])
        for i, gi in enumerate(gidx):
            if "gb" in self.skip:
                break
            self.load(self.gb[:, i, :], self.gains[gi:gi + 1, :].partition_broadcast(128), [self.gb_b])
        self.ident = self.cbf[:, 0:128]
        self.J = self.cbf[:, 128:256]
        self.xt = Ring([self.sb(st, f"xt{i}", [128, 4, D], F32) for i in range(2)])
        self.xt_sb = [[Buf() for _ in range(4)] for _ in range(2)]
        self.hb = Ring([self.sb(st, f"hb{i}", [128, D], BF16) for i in range(4)])
        self.sq = self.sb(st, "sq", [128, D], BF16)
        self.stat = self.sb(st, "stat", [128, 16], F32)
        self.stat_b = [Buf() for _ in range(8)]
        self.stat_i = 0
        self.hT_ring = [self.sb(st, f"hT{i}", [128, 8, 512], BF16) for i in range(2)]
        self.hT_bufs = [[Buf() for _ in range(4)] for _ in range(2)]
        self.hT_i = 0
        self.hT = self.hT_ring[0]
        self.hT_b = self.hT_bufs[0]
        self.actT = self.sb(st, "actT", [128, 32, 512], BF16)
        self.actT_b = [Buf() for _ in range(32)]
        self.sgate = Ring([self.sb(st, f"sgate{i}", [128, 512], F32) for i in range(2)])
        self.pmm = Ring([self.ps(st, f"pmm{i}", [128, 512], F32) for i in range(4)])
        self.pacc = Ring([self.ps(st, f"pacc{i}", [128, 512], F32) for i in range(2)])
        self.pmisc = Ring([self.ps(st, f"pmisc{i}", [128, 512], F32) for i in range(2)])
        self.pmisc_bf = [t[:].bitcast(BF16) for t in self.pmisc.t] if False else None

    def rmsnorm_hT(self, xap, np_, col, gi, xbuf, hTbuf, defer=False):
        si = self.stat_i % 8
        self.stat_i += 1
        ss = self.stat[0:np_, 2 * si:2 * si + 1]
        rs = self.stat[0:np_, 2 * si + 1:2 * si + 2]
        sb_ = self.stat_b[si]
        self.act(self.sq[0:np_, :], xap, AF.Square, [xbuf], [sb_], accum_out=ss)
        self.act(ss, ss, AF.Sqrt, [sb_], [sb_], scale=1.0 / D, bias=1e-6)
        self.S.op("dve", lambda e: e.reciprocal(out=rs, in_=ss), [sb_], [sb_])
        hb, hbb = self.hb.next()
        self.stt("dve", hb[0:np_, :], xap, rs, self.gb[0:np_, gi, :], ALU.mult, ALU.mult,
                 [xbuf, sb_, self.gb_b], [hbb])
        if defer:
            return (hb, hbb, np_, col, hTbuf)
        self.rmsnorm_part2((hb, hbb, np_, col, hTbuf))

    def rmsnorm_part2(self, h):
        hb, hbb, np_, col, hTbuf = h
        pm, pmb = self.pmisc.next()
        pv = pm[:].bitcast(BF16)
        for c in range(8):
            self.tr(pv[:, c * np_:(c + 1) * np_], hb[0:np_, c * 128:(c + 1) * 128], self.ident[0:np_, 0:np_],
                    [hbb, self.cbf_b], [pmb])
        self.copy("dve", self.hT[:, :, col:col + np_],
                  pv[:, 0:8 * np_].rearrange("p (c n) -> p c n", c=8), [pmb], [hTbuf])

    def hT_swap(self):
        self.hT_i ^= 1
        self.hT = self.hT_ring[self.hT_i]
        self.hT_b = self.hT_bufs[self.hT_i]

    def ffn(self, kg, ku, kd, subs, ntok, xt, xbufs, mid=None):
        hTr = self.hT_b
        for jg in range(8):
            G, Gb = self.wget(kg)
            U, Ub = self.wget(ku)
            for jj in range(4):
                j = jg * 4 + jj
                pg, pgb = self.pmm.next()
                for kc in range(8):
                    self.mm(pg[:, 0:ntok], G[:, kc, jj * 128:(jj + 1) * 128], self.hT[:, kc, 0:ntok],
                            kc == 0, kc == 7, [Gb] + hTr, [pgb])
                pu, pub = self.pmm.next()
                for kc in range(8):
                    self.mm(pu[:, 0:ntok], U[:, kc, jj * 128:(jj + 1) * 128], self.hT[:, kc, 0:ntok],
                            kc == 0, kc == 7, [Ub] + hTr, [pub])
                sg, sgb = self.sgate.next()
                self.act(sg[:, 0:ntok], pg[:, 0:ntok], AF.Silu, [pgb], [sgb])
                self.tt("dve", self.actT[:, j, 0:ntok], sg[:, 0:ntok], pu[:, 0:ntok], ALU.mult,
                        [sgb, pub], [self.actT_b[j]])
            self.wrel()
        if mid is not None:
            mid()
        for hh in range(2):
            Dp = [self.wget(kd) for _ in range(4)]
            for si, sub in enumerate(subs):
                np_, col = sub["np"], sub["col"]
                pa, pab = self.pacc.next()
                for fg in range(4):
                    Dv, Db = Dp[fg]
                    for fc in range(8):
                        f = fg * 8 + fc
                        self.mm(pa[0:np_, :], self.actT[:, f, col:col + np_], Dv[:, fc, :],
                                f == 0, f == 31, [self.actT_b[f], Db], [pab])
                xs_ = xt[0:np_, si, hh * 512:(hh + 1) * 512]
                self.stt("dve", xs_, pa[0:np_, :], 0.5, xs_, ALU.mult, ALU.add, [pab, xbufs[si]], [xbufs[si]])
            self.wrel()

    def tiles1(self):
        tl = []
        for ti in range(16):
            subs = []
            for s in range(4):
                b = 4 * ti + s
                subs.append(dict(kind="p", blk=b, np=128, col=s * 128, own=b in SLOT, slot=SLOT.get(b)))
            tl.append(dict(subs=subs, ntok=512))
        subs = [dict(kind="p", blk=64, np=128, col=0, own=True, slot=SLOT[64]),
                dict(kind="d", blk=None, np=32, col=128, own=True, slot=None)]
        tl.append(dict(subs=subs, ntok=160))
        return tl

    def phase1(self):
        nc = self.nc
        S = self.S
        tiles = self.tiles1()
        if self.dbg_tiles is not None:
            tiles = [tiles[i] for i in self.dbg_tiles]
        plan = []
        if self.dbg_stage < 4:
            tiles = [dict(t, noproj=True) for t in tiles]
        for t in tiles:
            for jg in range(8):
                plan.append(self.wsrc_cols("g1", jg * 512))
                plan.append(self.wsrc_cols("u1", jg * 512))
            for hh in range(2):
                for fg in range(4):
                    plan.append(self.wsrc_rows("d1", fg * 1024, 1024, hh * 512, 512))
            if self.dbg_stage < 2:
                plan = []
            if t.get("noproj"):
                continue
            for c0 in (0, 1024, 1536, 512):
                plan.append(self.wsrc_cols("in", c0))
        with ExitStack() as st:
            self.cast_weights(["g1", "u1", "d1", "in"])
            self.common_alloc(st, (0, 1))
            self.wstream_init(st, plan)
            uring = Ring([self.sb(st, f"u{i}", [128, 512], F32) for i in range(3)])
            ktm = Ring([self.sb(st, f"ktm{i}", [128, 512], F32) for i in range(1)])
            vtm = Ring([self.sb(st, f"vtm{i}", [128, 512], F32) for i in range(1)])
            kbf = Ring([self.sb(st, f"kbf{i}", [128, 512], BF16) for i in range(2)])
            vbf = Ring([self.sb(st, f"vbf{i}", [128, 512], BF16) for i in range(2)])
            ktr = Ring([self.sb(st, f"ktr{i}", [128, 4, 128], BF16) for i in range(2)])
            vrv = Ring([self.sb(st, f"vrv{i}", [128, 512], BF16) for i in range(2)])
            qts = Ring([self.sb(st, f"qts{i}", [128, 4, 128], BF16) for i in range(2)])
            dts = Ring([self.sb(st, f"dts{i}", [128, 4, 128], BF16) for i in range(2)])
            ats = Ring([self.sb(st, f"ats{i}", [128, 4, 128], BF16) for i in range(2)])
            pwf = self.sb(st, "pwf", [128, 4, 128], F32)
            pwb = self.sb(st, "pwb", [128, 4, 128], BF16)
            pw_b = Buf()
            psc = self.sb(st, "psc", [128, 4], F32)
            psc_b = Buf()
            spl = self.sb(st, "spl", [15, 512], F32)
            spl_b = Buf()
            self.load(pwf[:], self.poolw.rearrange("g c e -> c g e"), [pw_b])
            self.copy("dve", pwb[:], pwf[:], [pw_b], [pw_b])
            self.load(psc[:], self.pscale, [psc_b])
            self.load(spl[:], self.spool, [spl_b])
            cf = self.cf
            bc = lambda g: cf[:, g * 128:(g + 1) * 128]
            bp = lambda g: cf[:, 512 + g * 128:512 + (g + 1) * 128]
            bs = lambda g: cf[:, 1024 + g * 128:1024 + (g + 1) * 128]
            bdc = lambda g: cf[0:32, 1536 + g * 32:1536 + (g + 1) * 32]
            bdp = lambda g: cf[0:15, 1664 + g * 32:1664 + (g + 1) * 32]

            u_prev = None

            def load_x(t):
                xt, _ = self.xt.next()
                xb = self.xt_sb[self.xt.i]
                for si, sub in enumerate(t["subs"]):
                    if sub["kind"] == "p":
                        b = sub["blk"]
                        self.load(xt[:, si, :], self.xv[b * 128:(b + 1) * 128, :], [xb[si]])
                    else:
                        self.load(xt[0:32, si, :], self.xs, [xb[si]])
                return xt, xb

            def norm1_part1(t, xt, xb, hTb):
                return [self.rmsnorm_hT(xt[0:sub["np"], si, :], sub["np"], sub["col"], 0, xb[si], hTb[si], defer=True)
                        for si, sub in enumerate(t["subs"])]

            cur = load_x(tiles[0])
            for h in norm1_part1(tiles[0], cur[0], cur[1], self.hT_b):
                self.rmsnorm_part2(h)
            for ti, t in enumerate(tiles):
                subs, ntok = t["subs"], t["ntok"]
                xt, xb = cur
                nxt = None
                pend = []
                if ti + 1 < len(tiles):
                    nxt = load_x(tiles[ti + 1])
                    pend = norm1_part1(tiles[ti + 1], nxt[0], nxt[1], self.hT_bufs[self.hT_i ^ 1])

                def mid(pend=pend):
                    keep = (self.hT, self.hT_b)
                    self.hT_swap()
                    for h in pend:
                        self.rmsnorm_part2(h)
                    self.hT_swap()
                    assert keep[0] is self.hT

                self.ffn("g1", "u1", "d1", subs, ntok, xt, xb, mid=mid)
                cur = nxt
                if self.dbg_stage < 3:
                    continue
                for si, sub in enumerate(subs):
                    np_ = sub["np"]
                    if sub["own"]:
                        oc = sub["slot"] * 128 if sub["kind"] == "p" else DEC0
                        self.store(self.x1s[oc:oc + np_, :], xt[0:np_, si, :], [xb[si]])
                    self.rmsnorm_hT(xt[0:np_, si, :], np_, sub["col"], 1, xb[si], self.hT_b[si])
                    if sub["own"]:
                        self.store(self.hT2s[:, :, oc:oc + np_], self.hT[:, :, sub["col"]:sub["col"] + np_],
                                   [self.hT_b[si]])
                if self.dbg_stage < 4:
                    continue
                Uw, Uwb = self.wget("in")
                Kw, Kwb = self.wget("in")
                Vw, Vwb = self.wget("in")
                Qw, Qwb = self.wget("in")
                for si, sub in enumerate(subs):
                    np_, col = sub["np"], sub["col"]
                    dec = sub["kind"] == "d"
                    own = sub["own"]
                    oc = (sub["slot"] * 128 if not dec else DEC0) if own else None
                    hTs = [self.hT_b[si]]
                    Jn = self.J[0:np_, 128 - np_:128]
                    pa, pab = self.pacc.next()
                    for kc in range(8):
                        self.mm(pa[0:np_, :], self.hT[:, kc, col:col + np_], Uw[:, kc, :], kc == 0, kc == 7,
                                hTs + [Uwb], [pab])
                    ut, utb = uring.next()
                    self.copy("act", ut[0:np_, :], pa[0:np_, :], [pab], [utb])
                    pa, pab = self.pacc.next()
                    for kc in range(8):
                        self.mm(pa[0:np_, :], self.hT[:, kc, col:col + np_], Kw[:, kc, :], kc == 0, kc == 7,
                                hTs + [Kwb], [pab])
                    kb_, kbb = kbf.next()
                    if own and "nkst" not in self.skip:
                        km, kmb = ktm.next()
                        self.copy("act", km[0:np_, :], pa[0:np_, :], [pab], [kmb])
                        self.copy("pool", kb_[0:np_, :], km[0:np_, :], [kmb], [kbb])
                        for h in range(8):
                            self.store(self.nk_own[h, oc:oc + np_, :], km[0:np_, h * 64:(h + 1) * 64], [kmb])
                    else:
                        self.copy("act", kb_[0:np_, :], pa[0:np_, :], [pab], [kbb])
                    pm, pmb = self.pmisc.next()
                    for j in range(4):
                        self.mm(pm[:, j * np_:(j + 1) * np_], kb_[0:np_, j * 128:(j + 1) * 128], Jn, True, True,
                                [kbb, self.cbf_b], [pmb])
                    kr, krb = ktr.next()
                    self.copy("dve", kr[:, :, 0:np_], pm[:, 0:4 * np_].rearrange("p (j n) -> p j n", j=4),
                              [pmb], [krb])
                    if "ktst" in self.skip:
                        pass
                    elif not dec:
                        rb = 64 - sub["blk"]
                        self.store(self.KTs[:, :, rb * 128:(rb + 1) * 128].rearrange("j p n -> p j n"),
                                   kr[:], [krb])
                    else:
                        self.store(self.KTDs.rearrange("j p n -> p j n"), kr[:, :, 0:32], [krb])
                    pa, pab = self.pacc.next()
                    for kc in range(8):
                        self.mm(pa[0:np_, :], self.hT[:, kc, col:col + np_], Vw[:, kc, :], kc == 0, kc == 7,
                                hTs + [Vwb], [pab])
                    vb_, vbb = vbf.next()
                    if own and "nkst" not in self.skip:
                        vm, vmb = vtm.next()
                        self.copy("act", vm[0:np_, :], pa[0:np_, :], [pab], [vmb])
                        self.copy("pool", vb_[0:np_, :], vm[0:np_, :], [vmb], [vbb])
                        for h in range(8):
                            self.store(self.nv_own[h, oc:oc + np_, :], vm[0:np_, h * 64:(h + 1) * 64], [vmb])
                    else:
                        self.copy("act", vb_[0:np_, :], pa[0:np_, :], [pab], [vbb])
                    pm, pmb = self.pmisc.next()
                    self.mm(pm[0:np_, :], Jn, vb_[0:np_, :], True, True, [vbb, self.cbf_b], [pmb])
                    vr, vrb = vrv.next()
                    self.copy("dve", vr[0:np_, :], pm[0:np_, :], [pmb], [vrb])
                    if "ktst" in self.skip:
                        pass
                    elif not dec:
                        self.store(self.VRs[:, :, 64 - sub["blk"], :].rearrange("j p n -> p j n"),
                                   vr[:].rearrange("p (j n) -> p j n", j=4), [vrb])
                    else:
                        self.store(self.VRDs, vr[0:32, :], [vrb])
                    if own and "q" not in self.skip:
                        pm, pmb = self.pmisc.next()
                        for j in range(4):
                            for kc in range(8):
                                self.mm(pm[:, j * np_:(j + 1) * np_], Qw[:, kc, j * 128:(j + 1) * 128],
                                        self.hT[:, kc, col:col + np_], kc == 0, kc == 7, hTs + [Qwb], [pmb])
                        qt, qtb = qts.next()
                        self.copy("act", qt[:, :, 0:np_], pm[:, 0:4 * np_].rearrange("p (j n) -> p j n", j=4),
                                  [pmb], [qtb])
                        self.store(self.QTs[:, :, oc:oc + np_].rearrange("j p n -> p j n"), qt[:, :, 0:np_], [qtb])
                    if own and "pool" not in self.skip:
                        pm, pmb = self.pmisc.next()
                        for g in range(4):
                            ug = ut[0:np_, g * 128:(g + 1) * 128]
                            if dec:
                                self.mm(pm[:, g * np_:(g + 1) * np_], ug, bdc(g), True, False,
                                        [utb, self.cf_b], [pmb])
                                self.mm(pm[:, g * np_:(g + 1) * np_], spl[0:15, g * 128:(g + 1) * 128], bdp(g),
                                        False, True, [spl_b, self.cf_b], [pmb])
                            elif sub["blk"] == 0:
                                self.mm(pm[:, g * np_:(g + 1) * np_], ug, bs(g), True, True,
                                        [utb, self.cf_b], [pmb])
                            else:
                                upt, upb = u_prev
                                self.mm(pm[:, g * np_:(g + 1) * np_], ug, bc(g), True, False,
                                        [utb, self.cf_b], [pmb])
                                self.mm(pm[:, g * np_:(g + 1) * np_], upt[:, g * 128:(g + 1) * 128], bp(g),
                                        False, True, [upb, self.cf_b], [pmb])
                        dt_, dtb = dts.next()
                        self.copy("dve", dt_[:, :, 0:np_], pm[:, 0:4 * np_].rearrange("p (j n) -> p j n", j=4),
                                  [pmb], [dtb])
                        pm, pmb = self.pmisc.next()
                        for g in range(4):
                            self.mm(pm[:, g * np_:(g + 1) * np_], pwb[:, g, :], dt_[:, g, 0:np_], True, True,
                                    [pw_b, dtb], [pmb])
                        at, atb = ats.next()
                        for g in range(4):
                            self.S.op("dve", lambda e, o=at[:, g, 0:np_], i=pm[:, g * np_:(g + 1) * np_],
                                      s=psc[:, g:g + 1]: e.tensor_scalar(out=o, in0=i, scalar1=s, scalar2=None,
                                                                         op0=ALU.mult),
                                      [pmb, psc_b], [atb])
                        self.store(self.aTs[:, :, oc:oc + np_], at[:, :, 0:np_], [atb])
                    if not dec:
                        if sub["blk"] == 63:
                            self.store(self.pool_p, ut[113:128, :], [utb])
                        u_prev = (ut, utb)
                    else:
                        self.store(self.pool_s, ut[17:32, :], [utb])
                self.wrel()
                self.hT_swap()

    def phase2(self):
        S = self.S
        with ExitStack() as st:
            self.cast_weights(["g2", "u2", "bp", "sb", "o", "d2"])
            cbf = self.sb(st, "cbf2", [128, 128 * 3 + 32], BF16)
            cbf_b = Buf()
            self.load(cbf[:], self.c_bf, [cbf_b])
            ident = cbf[:, 0:128]
            J = cbf[:, 128:256]
            M128 = cbf[:, 256:384]
            M32 = cbf[0:32, 384:416]
            ones = self.sb(st, "ones", [128, 1024], BF16)
            ones_b = Buf()
            S.op("pool", lambda e: e.memset(ones[:], 1.0), [], [ones_b])
            ktp = Ring([self.sb(st, f"ktp{i}", [128, NB * 128], BF16) for i in range(2)])
            vrp = Ring([self.sb(st, f"vrp{i}", [128, NB, 128], BF16) for i in range(2)])
            qtp = Ring([(self.sb(st, f"qA{i}", [128, NQC], BF16), self.sb(st, f"qB{i}", [128, NQC], BF16))
                        for i in range(2)])
            for (qa_, qb_), qbuf_ in zip(qtp.t, qtp.b):
                S.op("pool", lambda e, t=qa_: e.memset(t[64:128, :], 0.0), [], [qbuf_])
                S.op("pool", lambda e, t=qb_: e.memset(t[0:64, :], 0.0), [], [qbuf_])
            ktd = Ring([self.sb(st, f"ktd{i}", [128, 32 + 1024], BF16) for i in range(2)])
            vrd = Ring([self.sb(st, f"vrd{i}", [128, 9, 128], BF16) for i in range(2)])
            ot = Ring([self.sb(st, f"ot{i}", [128, NQC], BF16) for i in range(2)])
            ckf = Ring([self.sb(st, f"ckf{i}", [128, 8, 2, 64], F32) for i in range(2)])
            ckb = Ring([self.sb(st, f"ckb{i}", [128, 8, 2, 64], BF16) for i in range(2)])
            zr = Ring([self.ps(st, f"z{i}", [128, 1024], F32) for i in range(2)])
            wtp = Ring([self.ps(st, f"wtp{i}", [128, 1024], BF16) for i in range(2)])
            oacc = Ring([self.ps(st, f"oacc{i}", [128, 512], F32) for i in range(2)])
            sg = Ring([self.sb(st, f"sg{i}", [128, 1024], F32) for i in range(3)])
            Pb = Ring([self.sb(st, f"Pb{i}", [128, 1025], F32) for i in range(3)])
            wb = Ring([self.sb(st, f"wb{i}", [128, 1024], BF16) for i in range(3)])
            wT = Ring([self.sb(st, f"wT{i}", [128, 1024], BF16) for i in range(3)])
            state = {"prevP": None, "oacc": None}

            def stageA(ch):
                nq, nc_ = ch["nq"], ch["ncols"]
                z, zb = zr.next()
                for off in range(0, nc_, 512):
                    n = min(512, nc_ - off)
                    self.mm(z[0:nq, off:off + n], ch["q"], ch["kT"][:, off:off + n], True,
                            not (ch["first"] and off == 0), ch["rb"], [zb])
                if ch["first"]:
                    M = M128 if nq == 128 else M32
                    self.mm(z[0:nq, 0:nq], ident[0:nq, 0:nq], M[0:nq, 0:nq], False, True, [cbf_b], [zb])
                sgt, sgb = sg.next()
                self.act(sgt[0:nq, 0:nc_], z[0:nq, 0:nc_], AF.Sigmoid, [zb], [sgb], scale=-0.125)
                pt, pb = Pb.next()
                if ch["first"]:
                    S.op("dve", lambda e: e.memset(pt[0:nq, 0:1], 1.0), [], [pb])
                else:
                    ppt, ppb, pn = state["prevP"]
                    self.copy("dve", pt[0:nq, 0:1], ppt[0:nq, pn:pn + 1], [ppb], [pb])
                S.op("dve", lambda e: e.tensor_tensor_scan(out=pt[0:nq, 1:nc_ + 1], data0=sgt[0:nq, 0:nc_],
                                                           data1=ones[0:nq, 0:nc_], initial=pt[0:nq, 0:1],
                                                           op0=ALU.mult, op1=ALU.mult),
                     [sgb, ones_b, pb], [pb])
                state["prevP"] = (pt, pb, nc_)
                ch["P"] = (pt, pb)

            def stageA2(ch):
                nq, nc_ = ch["nq"], ch["ncols"]
                pt, pb = ch["P"]
                wt, wbb = wb.next()
                self.tt("dve", wt[0:nq, 0:nc_], pt[0:nq, 0:nc_], pt[0:nq, 1:nc_ + 1], ALU.subtract, [pb], [wbb])
                ch["wb"] = (wt, wbb)

            def stageB(ch):
                nq = ch["nq"]
                wt, wbb = ch["wb"]
                tp, tpb = wtp.next()
                blocks = ch["blocks"]
                kb = blocks[0][1]
                off = 0
                for bi, (v_ap, kb_) in enumerate(blocks):
                    self.tr(tp[0:kb_, bi * 128:bi * 128 + nq], wt[0:nq, off:off + kb_], ident[0:nq, 0:nq],
                            [wbb, cbf_b], [tpb])
                    off += kb_
                wTt, wTb = wT.next()
                nb = len(blocks)
                self.copy("act", wTt[0:kb, 0:nb * 128].rearrange("p (b n) -> p b n", b=nb)[:, :, 0:nq],
                          tp[0:kb, 0:nb * 128].rearrange("p (b n) -> p b n", b=nb)[:, :, 0:nq], [tpb], [wTb])
                ch["wT"] = (wTt, wTb)

            def stageC(ch):
                nq = ch["nq"]
                blocks = ch["blocks"]
                nb = len(blocks)
                wTt, wTb = ch["wT"]
                if ch["first"]:
                    state["oacc"] = oacc.next()
                oa, oab = state["oacc"]
                pr0 = ch["pr0"]
                for bi, (v_ap, kb_) in enumerate(blocks):
                    self.mm(oa[:, 0:nq], v_ap, wTt[0:kb_, bi * 128:bi * 128 + nq],
                            ch["first"] and bi == 0, ch["last"] and bi == nb - 1, [wTb] + ch["vb"], [oab])
                if ch["last"]:
                    ott, otb = ch["ot"]
                    oc = ch["ocol"]
                    self.copy("act", ott[pr0:pr0 + 64, oc:oc + nq], oa[pr0:pr0 + 64, 0:nq], [oab], [otb])

            slots = list(range(NS)) if self.dbg_tiles is None else self.dbg_slots
            def prep_loads(j):
                kt, ktb = ktp.next()
                vr, vrb = vrp.next()
                qt, qtb = qtp.next()
                kd, kdb = ktd.next()
                vd, vdb = vrd.next()
                if self.dbg_tiles is None:
                    self.load(kt[:], self.KTs[j], [ktb])
                    self.load(vr[:], self.VRs[j], [vrb])
                    self.load(qt[0][0:64, :], self.QTs[j][0:64, :], [qtb])
                    self.load(qt[1][64:128, :], self.QTs[j][64:128, :], [qtb])
                else:
                    self.load(kt[:, 61 * 128:], self.KTs[j][:, 61 * 128:], [ktb])
                    self.load(vr[:, 61:, :], self.VRs[j][:, 61:, :], [vrb])
                    for hq in range(2):
                        hs = slice(hq * 64, hq * 64 + 64)
                        self.load(qt[hq][hs, 0:256], self.QTs[j][hs, 0:256], [qtb])
                        self.load(qt[hq][hs, DEC0:], self.QTs[j][hs, DEC0:], [qtb])
                self.load(kd[:, 0:32], self.KTDs[j], [kdb])
                self.load(vd[0:32, 0, :], self.VRDs[:, j * 128:(j + 1) * 128], [vdb])
                cfs = []
                for src in (self.ck, self.cv):
                    cf_, cfb = ckf.next()
                    for hh in range(2):
                        self.load(cf_[:, :, hh, :], src[2 * j + hh].rearrange("(b p) d -> p b d", p=128), [cfb])
                    cfs.append((cf_, cfb))
                return dict(kt=kt, ktb=ktb, vr=vr, vrb=vrb, qt=qt, qtb=qtb, kd=kd, kdb=kdb, vd=vd, vdb=vdb, cfs=cfs)

            def prep_compute(P):
                kd, kdb, vd, vdb = P["kd"], P["kdb"], P["vd"], P["vdb"]
                for (cf_, cfb), is_k in zip(P["cfs"], (True, False)):
                    cb_, cbb = ckb.next()
                    self.copy("pool", cb_[:], cf_[:], [cfb], [cbb])
                    z, zb = zr.next()
                    for blk in range(8):
                        rb = 7 - blk
                        cblk = cb_[:, blk, :, :].rearrange("p h d -> p (h d)")
                        if is_k:
                            self.mm(z[:, rb * 128:(rb + 1) * 128], cblk, J, True, True, [cbb, cbf_b], [zb])
                        else:
                            self.mm(z[:, rb * 128:(rb + 1) * 128], J, cblk, True, True, [cbb, cbf_b], [zb])
                    if is_k:
                        self.copy("act", kd[:, 32:1056], z[:, 0:1024], [zb], [kdb])
                    else:
                        self.copy("act", vd[:, 1:9, :], z[:, 0:1024].rearrange("p (b n) -> p b n", b=8), [zb], [vdb])


            nxtP = prep_loads(0)
            prep_compute(nxtP)
            for j in range(4):
                P = nxtP
                kt, ktb, vr, vrb, qt, qtb = P["kt"], P["ktb"], P["vr"], P["vrb"], P["qt"], P["qtb"]
                kd, kdb, vd, vdb = P["kd"], P["kdb"], P["vd"], P["vdb"]
                ott, otb = ot.next()
                if j < 3:
                    nxtP = prep_loads(j + 1)
                for hh in range(2):
                    if hh == 1 and j < 3:
                        prep_compute(nxtP)
                    pr0 = hh * 64
                    chunks = []
                    for k in slots:
                        a = OWN[k]
                        c0 = (64 - a) * 128
                        nblk = a + 1
                        for ci in range(0, nblk, 8):
                            nb = min(8, nblk - ci)
                            chunks.append(dict(
                                nq=128, ncols=nb * 128, first=ci == 0, last=ci + nb == nblk, pr0=pr0,
                                q=qt[hh][:, k * 128:(k + 1) * 128],
                                kT=kt[:, c0 + ci * 128:c0 + (ci + nb) * 128],
                                blocks=[(vr[:, 64 - a + ci + b, :], 128) for b in range(nb)],
                                rb=[qtb, ktb], vb=[vrb], ot=(ott, otb), ocol=k * 128))
                    chunks.append(dict(nq=32, ncols=32, first=True, last=False, pr0=pr0,
                                       q=qt[hh][:, DEC0:DEC0 + 32], kT=kd[:, 0:32],
                                       blocks=[(vd[0:32, 0, :], 32)],
                                       rb=[qtb, kdb], vb=[vdb], ot=(ott, otb), ocol=DEC0))
                    chunks.append(dict(nq=32, ncols=1024, first=False, last=True, pr0=pr0,
                                       q=qt[hh][:, DEC0:DEC0 + 32], kT=kd[:, 32:1056],
                                       blocks=[(vd[:, 1 + b, :], 128) for b in range(8)],
                                       rb=[qtb, kdb], vb=[vdb], ot=(ott, otb), ocol=DEC0))
                    n_ch = len(chunks)
                    for i in range(n_ch + 3):
                        if i < n_ch:
                            stageA(chunks[i])
                        if 0 <= i - 1 < n_ch:
                            stageA2(chunks[i - 1])
                        if 0 <= i - 2 < n_ch:
                            stageB(chunks[i - 2])
                        if 0 <= i - 3 < n_ch:
                            stageC(chunks[i - 3])
                if self.dbg_tiles is None:
                    self.store(self.OTs[j], ott[:], [otb])
                else:
                    self.store(self.OTs[j][:, 0:256], ott[:, 0:256], [otb])
                    self.store(self.OTs[j][:, DEC0:], ott[:, DEC0:], [otb])

    def tiles3(self):
        tl = []
        for t in range(8):
            subs = [dict(kind="p", np=128, col=s * 128, slot=4 * t + s, oc=(4 * t + s) * 128) for s in range(4)]
            tl.append(dict(subs=subs, ntok=512, oc=4 * t * 128))
        subs = [dict(kind="p", np=128, col=0, slot=32, oc=32 * 128),
                dict(kind="d", np=32, col=128, slot=None, oc=DEC0)]
        tl.append(dict(subs=subs, ntok=160, oc=32 * 128))
        return tl

    def phase3(self):
        S = self.S
        tiles = self.tiles3()
        if self.dbg_tiles is not None:
            tiles = [dict(subs=tiles[0]["subs"][0:2], ntok=256, oc=0),
                     dict(subs=[dict(kind="d", np=32, col=0, slot=None, oc=DEC0)], ntok=32, oc=DEC0)]
        plan = []
        for t in tiles:
            plan.append(("bp", self.w_bf["bp"].rearrange("(g p) n -> p g n", p=128), (128, 4, 1024),
                         (0, 512, 0, 1024)))
            plan.append(("sb", self.w_bf["sb"].rearrange("(g p) n -> p g n", p=128), (128, 4, 1024),
                         (0, 512, 0, 1024)))
            for half in range(2):
                plan.append(self.wsrc_cols("in", 2048 + half * 512))
                plan.append(self.wsrc_cols("in", 3072 + half * 512))
            for hh in range(2):
                plan.append(self.wsrc_cols("o", hh * 512))
            for jg in range(8):
                plan.append(self.wsrc_cols("g2", jg * 512))
                plan.append(self.wsrc_cols("u2", jg * 512))
            for hh in range(2):
                for fg in range(4):
                    plan.append(self.wsrc_rows("d2", fg * 1024, 1024, hh * 512, 512))
        with ExitStack() as st:
            self.common_alloc(st, (2, 3), with_cf=False)
            self.wstream_init(st, plan)
            aTr = Ring([self.sb(st, f"aT{i}", [128, 4, 512], BF16) for i in range(2)])
            oTr = Ring([self.sb(st, f"oT{i}", [128, 4, 512], BF16) for i in range(2)])
            mg = self.sb(st, "mg", [128, 8, 512], BF16)
            mg_b = [Buf() for _ in range(8)]
            m1r = Ring([self.sb(st, f"m1_{i}", [128, 512], F32) for i in range(2)])
            m2r = Ring([self.sb(st, f"m2_{i}", [128, 512], F32) for i in range(2)])
            ytr = Ring([self.sb(st, f"yt{i}", [128, D], F32) for i in range(2)])
            def p3_loads(t, hT_t, hT_bufs):
                subs, ntok, oc0 = t["subs"], t["ntok"], t["oc"]
                xt, _ = self.xt.next()
                xb = self.xt_sb[self.xt.i]
                for si, sub in enumerate(subs):
                    np_, oc = sub["np"], sub["oc"]
                    self.load(xt[0:np_, si, :], self.x1s[oc:oc + np_, :], [xb[si]])
                self.load(hT_t[:, :, 0:ntok], self.hT2s[:, :, oc0:oc0 + ntok], hT_bufs)
                aT, aT_b = aTr.next()
                oT, oT_b = oTr.next()
                self.load(aT[:, :, 0:ntok], self.aTs[:, :, oc0:oc0 + ntok], [aT_b])
                self.load(oT[:, :, 0:ntok], self.OTs[:, :, oc0:oc0 + ntok].rearrange("j p n -> p j n"), [oT_b])
                return xt, xb, aT, aT_b, oT, oT_b

            cur = p3_loads(tiles[0], self.hT, self.hT_b)
            for ti, t in enumerate(tiles):
                subs, ntok, oc0 = t["subs"], t["ntok"], t["oc"]
                xt, xb, aT, aT_b, oT, oT_b = cur
                nxt_box = [None]

                def mid(ti=ti, nxt_box=nxt_box):
                    if ti + 1 < len(tiles):
                        o = self.hT_i ^ 1
                        nxt_box[0] = p3_loads(tiles[ti + 1], self.hT_ring[o], self.hT_bufs[o])
                BP, BPb = self.wget("bp")
                SBw, SBb = self.wget("sb")
                for half in range(2):
                    GA, GAb = self.wget("in")
                    GB, GBb = self.wget("in")
                    for cc in range(4):
                        c = half * 4 + cc
                        cs = slice(cc * 128, (cc + 1) * 128)
                        cg = slice(c * 128, (c + 1) * 128)
                        pga, pgab = self.pmm.next()
                        for kc in range(8):
                            self.mm(pga[:, 0:ntok], GA[:, kc, cs], self.hT[:, kc, 0:ntok], kc == 0, kc == 7,
                                    [GAb] + self.hT_b, [pgab])
                        pbp, pbpb = self.pmm.next()
                        for g in range(4):
                            self.mm(pbp[:, 0:ntok], BP[:, g, cg], aT[:, g, 0:ntok], g == 0, g == 3,
                                    [BPb, aT_b], [pbpb])
                        sa, sab = self.sgate.next()
                        self.act(sa[:, 0:ntok], pga[:, 0:ntok], AF.Sigmoid, [pgab], [sab])
                        m1, m1b = m1r.next()
                        self.tt("dve", m1[:, 0:ntok], sa[:, 0:ntok], pbp[:, 0:ntok], ALU.mult, [sab, pbpb], [m1b])
                        pgb, pgbb = self.pmm.next()
                        for kc in range(8):
                            self.mm(pgb[:, 0:ntok], GB[:, kc, cs], self.hT[:, kc, 0:ntok], kc == 0, kc == 7,
                                    [GBb] + self.hT_b, [pgbb])
                        psb, psbb = self.pmm.next()
                        for g in range(4):
                            self.mm(psb[:, 0:ntok], SBw[:, g, cg], oT[:, g, 0:ntok], g == 0, g == 3,
                                    [SBb, oT_b], [psbb])
                        sb2, sb2b = self.sgate.next()
                        self.act(sb2[:, 0:ntok], pgb[:, 0:ntok], AF.Sigmoid, [pgbb], [sb2b])
                        m2, m2b = m2r.next()
                        self.tt("dve", m2[:, 0:ntok], sb2[:, 0:ntok], psb[:, 0:ntok], ALU.mult, [sb2b, psbb], [m2b])
                        self.tt("pool", mg[:, c, 0:ntok], m1[:, 0:ntok], m2[:, 0:ntok], ALU.add, [m1b, m2b], [mg_b[c]])
                self.wrel()
                WO = [self.wget("o") for _ in range(2)]
                for si, sub in enumerate(subs):
                    np_, col = sub["np"], sub["col"]
                    for hh in range(2):
                        Wv, Wb = WO[hh]
                        pa, pab = self.pacc.next()
                        for c in range(8):
                            self.mm(pa[0:np_, :], mg[:, c, col:col + np_], Wv[:, c, :], c == 0, c == 7,
                                    [mg_b[c], Wb], [pab])
                        xs_ = xt[0:np_, si, hh * 512:(hh + 1) * 512]
                        self.tt("dve", xs_, pa[0:np_, :], xs_, ALU.add, [pab, xb[si]], [xb[si]])
                self.wrel()
                for si, sub in enumerate(subs):
                    self.rmsnorm_hT(xt[0:sub["np"], si, :], sub["np"], sub["col"], 0, xb[si], self.hT_b[si])
                self.ffn("g2", "u2", "d2", subs, ntok, xt, xb, mid=mid)
                cur = nxt_box[0]
                self.hT_swap()
                for si, sub in enumerate(subs):
                    np_, oc = sub["np"], sub["oc"]
                    xap = xt[0:np_, si, :]
                    k = self.stat_i % 8
                    self.stat_i += 1
                    ss = self.stat[0:np_, 2 * k:2 * k + 1]
                    rs = self.stat[0:np_, 2 * k + 1:2 * k + 2]
                    sb_ = self.stat_b[k]
                    self.act(self.sq[0:np_, :], xap, AF.Square, [xb[si]], [sb_], accum_out=ss)
                    self.act(ss, ss, AF.Sqrt, [sb_], [sb_], scale=1.0 / D, bias=1e-6)
                    S.op("dve", lambda e, rs=rs, ss=ss: e.reciprocal(out=rs, in_=ss), [sb_], [sb_])
                    yt, ytb = ytr.next()
                    self.stt("dve", yt[0:np_, :], xap, rs, self.gb[0:np_, 1, :], ALU.mult, ALU.mult,
                             [xb[si], sb_, self.gb_b], [ytb])
                    self.store(self.y_own[oc:oc + np_, :], yt[0:np_, :], [ytb])

def _consts():
    ident = np.eye(128, dtype=np.float32)
    J = ident[::-1].copy()
    q = np.arange(128)[:, None]
    c = np.arange(128)[None, :]
    M128 = np.where(c <= 127 - q, MASKV, 0.0).astype(np.float32)
    M32 = np.zeros((128, 32), np.float32)
    M32[0:32] = np.where(c[:, 0:32] <= 31 - q[0:32], MASKV, 0.0)
    c_bf = np.concatenate([ident, J, M128, M32], axis=1).astype(ml_dtypes.bfloat16)
    s = np.arange(128)[:, None].astype(np.float64)
    t = np.arange(128)[None, :].astype(np.float64)
    bc, bp, bs = [], [], []
    for w in WINS:
        bc.append(np.where((s > t - w) & (s <= t), 1.0 / w, 0.0) - (s == t))
        bp.append(np.where(s > t - w + 128, 1.0 / w, 0.0))
        cnt = np.minimum(w, t + 1)
        bs.append(np.where((s > t - w) & (s <= t), 1.0 / cnt, 0.0) - (s == t))
    bdc = np.zeros((128, 128))
    bdp = np.zeros((128, 128))
    s3 = np.arange(32)[:, None]
    t3 = np.arange(32)[None, :]
    r3 = np.arange(15)[:, None]
    for g, w in enumerate(WINS):
        bdc[0:32, g * 32:(g + 1) * 32] = np.where((s3 > t3 - w) & (s3 <= t3), 1.0 / w, 0.0) - (s3 == t3)
        bdp[0:15, g * 32:(g + 1) * 32] = np.where(r3 >= 16 + t3 - w, 1.0 / w, 0.0)
    c_f32 = np.concatenate(bc + bp + bs + [bdc, bdp], axis=1).astype(np.float32)
    return np.ascontiguousarray(c_bf), np.ascontiguousarray(c_f32)


_NC_CACHE = {}


def _get_nc():
    if "nc" not in _NC_CACHE:
        _NC_CACHE["nc"] = Builder().build()
    return _NC_CACHE["nc"]


def kernel(x_prompt, x_sample, cache_k, cache_v, state_pool,
           ffn1_norm, ffn1_gate, ffn1_up, ffn1_down, mix_norm, w_in, pool_w, pool_scale,
           w_branch_pool, w_branch_sb, w_out, ffn2_norm, ffn2_gate, ffn2_up, ffn2_down, final_norm):
    f = lambda a: np.ascontiguousarray(np.asarray(a, dtype=np.float32))
    x_prompt, x_sample = f(x_prompt), f(x_sample)
    cache_k, cache_v, state_pool = f(cache_k), f(cache_v), f(state_pool)
    c_bf, c_f32 = _consts()
    gains = np.stack([f(ffn1_norm)[0], f(mix_norm)[0], f(ffn2_norm)[0], f(final_norm)], axis=0)
    pscale = np.ascontiguousarray(f(pool_scale)[0].reshape(4, 128).T)
    shared = {
        "gains": np.ascontiguousarray(gains), "pscale": pscale, "poolw": f(pool_w)[0],
        "w_g1": f(ffn1_gate)[0], "w_u1": f(ffn1_up)[0], "w_d1": f(ffn1_down)[0], "w_in": f(w_in)[0],
        "w_bp": f(w_branch_pool)[0], "w_sb": f(w_branch_sb)[0], "w_o": f(w_out)[0],
        "w_g2": f(ffn2_gate)[0], "w_u2": f(ffn2_up)[0], "w_d2": f(ffn2_down)[0],
        "c_bf": c_bf, "c_f32": c_f32,
    }
    in_maps = []
    for c in range(8):
        b, par = c // 2, c % 2
        xv = np.zeros((NB * 128, D), np.float32)
        if par == 0:
            xv[0:8192] = x_prompt[b]
        else:
            xv[256:NB * 128] = x_prompt[b][0:NB * 128 - 256]
        m = dict(shared)
        m.update({"xv": xv, "xs": x_sample[c], "ck": cache_k[0, c], "cv": cache_v[0, c],
                  "spool": state_pool[0, c]})
        in_maps.append(m)
    nc = _get_nc()
    res = run_bass_kernel_spmd(nc, in_maps, core_ids=list(range(8)))
    R = res.results
    B, SEQ = 4, 8192
    y_prompt = np.zeros((B, SEQ, D), np.float32)
    y_sample = np.zeros((8, 32, D), np.float32)
    nkp = np.zeros((1, B, 8, SEQ, 64), np.float32)
    nvp = np.zeros((1, B, 8, SEQ, 64), np.float32)
    npp = np.zeros((1, B, 15, 512), np.float32)
    nks = np.zeros((1, 8, 8, 32, 64), np.float32)
    nvs = np.zeros((1, 8, 8, 32, 64), np.float32)
    nps = np.zeros((1, 8, 15, 512), np.float32)
    for c in range(8):
        b, par = c // 2, c % 2
        r = R[c]
        for k, vb in enumerate(OWN):
            rb = vb - 2 * par
            if rb < 0 or rb >= 64:
                continue
            y_prompt[b, rb * 128:(rb + 1) * 128] = r["y_own"][k * 128:(k + 1) * 128]
            nkp[0, b, :, rb * 128:(rb + 1) * 128] = r["nk_own"][:, k * 128:(k + 1) * 128]
            nvp[0, b, :, rb * 128:(rb + 1) * 128] = r["nv_own"][:, k * 128:(k + 1) * 128]
        y_sample[c] = r["y_own"][DEC0:DEC0 + 32]
        nks[0, c] = r["nk_own"][:, DEC0:DEC0 + 32]
        nvs[0, c] = r["nv_own"][:, DEC0:DEC0 + 32]
        nps[0, c] = r["pool_s"]
        if par == 0:
            npp[0, b] = r["pool_p"]
    return (y_prompt, y_sample, nkp, nvp, npp, nks, nvs, nps)
```

```python
import numpy as np
import ml_dtypes
from contextlib import ExitStack
import concourse.bass as bass
import concourse.mybir as mybir
from concourse.bass_utils import run_bass_kernel_spmd

F32 = mybir.dt.float32
BF16 = mybir.dt.bfloat16
AF = mybir.ActivationFunctionType
ALU = mybir.AluOpType

D = 1024
DFF = 4096
NB = 65
OWN = [0] + [x for m in range(16) for x in (4 * m + 3, 4 * m + 4)]
SLOT = {b: i for i, b in enumerate(OWN)}
NS = len(OWN)
NQC = NS * 128 + 32
DEC0 = NS * 128
NPAGE = 8
LOOKAHEAD = NPAGE - 1
MASKV = -30000.0
WINS = (2, 4, 8, 16)


class Buf:
    __slots__ = ("w", "rc", "rd", "name")

    def __init__(self, name=""):
        self.w = None
        self.rc = {}
        self.rd = []
        self.name = name


class DSem:
    def __init__(self, h):
        self.h = h
        self.v = 0
        self.buf = Buf("dsem")


class Op:
    __slots__ = ("eng", "fn", "deps", "needed", "val", "dsem", "phase")


class Sched:
    ENG = ("pe", "act", "dve", "pool", "sp")
    COMPUTE = ("pe", "act", "dve", "pool")

    def __init__(self, nc, csem):
        self.nc = nc
        self.csem = csem
        self.q = {e: [] for e in self.ENG}
        self.cnt = {e: 0 for e in self.COMPUTE}
        self.known = {}
        self.phase = 0
        self.dsems = []

    def dsem(self, h):
        d = DSem(h)
        self.dsems.append(d)
        return d

    def op(self, eng, fn, reads=(), writes=(), dsem=None):
        o = Op()
        o.eng = eng
        o.fn = fn
        o.needed = False
        o.val = None
        o.dsem = dsem
        o.phase = self.phase
        deps = []
        if dsem is not None:
            writes = list(writes) + [dsem.buf]
        for b in reads:
            if b.w is not None:
                deps.append(b.w)
        for b in writes:
            if b.w is not None:
                deps.append(b.w)
            deps.extend(b.rc.values())
            deps.extend(b.rd)
        ph = self.phase
        deps = [d for d in deps if d[-1] == ph]
        o.deps = deps
        for d in deps:
            if d[0] == "c":
                d[1].needed = True
        if dsem is not None:
            dsem.v += 16
            tok = ("d", dsem, dsem.v, ph)
        else:
            tok = ("c", o, ph)
        for b in reads:
            if tok[0] == "c":
                b.rc[eng] = tok
            else:
                b.rd.append(tok)
        for b in writes:
            b.w = tok
            b.rc = {}
            b.rd = []
        self.q[eng].append(o)
        return tok

    def _emit(self, e, eng):
        known = self.known
        for o in self.q[eng]:
            need = {}
            for d in o.deps:
                if d[0] == "c":
                    src = d[1]
                    if src.eng == eng and eng == "pe":
                        continue
                    key = ("c", src.eng)
                    val = src.val
                    sem = self.csem[src.eng]
                else:
                    key = ("d", id(d[1]))
                    val = d[2]
                    sem = d[1].h
                if known.get((eng, key), 0) >= val:
                    continue
                if key not in need or need[key][1] < val:
                    need[key] = (sem, val)
            for key, (sem, val) in need.items():
                e.wait_ge(sem, val)
                known[(eng, key)] = val
            ins = o.fn(e)
            if o.dsem is not None:
                ins.then_inc(o.dsem.h, 16)
            elif o.needed:
                ins.then_inc(self.csem[eng], 1)

    def flush(self):
        nc = self.nc
        for eng in self.COMPUTE:
            for o in self.q[eng]:
                if o.needed:
                    self.cnt[eng] += 1
                    o.val = self.cnt[eng]
        with nc.Block() as block:
            @block.tensor
            def _(e):
                self._emit(e, "pe")

            @block.scalar
            def _(e):
                self._emit(e, "act")

            @block.vector
            def _(e):
                self._emit(e, "dve")

            @block.gpsimd
            def _(e):
                self._emit(e, "pool")

            @block.sync
            def _(e):
                self._emit(e, "sp")
                for d in self.dsems:
                    if d.v > 0 and self.known.get(("sp", ("d", id(d))), 0) < d.v:
                        e.wait_ge(d.h, d.v)
                        self.known[("sp", ("d", id(d)))] = d.v
        self.q = {e: [] for e in self.ENG}
        self.phase += 1


class Ring:
    def __init__(self, tiles):
        self.t = tiles
        self.b = [Buf() for _ in tiles]
        self.i = -1

    def next(self):
        self.i = (self.i + 1) % len(self.t)
        return self.t[self.i], self.b[self.i]

    def cur(self):
        return self.t[self.i], self.b[self.i]


class Builder:
    def __init__(self, dbg_tiles=None, phases=(1, 2, 3)):
        self.nc = bass.Bass("TRN2", target_bir_lowering=False)
        self.es = ExitStack()
        self.dbg_tiles = dbg_tiles
        self.phases = phases
        self.skip = set()
        self.dbg_stage = 99
        self.dbg_slots = [0, 1]
        self.dbg_tiles3 = [0, 8]

    def dram(self, name, shape, dt, kind):
        return self.nc.dram_tensor(name, list(shape), dt, kind=kind).ap()

    def sb(self, st, name, shape, dt):
        return st.enter_context(self.nc.sbuf_tensor(f"{name}_p{self.S.phase}", list(shape), dt))

    def ps(self, st, name, shape, dt):
        return st.enter_context(self.nc.psum_tensor(f"{name}_p{self.S.phase}", list(shape), dt))

    def sem(self, name):
        return self.es.enter_context(self.nc.semaphore(name))

    def build(self):
        nc = self.nc
        A = self.dram
        self.xv = A("xv", [NB * 128, D], F32, "ExternalInput")
        self.xs = A("xs", [32, D], F32, "ExternalInput")
        self.ck = A("ck", [8, 1024, 64], F32, "ExternalInput")
        self.cv = A("cv", [8, 1024, 64], F32, "ExternalInput")
        self.spool = A("spool", [15, 512], F32, "ExternalInput")
        self.gains = A("gains", [4, D], F32, "ExternalInput")
        self.pscale = A("pscale", [128, 4], F32, "ExternalInput")
        self.poolw = A("poolw", [4, 128, 128], F32, "ExternalInput")
        self.w_f32 = {
            "g1": A("w_g1", [D, DFF], F32, "ExternalInput"),
            "u1": A("w_u1", [D, DFF], F32, "ExternalInput"),
            "d1": A("w_d1", [DFF, D], F32, "ExternalInput"),
            "in": A("w_in", [D, DFF], F32, "ExternalInput"),
            "bp": A("w_bp", [512, D], F32, "ExternalInput"),
            "sb": A("w_sb", [512, D], F32, "ExternalInput"),
            "o": A("w_o", [D, D], F32, "ExternalInput"),
            "g2": A("w_g2", [D, DFF], F32, "ExternalInput"),
            "u2": A("w_u2", [D, DFF], F32, "ExternalInput"),
            "d2": A("w_d2", [DFF, D], F32, "ExternalInput"),
        }
        self.c_bf = A("c_bf", [128, 128 * 3 + 32], BF16, "ExternalInput")
        self.c_f32 = A("c_f32", [128, 3 * 512 + 2 * 128], F32, "ExternalInput")
        self.y_own = A("y_own", [NQC, D], F32, "ExternalOutput")
        self.nk_own = A("nk_own", [8, NQC, 64], F32, "ExternalOutput")
        self.nv_own = A("nv_own", [8, NQC, 64], F32, "ExternalOutput")
        self.pool_p = A("pool_p", [15, 512], F32, "ExternalOutput")
        self.pool_s = A("pool_s", [15, 512], F32, "ExternalOutput")
        self.w_bf = {k: A("s_" + k, v.shape, BF16, "Internal") for k, v in self.w_f32.items()}
        self.w_cbuf = {k: [] for k in self.w_f32}
        self.x1s = A("s_x1", [NQC, D], F32, "Internal")
        self.hT2s = A("s_hT2", [128, 8, NQC], BF16, "Internal")
        self.QTs = A("s_QT", [4, 128, NQC], BF16, "Internal")
        self.KTs = A("s_KT", [4, 128, NB * 128], BF16, "Internal")
        self.VRs = A("s_VR", [4, 128, NB, 128], BF16, "Internal")
        self.KTDs = A("s_KTD", [4, 128, 32], BF16, "Internal")
        self.VRDs = A("s_VRD", [32, 512], BF16, "Internal")
        self.aTs = A("s_aT", [128, 4, NQC], BF16, "Internal")
        self.OTs = A("s_OT", [4, 128, NQC], BF16, "Internal")

        csem = {e: self.sem("c_" + e) for e in Sched.COMPUTE}
        self.S = Sched(nc, csem)
        self.page_sems = [self.S.dsem(self.sem(f"pg{i}")) for i in range(NPAGE)]
        self.ld_sems = [self.S.dsem(self.sem(f"ld{i}")) for i in range(16)]
        self.st_sems = [self.S.dsem(self.sem(f"st{i}")) for i in range(8)]
        self.cast_sems2 = [self.S.dsem(self.sem(f"cs{i}")) for i in range(2)]
        self._cast_i = 0
        self._ld_i = 0
        self._st_i = 0

        if 1 in self.phases:
            self.phase1()
            self.S.flush()
        if 2 in self.phases:
            self.phase2()
            self.S.flush()
        if 3 in self.phases:
            self.phase3()
            self.S.flush()
        self.es.close()
        return nc

    def load(self, out, in_, wbufs, rbufs=(), eng="sp"):
        d = self.ld_sems[self._ld_i % len(self.ld_sems)]
        self._ld_i += 1
        return self.S.op(eng, lambda e: e.dma_start(out=out, in_=in_), reads=rbufs, writes=wbufs, dsem=d)

    def store(self, out, in_, rbufs, wbufs=(), eng="sp"):
        d = self.st_sems[self._st_i % len(self.st_sems)]
        self._st_i += 1
        return self.S.op(eng, lambda e: e.dma_start(out=out, in_=in_), reads=rbufs, writes=wbufs, dsem=d)

    def mm(self, out, lhsT, rhs, start, stop, reads, writes):
        self.S.op("pe", lambda e: e.matmul(out, lhsT=lhsT, rhs=rhs, start=start, stop=stop), reads, writes)

    def tr(self, out, in_, ident, reads, writes):
        self.S.op("pe", lambda e: e.transpose(out, in_, ident), reads, writes)

    def act(self, out, in_, func, reads, writes, scale=1.0, bias=0.0, accum_out=None):
        if accum_out is None:
            self.S.op("act", lambda e: e.activation(out=out, in_=in_, func=func, scale=scale, bias=bias),
                      reads, writes)
        else:
            self.S.op("act", lambda e: e.activation(out=out, in_=in_, func=func, scale=scale, bias=bias,
                                                    accum_out=accum_out), reads, writes)

    def copy(self, eng, out, in_, reads, writes):
        if eng == "act":
            self.S.op("act", lambda e: e.activation(out=out, in_=in_, func=AF.Copy), reads, writes)
        else:
            self.S.op(eng, lambda e: e.tensor_copy(out=out, in_=in_), reads, writes)

    def tt(self, eng, out, in0, in1, op, reads, writes):
        self.S.op(eng, lambda e: e.tensor_tensor(out=out, in0=in0, in1=in1, op=op), reads, writes)

    def stt(self, eng, out, in0, scalar, in1, op0, op1, reads, writes):
        self.S.op(eng, lambda e: e.scalar_tensor_tensor(out=out, in0=in0, scalar=scalar, in1=in1,
                                                        op0=op0, op1=op1), reads, writes)

    def wstream_init(self, st, plan):
        self.pages = [self.sb(st, f"page{i}", [128, 4096], BF16) for i in range(NPAGE)]
        self.page_bufs = [Buf(f"page{i}") for i in range(NPAGE)]
        self.wplan = plan
        self.w_issued = 0
        self.w_used = 0
        self.w_rel = 0

    def _issue(self, upto):
        while self.w_issued < min(upto, len(self.wplan), self.w_rel + NPAGE):
            i = self.w_issued
            key, src, shp, rng = self.wplan[i]
            rbufs = [b_ for (r0, r1, c0, c1, b_) in self.w_cbuf[key]
                     if r0 < rng[1] and rng[0] < r1 and c0 < rng[3] and rng[2] < c1]
            pg = i % NPAGE
            dst = self.pages[pg][:, 0:shp[1] * shp[2]].rearrange("p (a b) -> p a b", a=shp[1])
            if shp[0] < 128:
                dst = self.pages[pg][0:shp[0], 0:shp[1] * shp[2]].rearrange("p (a b) -> p a b", a=shp[1])
            self.S.op("sp", lambda e, dst=dst, src=src: e.dma_start(out=dst, in_=src),
                      reads=rbufs, writes=[self.page_bufs[pg]], dsem=self.page_sems[pg])
            self.w_issued += 1

    def wget(self, key):
        i = self.w_used
        k, src, shp, _ = self.wplan[i]
        assert k == key, (k, key, i)
        self._issue(i + 1 + LOOKAHEAD)
        self.w_used += 1
        pg = i % NPAGE
        v = self.pages[pg][0:shp[0], 0:shp[1] * shp[2]].rearrange("p (a b) -> p a b", a=shp[1])
        return v, self.page_bufs[pg]

    def wrel(self):
        self.w_rel = self.w_used

    def wsrc_cols(self, key, c0, ncol=512):
        w = self.w_bf[key]
        return (key, w[:, c0:c0 + ncol].rearrange("(kc p) n -> p kc n", p=128), (128, 8, ncol),
                (0, 1024, c0, c0 + ncol))

    def wsrc_rows(self, key, r0, nrow, c0, ncol):
        w = self.w_bf[key]
        return (key, w[r0:r0 + nrow, c0:c0 + ncol].rearrange("(kc p) n -> p kc n", p=128),
                (128, nrow // 128, ncol), (r0, r0 + nrow, c0, c0 + ncol))

    def cast_weights(self, keys):
        if "cast" in self.skip:
            return
        work = []
        for k in keys:
            if ("cast_" + k) in self.skip:
                continue
            src = self.w_f32[k]
            dst = self.w_bf[k]
            rows, cols = src.shape
            if rows == 1024 and cols == 4096:
                chunks = [(slice(0, rows), slice(c, c + 512)) for c in range(0, cols, 512)]
            else:
                step = max(1, min(256, (1 << 20) // cols))
                chunks = [(slice(r, r + step), slice(0, cols)) for r in range(0, rows, step)]
            item = []
            for rs, cs in chunks:
                b_ = Buf()
                self.w_cbuf[k].append((rs.start, rs.stop, cs.start, cs.stop, b_))
                item.append((b_, dst[rs, cs], src[rs, cs]))
            work.append(item)
        order = []
        if len(work) >= 2 and len(work[0]) == len(work[1]):
            for a, b in zip(work[0], work[1]):
                order += [a, b]
            work = work[2:]
        for w_ in work:
            order += w_
        for b_, o, s_ in order:
            d = self.cast_sems2[self._cast_i % 2]
            self._cast_i += 1
            self.S.op("pool", lambda e, o=o, s_=s_: e.dma_start(out=o, in_=s_), reads=[],
                      writes=[b_], dsem=d)

    def common_alloc(self, st, gidx, with_cf=True):
        nc = self.nc
        self.cbf = self.sb(st, "cbf", [128, 128 * 3 + 32], BF16)
        self.cbf_b = Buf("cbf")
        if with_cf:
            self.cf = self.sb(st, "cf", [128, 3 * 512 + 256], F32)
        self.cf_b = Buf("cf")
        self.gb = self.sb(st, "gb", [128, 2, D], F32)
        self.gb_b = Buf("gb")
        self.load(self.cbf[:], self.c_bf, [self.cbf_b])
        if with_cf:
            self.load(self.cf[:], self.c_f32, [self.cf_b])
        for i, gi in enumerate(gidx):
            if "gb" in self.skip:
                break
            self.load(self.gb[:, i, :], self.gains[gi:gi + 1, :].partition_broadcast(128), [self.gb_b])
        self.ident = self.cbf[:, 0:128]
        self.J = self.cbf[:, 128:256]
        self.xt = Ring([self.sb(st, f"xt{i}", [128, 4, D], F32) for i in range(2)])
        self.xt_sb = [[Buf() for _ in range(4)] for _ in range(2)]
        self.hb = Ring([self.sb(st, f"hb{i}", [128, D], BF16) for i in range(4)])
        self.sq = self.sb(st, "sq", [128, D], BF16)
        self.stat = self.sb(st, "stat", [128, 16], F32)
        self.stat_b = [Buf() for _ in range(8)]
        self.stat_i = 0
        self.hT_ring = [self.sb(st, f"hT{i}", [128, 8, 512], BF16) for i in range(2)]
        self.hT_bufs = [[Buf() for _ in range(4)] for _ in range(2)]
        self.hT_i = 0
        self.hT = self.hT_ring[0]
        self.hT_b = self.hT_bufs[0]
        self.actT = self.sb(st, "actT", [128, 32, 512], BF16)
        self.actT_b = [Buf() for _ in range(32)]
        self.sgate = Ring([self.sb(st, f"sgate{i}", [128, 512], F32) for i in range(2)])
        self.pmm = Ring([self.ps(st, f"pmm{i}", [128, 512], F32) for i in range(4)])
        self.pacc = Ring([self.ps(st, f"pacc{i}", [128, 512], F32) for i in range(2)])
        self.pmisc = Ring([self.ps(st, f"pmisc{i}", [128, 512], F32) for i in range(2)])
        self.pmisc_bf = [t[:].bitcast(BF16) for t in self.pmisc.t] if False else None

    def rmsnorm_hT(self, xap, np_, col, gi, xbuf, hTbuf, defer=False):
        si = self.stat_i % 8
        self.stat_i += 1
        ss = self.stat[0:np_, 2 * si:2 * si + 1]
        rs = self.stat[0:np_, 2 * si + 1:2 * si + 2]
        sb_ = self.stat_b[si]
        self.act(self.sq[0:np_, :], xap, AF.Square, [xbuf], [sb_], accum_out=ss)
        self.act(ss, ss, AF.Sqrt, [sb_], [sb_], scale=1.0 / D, bias=1e-6)
        self.S.op("dve", lambda e: e.reciprocal(out=rs, in_=ss), [sb_], [sb_])
        hb, hbb = self.hb.next()
        self.stt("dve", hb[0:np_, :], xap, rs, self.gb[0:np_, gi, :], ALU.mult, ALU.mult,
                 [xbuf, sb_, self.gb_b], [hbb])
        if defer:
            return (hb, hbb, np_, col, hTbuf)
        self.rmsnorm_part2((hb, hbb, np_, col, hTbuf))

    def rmsnorm_part2(self, h):
        hb, hbb, np_, col, hTbuf = h
        pm, pmb = self.pmisc.next()
        pv = pm[:].bitcast(BF16)
        for c in range(8):
            self.tr(pv[:, c * np_:(c + 1) * np_], hb[0:np_, c * 128:(c + 1) * 128], self.ident[0:np_, 0:np_],
                    [hbb, self.cbf_b], [pmb])
        self.copy("dve", self.hT[:, :, col:col + np_],
                  pv[:, 0:8 * np_].rearrange("p (c n) -> p c n", c=8), [pmb], [hTbuf])

    def hT_swap(self):
        self.hT_i ^= 1
        self.hT = self.hT_ring[self.hT_i]
        self.hT_b = self.hT_bufs[self.hT_i]

    def ffn(self, kg, ku, kd, subs, ntok, xt, xbufs, mid=None):
        hTr = self.hT_b
        for jg in range(8):
            G, Gb = self.wget(kg)
            U, Ub = self.wget(ku)
            for jj in range(4):
                j = jg * 4 + jj
                pg, pgb = self.pmm.next()
                for kc in range(8):
                    self.mm(pg[:, 0:ntok], G[:, kc, jj * 128:(jj + 1) * 128], self.hT[:, kc, 0:ntok],
                            kc == 0, kc == 7, [Gb] + hTr, [pgb])
                pu, pub = self.pmm.next()
                for kc in range(8):
                    self.mm(pu[:, 0:ntok], U[:, kc, jj * 128:(jj + 1) * 128], self.hT[:, kc, 0:ntok],
                            kc == 0, kc == 7, [Ub] + hTr, [pub])
                sg, sgb = self.sgate.next()
                self.act(sg[:, 0:ntok], pg[:, 0:ntok], AF.Silu, [pgb], [sgb])
                self.tt("dve", self.actT[:, j, 0:ntok], sg[:, 0:ntok], pu[:, 0:ntok], ALU.mult,
                        [sgb, pub], [self.actT_b[j]])
            self.wrel()
        if mid is not None:
            mid()
        for hh in range(2):
            Dp = [self.wget(kd) for _ in range(4)]
            for si, sub in enumerate(subs):
                np_, col = sub["np"], sub["col"]
                pa, pab = self.pacc.next()
                for fg in range(4):
                    Dv, Db = Dp[fg]
                    for fc in range(8):
                        f = fg * 8 + fc
                        self.mm(pa[0:np_, :], self.actT[:, f, col:col + np_], Dv[:, fc, :],
                                f == 0, f == 31, [self.actT_b[f], Db], [pab])
                xs_ = xt[0:np_, si, hh * 512:(hh + 1) * 512]
                self.stt("dve", xs_, pa[0:np_, :], 0.5, xs_, ALU.mult, ALU.add, [pab, xbufs[si]], [xbufs[si]])
            self.wrel()

    def tiles1(self):
        tl = []
        for ti in range(16):
            subs = []
            for s in range(4):
                b = 4 * ti + s
                subs.append(dict(kind="p", blk=b, np=128, col=s * 128, own=b in SLOT, slot=SLOT.get(b)))
            tl.append(dict(subs=subs, ntok=512))
        subs = [dict(kind="p", blk=64, np=128, col=0, own=True, slot=SLOT[64]),
                dict(kind="d", blk=None, np=32, col=128, own=True, slot=None)]
        tl.append(dict(subs=subs, ntok=160))
        return tl

    def phase1(self):
        nc = self.nc
        S = self.S
        tiles = self.tiles1()
        if self.dbg_tiles is not None:
            tiles = [tiles[i] for i in self.dbg_tiles]
        plan = []
        if self.dbg_stage < 4:
            tiles = [dict(t, noproj=True) for t in tiles]
        for t in tiles:
            for jg in range(8):
                plan.append(self.wsrc_cols("g1", jg * 512))
                plan.append(self.wsrc_cols("u1", jg * 512))
            for hh in range(2):
                for fg in range(4):
                    plan.append(self.wsrc_rows("d1", fg * 1024, 1024, hh * 512, 512))
            if self.dbg_stage < 2:
                plan = []
            if t.get("noproj"):
                continue
            for c0 in (0, 1024, 1536, 512):
                plan.append(self.wsrc_cols("in", c0))
        with ExitStack() as st:
            self.cast_weights(["g1", "u1", "d1", "in"])
            self.common_alloc(st, (0, 1))
            self.wstream_init(st, plan)
            uring = Ring([self.sb(st, f"u{i}", [128, 512], F32) for i in range(3)])
            ktm = Ring([self.sb(st, f"ktm{i}", [128, 512], F32) for i in range(1)])
            vtm = Ring([self.sb(st, f"vtm{i}", [128, 512], F32) for i in range(1)])
            kbf = Ring([self.sb(st, f"kbf{i}", [128, 512], BF16) for i in range(2)])
            vbf = Ring([self.sb(st, f"vbf{i}", [128, 512], BF16) for i in range(2)])
            ktr = Ring([self.sb(st, f"ktr{i}", [128, 4, 128], BF16) for i in range(2)])
            vrv = Ring([self.sb(st, f"vrv{i}", [128, 512], BF16) for i in range(2)])
            qts = Ring([self.sb(st, f"qts{i}", [128, 4, 128], BF16) for i in range(2)])
            dts = Ring([self.sb(st, f"dts{i}", [128, 4, 128], BF16) for i in range(2)])
            ats = Ring([self.sb(st, f"ats{i}", [128, 4, 128], BF16) for i in range(2)])
            pwf = self.sb(st, "pwf", [128, 4, 128], F32)
            pwb = self.sb(st, "pwb", [128, 4, 128], BF16)
            pw_b = Buf()
            psc = self.sb(st, "psc", [128, 4], F32)
            psc_b = Buf()
            spl = self.sb(st, "spl", [15, 512], F32)
            spl_b = Buf()
            self.load(pwf[:], self.poolw.rearrange("g c e -> c g e"), [pw_b])
            self.copy("dve", pwb[:], pwf[:], [pw_b], [pw_b])
            self.load(psc[:], self.pscale, [psc_b])
            self.load(spl[:], self.spool, [spl_b])
            cf = self.cf
            bc = lambda g: cf[:, g * 128:(g + 1) * 128]
            bp = lambda g: cf[:, 512 + g * 128:512 + (g + 1) * 128]
            bs = lambda g: cf[:, 1024 + g * 128:1024 + (g + 1) * 128]
            bdc = lambda g: cf[0:32, 1536 + g * 32:1536 + (g + 1) * 32]
            bdp = lambda g: cf[0:15, 1664 + g * 32:1664 + (g + 1) * 32]

            u_prev = None

            def load_x(t):
                xt, _ = self.xt.next()
                xb = self.xt_sb[self.xt.i]
                for si, sub in enumerate(t["subs"]):
                    if sub["kind"] == "p":
                        b = sub["blk"]
                        self.load(xt[:, si, :], self.xv[b * 128:(b + 1) * 128, :], [xb[si]])
                    else:
                        self.load(xt[0:32, si, :], self.xs, [xb[si]])
                return xt, xb

            def norm1_part1(t, xt, xb, hTb):
                return [self.rmsnorm_hT(xt[0:sub["np"], si, :], sub["np"], sub["col"], 0, xb[si], hTb[si], defer=True)
                        for si, sub in enumerate(t["subs"])]

            cur = load_x(tiles[0])
            for h in norm1_part1(tiles[0], cur[0], cur[1], self.hT_b):
                self.rmsnorm_part2(h)
            for ti, t in enumerate(tiles):
                subs, ntok = t["subs"], t["ntok"]
                xt, xb = cur
                nxt = None
                pend = []
                if ti + 1 < len(tiles):
                    nxt = load_x(tiles[ti + 1])
                    pend = norm1_part1(tiles[ti + 1], nxt[0], nxt[1], self.hT_bufs[self.hT_i ^ 1])

                def mid(pend=pend):
                    keep = (self.hT, self.hT_b)
                    self.hT_swap()
                    for h in pend:
                        self.rmsnorm_part2(h)
                    self.hT_swap()
                    assert keep[0] is self.hT

                self.ffn("g1", "u1", "d1", subs, ntok, xt, xb, mid=mid)
                cur = nxt
                if self.dbg_stage < 3:
                    continue
                for si, sub in enumerate(subs):
                    np_ = sub["np"]
                    if sub["own"]:
                        oc = sub["slot"] * 128 if sub["kind"] == "p" else DEC0
                        self.store(self.x1s[oc:oc + np_, :], xt[0:np_, si, :], [xb[si]])
                    self.rmsnorm_hT(xt[0:np_, si, :], np_, sub["col"], 1, xb[si], self.hT_b[si])
                    if sub["own"]:
                        self.store(self.hT2s[:, :, oc:oc + np_], self.hT[:, :, sub["col"]:sub["col"] + np_],
                                   [self.hT_b[si]])
                if self.dbg_stage < 4:
                    continue
                Uw, Uwb = self.wget("in")
                Kw, Kwb = self.wget("in")
                Vw, Vwb = self.wget("in")
                Qw, Qwb = self.wget("in")
                for si, sub in enumerate(subs):
                    np_, col = sub["np"], sub["col"]
                    dec = sub["kind"] == "d"
                    own = sub["own"]
                    oc = (sub["slot"] * 128 if not dec else DEC0) if own else None
                    hTs = [self.hT_b[si]]
                    Jn = self.J[0:np_, 128 - np_:128]
                    pa, pab = self.pacc.next()
                    for kc in range(8):
                        self.mm(pa[0:np_, :], self.hT[:, kc, col:col + np_], Uw[:, kc, :], kc == 0, kc == 7,
                                hTs + [Uwb], [pab])
                    ut, utb = uring.next()
                    self.copy("act", ut[0:np_, :], pa[0:np_, :], [pab], [utb])
                    pa, pab = self.pacc.next()
                    for kc in range(8):
                        self.mm(pa[0:np_, :], self.hT[:, kc, col:col + np_], Kw[:, kc, :], kc == 0, kc == 7,
                                hTs + [Kwb], [pab])
                    kb_, kbb = kbf.next()
                    if own and "nkst" not in self.skip:
                        km, kmb = ktm.next()
                        self.copy("act", km[0:np_, :], pa[0:np_, :], [pab], [kmb])
                        self.copy("pool", kb_[0:np_, :], km[0:np_, :], [kmb], [kbb])
                        for h in range(8):
                            self.store(self.nk_own[h, oc:oc + np_, :], km[0:np_, h * 64:(h + 1) * 64], [kmb])
                    else:
                        self.copy("act", kb_[0:np_, :], pa[0:np_, :], [pab], [kbb])
                    pm, pmb = self.pmisc.next()
                    for j in range(4):
                        self.mm(pm[:, j * np_:(j + 1) * np_], kb_[0:np_, j * 128:(j + 1) * 128], Jn, True, True,
                                [kbb, self.cbf_b], [pmb])
                    kr, krb = ktr.next()
                    self.copy("dve", kr[:, :, 0:np_], pm[:, 0:4 * np_].rearrange("p (j n) -> p j n", j=4),
                              [pmb], [krb])
                    if "ktst" in self.skip:
                        pass
                    elif not dec:
                        rb = 64 - sub["blk"]
                        self.store(self.KTs[:, :, rb * 128:(rb + 1) * 128].rearrange("j p n -> p j n"),
                                   kr[:], [krb])
                    else:
                        self.store(self.KTDs.rearrange("j p n -> p j n"), kr[:, :, 0:32], [krb])
                    pa, pab = self.pacc.next()
                    for kc in range(8):
                        self.mm(pa[0:np_, :], self.hT[:, kc, col:col + np_], Vw[:, kc, :], kc == 0, kc == 7,
                                hTs + [Vwb], [pab])
                    vb_, vbb = vbf.next()
                    if own and "nkst" not in self.skip:
                        vm, vmb = vtm.next()
                        self.copy("act", vm[0:np_, :], pa[0:np_, :], [pab], [vmb])
                        self.copy("pool", vb_[0:np_, :], vm[0:np_, :], [vmb], [vbb])
                        for h in range(8):
                            self.store(self.nv_own[h, oc:oc + np_, :], vm[0:np_, h * 64:(h + 1) * 64], [vmb])
                    else:
                        self.copy("act", vb_[0:np_, :], pa[0:np_, :], [pab], [vbb])
                    pm, pmb = self.pmisc.next()
                    self.mm(pm[0:np_, :], Jn, vb_[0:np_, :], True, True, [vbb, self.cbf_b], [pmb])
                    vr, vrb = vrv.next()
                    self.copy("dve", vr[0:np_, :], pm[0:np_, :], [pmb], [vrb])
                    if "ktst" in self.skip:
                        pass
                    elif not dec:
                        self.store(self.VRs[:, :, 64 - sub["blk"], :].rearrange("j p n -> p j n"),
                                   vr[:].rearrange("p (j n) -> p j n", j=4), [vrb])
                    else:
                        self.store(self.VRDs, vr[0:32, :], [vrb])
                    if own and "q" not in self.skip:
                        pm, pmb = self.pmisc.next()
                        for j in range(4):
                            for kc in range(8):
                                self.mm(pm[:, j * np_:(j + 1) * np_], Qw[:, kc, j * 128:(j + 1) * 128],
                                        self.hT[:, kc, col:col + np_], kc == 0, kc == 7, hTs + [Qwb], [pmb])
                        qt, qtb = qts.next()
                        self.copy("act", qt[:, :, 0:np_], pm[:, 0:4 * np_].rearrange("p (j n) -> p j n", j=4),
                                  [pmb], [qtb])
                        self.store(self.QTs[:, :, oc:oc + np_].rearrange("j p n -> p j n"), qt[:, :, 0:np_], [qtb])
                    if own and "pool" not in self.skip:
                        pm, pmb = self.pmisc.next()
                        for g in range(4):
                            ug = ut[0:np_, g * 128:(g + 1) * 128]
                            if dec:
                                self.mm(pm[:, g * np_:(g + 1) * np_], ug, bdc(g), True, False,
                                        [utb, self.cf_b], [pmb])
                                self.mm(pm[:, g * np_:(g + 1) * np_], spl[0:15, g * 128:(g + 1) * 128], bdp(g),
                                        False, True, [spl_b, self.cf_b], [pmb])
                            elif sub["blk"] == 0:
                                self.mm(pm[:, g * np_:(g + 1) * np_], ug, bs(g), True, True,
                                        [utb, self.cf_b], [pmb])
                            else:
                                upt, upb = u_prev
                                self.mm(pm[:, g * np_:(g + 1) * np_], ug, bc(g), True, False,
                                        [utb, self.cf_b], [pmb])
                                self.mm(pm[:, g * np_:(g + 1) * np_], upt[:, g * 128:(g + 1) * 128], bp(g),
                                        False, True, [upb, self.cf_b], [pmb])
                        dt_, dtb = dts.next()
                        self.copy("dve", dt_[:, :, 0:np_], pm[:, 0:4 * np_].rearrange("p (j n) -> p j n", j=4),
                                  [pmb], [dtb])
                        pm, pmb = self.pmisc.next()
                        for g in range(4):
                            self.mm(pm[:, g * np_:(g + 1) * np_], pwb[:, g, :], dt_[:, g, 0:np_], True, True,
                                    [pw_b, dtb], [pmb])
                        at, atb = ats.next()
                        for g in range(4):
                            self.S.op("dve", lambda e, o=at[:, g, 0:np_], i=pm[:, g * np_:(g + 1) * np_],
                                      s=psc[:, g:g + 1]: e.tensor_scalar(out=o, in0=i, scalar1=s, scalar2=None,
                                                                         op0=ALU.mult),
                                      [pmb, psc_b], [atb])
                        self.store(self.aTs[:, :, oc:oc + np_], at[:, :, 0:np_], [atb])
                    if not dec:
                        if sub["blk"] == 63:
                            self.store(self.pool_p, ut[113:128, :], [utb])
                        u_prev = (ut, utb)
                    else:
                        self.store(self.pool_s, ut[17:32, :], [utb])
                self.wrel()
                self.hT_swap()

    def phase2(self):
        S = self.S
        with ExitStack() as st:
            self.cast_weights(["g2", "u2", "bp", "sb", "o", "d2"])
            cbf = self.sb(st, "cbf2", [128, 128 * 3 + 32], BF16)
            cbf_b = Buf()
            self.load(cbf[:], self.c_bf, [cbf_b])
            ident = cbf[:, 0:128]
            J = cbf[:, 128:256]
            M128 = cbf[:, 256:384]
            M32 = cbf[0:32, 384:416]
            ones = self.sb(st, "ones", [128, 1024], BF16)
            ones_b = Buf()
            S.op("pool", lambda e: e.memset(ones[:], 1.0), [], [ones_b])
            ktp = Ring([self.sb(st, f"ktp{i}", [128, NB * 128], BF16) for i in range(2)])
            vrp = Ring([self.sb(st, f"vrp{i}", [128, NB, 128], BF16) for i in range(2)])
            qtp = Ring([(self.sb(st, f"qA{i}", [128, NQC], BF16), self.sb(st, f"qB{i}", [128, NQC], BF16))
                        for i in range(2)])
            for (qa_, qb_), qbuf_ in zip(qtp.t, qtp.b):
                S.op("pool", lambda e, t=qa_: e.memset(t[64:128, :], 0.0), [], [qbuf_])
                S.op("pool", lambda e, t=qb_: e.memset(t[0:64, :], 0.0), [], [qbuf_])
            ktd = Ring([self.sb(st, f"ktd{i}", [128, 32 + 1024], BF16) for i in range(2)])
            vrd = Ring([self.sb(st, f"vrd{i}", [128, 9, 128], BF16) for i in range(2)])
            ot = Ring([self.sb(st, f"ot{i}", [128, NQC], BF16) for i in range(2)])
            ckf = Ring([self.sb(st, f"ckf{i}", [128, 8, 2, 64], F32) for i in range(2)])
            ckb = Ring([self.sb(st, f"ckb{i}", [128, 8, 2, 64], BF16) for i in range(2)])
            zr = Ring([self.ps(st, f"z{i}", [128, 1024], F32) for i in range(2)])
            wtp = Ring([self.ps(st, f"wtp{i}", [128, 1024], BF16) for i in range(2)])
            oacc = Ring([self.ps(st, f"oacc{i}", [128, 512], F32) for i in range(2)])
            sg = Ring([self.sb(st, f"sg{i}", [128, 1024], F32) for i in range(3)])
            Pb = Ring([self.sb(st, f"Pb{i}", [128, 1025], F32) for i in range(3)])
            wb = Ring([self.sb(st, f"wb{i}", [128, 1024], BF16) for i in range(3)])
            wT = Ring([self.sb(st, f"wT{i}", [128, 1024], BF16) for i in range(3)])
            state = {"prevP": None, "oacc": None}
            Pbc = [Buf() for _ in range(len(Pb.t))]

            def stageA(ch):
                nq, nc_ = ch["nq"], ch["ncols"]
                z, zb = zr.next()
                for off in range(0, nc_, 512):
                    n = min(512, nc_ - off)
                    self.mm(z[0:nq, off:off + n], ch["q"], ch["kT"][:, off:off + n], True,
                            not (ch["first"] and off == 0), ch["rb"], [zb])
                if ch["first"]:
                    M = M128 if nq == 128 else M32
                    self.mm(z[0:nq, 0:nq], ident[0:nq, 0:nq], M[0:nq, 0:nq], False, True, [cbf_b], [zb])
                sgt, sgb = sg.next()
                self.act(sgt[0:nq, 0:nc_], z[0:nq, 0:nc_], AF.Sigmoid, [zb], [sgb], scale=-0.125)
                pt, pb = Pb.next()
                pbc = Pbc[Pb.i]
                if ch["first"]:
                    S.op("dve", lambda e: e.memset(pt[0:nq, 0:1], 1.0), [], [pbc])
                    init = 1.0
                    rd = []
                else:
                    ppt, ppb, pn = state["prevP"]
                    self.copy("dve", pt[0:nq, 0:1], ppt[0:nq, pn:pn + 1], [ppb], [pbc])
                    init = ppt[0:nq, pn:pn + 1]
                    rd = [ppb]
                S.op("dve", lambda e: e.tensor_tensor_scan(out=pt[0:nq, 1:nc_ + 1], data0=sgt[0:nq, 0:nc_],
                                                           data1=ones[0:nq, 0:nc_], initial=init,
                                                           op0=ALU.mult, op1=ALU.mult),
                     [sgb, ones_b] + rd, [pb])
                ch["Pc"] = pbc
                state["prevP"] = (pt, pb, nc_)
                ch["P"] = (pt, pb)

            def stageA2(ch):
                nq, nc_ = ch["nq"], ch["ncols"]
                pt, pb = ch["P"]
                wt, wbb = wb.next()
                self.tt("dve", wt[0:nq, 0:nc_], pt[0:nq, 0:nc_], pt[0:nq, 1:nc_ + 1], ALU.subtract,
                        [pb, ch["Pc"]], [wbb])
                ch["wb"] = (wt, wbb)

            def stageB(ch):
                nq = ch["nq"]
                wt, wbb = ch["wb"]
                tp, tpb = wtp.next()
                blocks = ch["blocks"]
                kb = blocks[0][1]
                off = 0
                for bi, (v_ap, kb_) in enumerate(blocks):
                    self.tr(tp[0:kb_, bi * 128:bi * 128 + nq], wt[0:nq, off:off + kb_], ident[0:nq, 0:nq],
                            [wbb, cbf_b], [tpb])
                    off += kb_
                wTt, wTb = wT.next()
                nb = len(blocks)
                self.copy("act", wTt[0:kb, 0:nb * 128].rearrange("p (b n) -> p b n", b=nb)[:, :, 0:nq],
                          tp[0:kb, 0:nb * 128].rearrange("p (b n) -> p b n", b=nb)[:, :, 0:nq], [tpb], [wTb])
                ch["wT"] = (wTt, wTb)

            def stageC(ch):
                nq = ch["nq"]
                blocks = ch["blocks"]
                nb = len(blocks)
                wTt, wTb = ch["wT"]
                if ch["first"]:
                    state["oacc"] = oacc.next()
                oa, oab = state["oacc"]
                pr0 = ch["pr0"]
                for bi, (v_ap, kb_) in enumerate(blocks):
                    self.mm(oa[:, 0:nq], v_ap, wTt[0:kb_, bi * 128:bi * 128 + nq],
                            ch["first"] and bi == 0, ch["last"] and bi == nb - 1, [wTb] + ch["vb"], [oab])
                if ch["last"]:
                    ott, otb = ch["ot"]
                    oc = ch["ocol"]
                    self.copy("act", ott[pr0:pr0 + 64, oc:oc + nq], oa[pr0:pr0 + 64, 0:nq], [oab], [otb])

            slots = list(range(NS)) if self.dbg_tiles is None else self.dbg_slots
            def prep_loads(j):
                kt, ktb = ktp.next()
                vr, vrb = vrp.next()
                qt, qtb = qtp.next()
                kd, kdb = ktd.next()
                vd, vdb = vrd.next()
                if self.dbg_tiles is None:
                    self.load(kt[:], self.KTs[j], [ktb])
                    self.load(vr[:], self.VRs[j], [vrb])
                    self.load(qt[0][0:64, :], self.QTs[j][0:64, :], [qtb])
                    self.load(qt[1][64:128, :], self.QTs[j][64:128, :], [qtb])
                else:
                    self.load(kt[:, 61 * 128:], self.KTs[j][:, 61 * 128:], [ktb])
                    self.load(vr[:, 61:, :], self.VRs[j][:, 61:, :], [vrb])
                    for hq in range(2):
                        hs = slice(hq * 64, hq * 64 + 64)
                        self.load(qt[hq][hs, 0:256], self.QTs[j][hs, 0:256], [qtb])
                        self.load(qt[hq][hs, DEC0:], self.QTs[j][hs, DEC0:], [qtb])
                self.load(kd[:, 0:32], self.KTDs[j], [kdb])
                self.load(vd[0:32, 0, :], self.VRDs[:, j * 128:(j + 1) * 128], [vdb])
                cfs = []
                for src in (self.ck, self.cv):
                    cf_, cfb = ckf.next()
                    for hh in range(2):
                        self.load(cf_[:, :, hh, :], src[2 * j + hh].rearrange("(b p) d -> p b d", p=128), [cfb])
                    cfs.append((cf_, cfb))
                return dict(kt=kt, ktb=ktb, vr=vr, vrb=vrb, qt=qt, qtb=qtb, kd=kd, kdb=kdb, vd=vd, vdb=vdb, cfs=cfs)

            def prep_compute(P):
                kd, kdb, vd, vdb = P["kd"], P["kdb"], P["vd"], P["vdb"]
                for (cf_, cfb), is_k in zip(P["cfs"], (True, False)):
                    cb_, cbb = ckb.next()
                    self.copy("pool", cb_[:], cf_[:], [cfb], [cbb])
                    z, zb = zr.next()
                    for blk in range(8):
                        rb = 7 - blk
                        cblk = cb_[:, blk, :, :].rearrange("p h d -> p (h d)")
                        if is_k:
                            self.mm(z[:, rb * 128:(rb + 1) * 128], cblk, J, True, True, [cbb, cbf_b], [zb])
                        else:
                            self.mm(z[:, rb * 128:(rb + 1) * 128], J, cblk, True, True, [cbb, cbf_b], [zb])
                    if is_k:
                        self.copy("act", kd[:, 32:1056], z[:, 0:1024], [zb], [kdb])
                    else:
                        self.copy("act", vd[:, 1:9, :], z[:, 0:1024].rearrange("p (b n) -> p b n", b=8), [zb], [vdb])


            nxtP = prep_loads(0)
            prep_compute(nxtP)
            for j in range(4):
                P = nxtP
                kt, ktb, vr, vrb, qt, qtb = P["kt"], P["ktb"], P["vr"], P["vrb"], P["qt"], P["qtb"]
                kd, kdb, vd, vdb = P["kd"], P["kdb"], P["vd"], P["vdb"]
                ott, otb = ot.next()
                if j < 3:
                    nxtP = prep_loads(j + 1)
                for hh in range(2):
                    if hh == 1 and j < 3:
                        prep_compute(nxtP)
                    pr0 = hh * 64
                    chunks = []
                    for k in slots:
                        a = OWN[k]
                        c0 = (64 - a) * 128
                        nblk = a + 1
                        for ci in range(0, nblk, 8):
                            nb = min(8, nblk - ci)
                            chunks.append(dict(
                                nq=128, ncols=nb * 128, first=ci == 0, last=ci + nb == nblk, pr0=pr0,
                                q=qt[hh][:, k * 128:(k + 1) * 128],
                                kT=kt[:, c0 + ci * 128:c0 + (ci + nb) * 128],
                                blocks=[(vr[:, 64 - a + ci + b, :], 128) for b in range(nb)],
                                rb=[qtb, ktb], vb=[vrb], ot=(ott, otb), ocol=k * 128))
                    chunks.append(dict(nq=32, ncols=32, first=True, last=False, pr0=pr0,
                                       q=qt[hh][:, DEC0:DEC0 + 32], kT=kd[:, 0:32],
                                       blocks=[(vd[0:32, 0, :], 32)],
                                       rb=[qtb, kdb], vb=[vdb], ot=(ott, otb), ocol=DEC0))
                    chunks.append(dict(nq=32, ncols=1024, first=False, last=True, pr0=pr0,
                                       q=qt[hh][:, DEC0:DEC0 + 32], kT=kd[:, 32:1056],
                                       blocks=[(vd[:, 1 + b, :], 128) for b in range(8)],
                                       rb=[qtb, kdb], vb=[vdb], ot=(ott, otb), ocol=DEC0))
                    n_ch = len(chunks)
                    for i in range(n_ch + 3):
                        if i < n_ch:
                            stageA(chunks[i])
                        if 0 <= i - 1 < n_ch:
                            stageA2(chunks[i - 1])
                        if 0 <= i - 2 < n_ch:
                            stageB(chunks[i - 2])
                        if 0 <= i - 3 < n_ch:
                            stageC(chunks[i - 3])
                if self.dbg_tiles is None:
                    self.store(self.OTs[j], ott[:], [otb])
                else:
                    self.store(self.OTs[j][:, 0:256], ott[:, 0:256], [otb])
                    self.store(self.OTs[j][:, DEC0:], ott[:, DEC0:], [otb])

    def tiles3(self):
        tl = []
        for t in range(8):
            subs = [dict(kind="p", np=128, col=s * 128, slot=4 * t + s, oc=(4 * t + s) * 128) for s in range(4)]
            tl.append(dict(subs=subs, ntok=512, oc=4 * t * 128))
        subs = [dict(kind="p", np=128, col=0, slot=32, oc=32 * 128),
                dict(kind="d", np=32, col=128, slot=None, oc=DEC0)]
        tl.append(dict(subs=subs, ntok=160, oc=32 * 128))
        return tl

    def phase3(self):
        S = self.S
        tiles = self.tiles3()
        if self.dbg_tiles is not None:
            tiles = [dict(subs=tiles[0]["subs"][0:2], ntok=256, oc=0),
                     dict(subs=[dict(kind="d", np=32, col=0, slot=None, oc=DEC0)], ntok=32, oc=DEC0)]
        plan = []
        for t in tiles:
            plan.append(("bp", self.w_bf["bp"].rearrange("(g p) n -> p g n", p=128), (128, 4, 1024),
                         (0, 512, 0, 1024)))
            plan.append(("sb", self.w_bf["sb"].rearrange("(g p) n -> p g n", p=128), (128, 4, 1024),
                         (0, 512, 0, 1024)))
            for half in range(2):
                plan.append(self.wsrc_cols("in", 2048 + half * 512))
                plan.append(self.wsrc_cols("in", 3072 + half * 512))
            for hh in range(2):
                plan.append(self.wsrc_cols("o", hh * 512))
            for jg in range(8):
                plan.append(self.wsrc_cols("g2", jg * 512))
                plan.append(self.wsrc_cols("u2", jg * 512))
            for hh in range(2):
                for fg in range(4):
                    plan.append(self.wsrc_rows("d2", fg * 1024, 1024, hh * 512, 512))
        with ExitStack() as st:
            self.common_alloc(st, (2, 3), with_cf=False)
            self.wstream_init(st, plan)
            aTr = Ring([self.sb(st, f"aT{i}", [128, 4, 512], BF16) for i in range(2)])
            oTr = Ring([self.sb(st, f"oT{i}", [128, 4, 512], BF16) for i in range(2)])
            mg = self.sb(st, "mg", [128, 8, 512], BF16)
            mg_b = [Buf() for _ in range(8)]
            m1r = Ring([self.sb(st, f"m1_{i}", [128, 512], F32) for i in range(2)])
            m2r = Ring([self.sb(st, f"m2_{i}", [128, 512], F32) for i in range(2)])
            ytr = Ring([self.sb(st, f"yt{i}", [128, D], F32) for i in range(2)])
            def p3_loads(t, hT_t, hT_bufs):
                subs, ntok, oc0 = t["subs"], t["ntok"], t["oc"]
                xt, _ = self.xt.next()
                xb = self.xt_sb[self.xt.i]
                for si, sub in enumerate(subs):
                    np_, oc = sub["np"], sub["oc"]
                    self.load(xt[0:np_, si, :], self.x1s[oc:oc + np_, :], [xb[si]])
                self.load(hT_t[:, :, 0:ntok], self.hT2s[:, :, oc0:oc0 + ntok], hT_bufs)
                aT, aT_b = aTr.next()
                oT, oT_b = oTr.next()
                self.load(aT[:, :, 0:ntok], self.aTs[:, :, oc0:oc0 + ntok], [aT_b])
                self.load(oT[:, :, 0:ntok], self.OTs[:, :, oc0:oc0 + ntok].rearrange("j p n -> p j n"), [oT_b])
                return xt, xb, aT, aT_b, oT, oT_b

            cur = p3_loads(tiles[0], self.hT, self.hT_b)
            for ti, t in enumerate(tiles):
                subs, ntok, oc0 = t["subs"], t["ntok"], t["oc"]
                xt, xb, aT, aT_b, oT, oT_b = cur
                nxt_box = [None]

                def mid(ti=ti, nxt_box=nxt_box):
                    if ti + 1 < len(tiles):
                        o = self.hT_i ^ 1
                        nxt_box[0] = p3_loads(tiles[ti + 1], self.hT_ring[o], self.hT_bufs[o])
                BP, BPb = self.wget("bp")
                SBw, SBb = self.wget("sb")
                for half in range(2):
                    GA, GAb = self.wget("in")
                    GB, GBb = self.wget("in")
                    for cc in range(4):
                        c = half * 4 + cc
                        cs = slice(cc * 128, (cc + 1) * 128)
                        cg = slice(c * 128, (c + 1) * 128)
                        pga, pgab = self.pmm.next()
                        for kc in range(8):
                            self.mm(pga[:, 0:ntok], GA[:, kc, cs], self.hT[:, kc, 0:ntok], kc == 0, kc == 7,
                                    [GAb] + self.hT_b, [pgab])
                        pbp, pbpb = self.pmm.next()
                        for g in range(4):
                            self.mm(pbp[:, 0:ntok], BP[:, g, cg], aT[:, g, 0:ntok], g == 0, g == 3,
                                    [BPb, aT_b], [pbpb])
                        sa, sab = self.sgate.next()
                        self.act(sa[:, 0:ntok], pga[:, 0:ntok], AF.Sigmoid, [pgab], [sab])
                        m1, m1b = m1r.next()
                        self.tt("dve", m1[:, 0:ntok], sa[:, 0:ntok], pbp[:, 0:ntok], ALU.mult, [sab, pbpb], [m1b])
                        pgb, pgbb = self.pmm.next()
                        for kc in range(8):
                            self.mm(pgb[:, 0:ntok], GB[:, kc, cs], self.hT[:, kc, 0:ntok], kc == 0, kc == 7,
                                    [GBb] + self.hT_b, [pgbb])
                        psb, psbb = self.pmm.next()
                        for g in range(4):
                            self.mm(psb[:, 0:ntok], SBw[:, g, cg], oT[:, g, 0:ntok], g == 0, g == 3,
                                    [SBb, oT_b], [psbb])
                        sb2, sb2b = self.sgate.next()
                        self.act(sb2[:, 0:ntok], pgb[:, 0:ntok], AF.Sigmoid, [pgbb], [sb2b])
                        m2, m2b = m2r.next()
                        self.tt("dve", m2[:, 0:ntok], sb2[:, 0:ntok], psb[:, 0:ntok], ALU.mult, [sb2b, psbb], [m2b])
                        self.tt("pool", mg[:, c, 0:ntok], m1[:, 0:ntok], m2[:, 0:ntok], ALU.add, [m1b, m2b], [mg_b[c]])
                self.wrel()
                WO = [self.wget("o") for _ in range(2)]
                for si, sub in enumerate(subs):
                    np_, col = sub["np"], sub["col"]
                    for hh in range(2):
                        Wv, Wb = WO[hh]
                        pa, pab = self.pacc.next()
                        for c in range(8):
                            self.mm(pa[0:np_, :], mg[:, c, col:col + np_], Wv[:, c, :], c == 0, c == 7,
                                    [mg_b[c], Wb], [pab])
                        xs_ = xt[0:np_, si, hh * 512:(hh + 1) * 512]
                        self.tt("dve", xs_, pa[0:np_, :], xs_, ALU.add, [pab, xb[si]], [xb[si]])
                self.wrel()
                for si, sub in enumerate(subs):
                    self.rmsnorm_hT(xt[0:sub["np"], si, :], sub["np"], sub["col"], 0, xb[si], self.hT_b[si])
                self.ffn("g2", "u2", "d2", subs, ntok, xt, xb, mid=mid)
                cur = nxt_box[0]
                self.hT_swap()
                for si, sub in enumerate(subs):
                    np_, oc = sub["np"], sub["oc"]
                    xap = xt[0:np_, si, :]
                    k = self.stat_i % 8
                    self.stat_i += 1
                    ss = self.stat[0:np_, 2 * k:2 * k + 1]
                    rs = self.stat[0:np_, 2 * k + 1:2 * k + 2]
                    sb_ = self.stat_b[k]
                    self.act(self.sq[0:np_, :], xap, AF.Square, [xb[si]], [sb_], accum_out=ss)
                    self.act(ss, ss, AF.Sqrt, [sb_], [sb_], scale=1.0 / D, bias=1e-6)
                    S.op("dve", lambda e, rs=rs, ss=ss: e.reciprocal(out=rs, in_=ss), [sb_], [sb_])
                    yt, ytb = ytr.next()
                    self.stt("dve", yt[0:np_, :], xap, rs, self.gb[0:np_, 1, :], ALU.mult, ALU.mult,
                             [xb[si], sb_, self.gb_b], [ytb])
                    self.store(self.y_own[oc:oc + np_, :], yt[0:np_, :], [ytb])

def _consts():
    ident = np.eye(128, dtype=np.float32)
    J = ident[::-1].copy()
    q = np.arange(128)[:, None]
    c = np.arange(128)[None, :]
    M128 = np.where(c <= 127 - q, MASKV, 0.0).astype(np.float32)
    M32 = np.zeros((128, 32), np.float32)
    M32[0:32] = np.where(c[:, 0:32] <= 31 - q[0:32], MASKV, 0.0)
    c_bf = np.concatenate([ident, J, M128, M32], axis=1).astype(ml_dtypes.bfloat16)
    s = np.arange(128)[:, None].astype(np.float64)
    t = np.arange(128)[None, :].astype(np.float64)
    bc, bp, bs = [], [], []
    for w in WINS:
        bc.append(np.where((s > t - w) & (s <= t), 1.0 / w, 0.0) - (s == t))
        bp.append(np.where(s > t - w + 128, 1.0 / w, 0.0))
        cnt = np.minimum(w, t + 1)
        bs.append(np.where((s > t - w) & (s <= t), 1.0 / cnt, 0.0) - (s == t))
    bdc = np.zeros((128, 128))
    bdp = np.zeros((128, 128))
    s3 = np.arange(32)[:, None]
    t3 = np.arange(32)[None, :]
    r3 = np.arange(15)[:, None]
    for g, w in enumerate(WINS):
        bdc[0:32, g * 32:(g + 1) * 32] = np.where((s3 > t3 - w) & (s3 <= t3), 1.0 / w, 0.0) - (s3 == t3)
        bdp[0:15, g * 32:(g + 1) * 32] = np.where(r3 >= 16 + t3 - w, 1.0 / w, 0.0)
    c_f32 = np.concatenate(bc + bp + bs + [bdc, bdp], axis=1).astype(np.float32)
    return np.ascontiguousarray(c_bf), np.ascontiguousarray(c_f32)


_NC_CACHE = {}


def _get_nc():
    if "nc" not in _NC_CACHE:
        _NC_CACHE["nc"] = Builder().build()
    return _NC_CACHE["nc"]


def kernel(x_prompt, x_sample, cache_k, cache_v, state_pool,
           ffn1_norm, ffn1_gate, ffn1_up, ffn1_down, mix_norm, w_in, pool_w, pool_scale,
           w_branch_pool, w_branch_sb, w_out, ffn2_norm, ffn2_gate, ffn2_up, ffn2_down, final_norm):
    f = lambda a: np.ascontiguousarray(np.asarray(a, dtype=np.float32))
    x_prompt, x_sample = f(x_prompt), f(x_sample)
    cache_k, cache_v, state_pool = f(cache_k), f(cache_v), f(state_pool)
    c_bf, c_f32 = _consts()
    gains = np.stack([f(ffn1_norm)[0], f(mix_norm)[0], f(ffn2_norm)[0], f(final_norm)], axis=0)
    pscale = np.ascontiguousarray(f(pool_scale)[0].reshape(4, 128).T)
    shared = {
        "gains": np.ascontiguousarray(gains), "pscale": pscale, "poolw": f(pool_w)[0],
        "w_g1": f(ffn1_gate)[0], "w_u1": f(ffn1_up)[0], "w_d1": f(ffn1_down)[0], "w_in": f(w_in)[0],
        "w_bp": f(w_branch_pool)[0], "w_sb": f(w_branch_sb)[0], "w_o": f(w_out)[0],
        "w_g2": f(ffn2_gate)[0], "w_u2": f(ffn2_up)[0], "w_d2": f(ffn2_down)[0],
        "c_bf": c_bf, "c_f32": c_f32,
    }
    in_maps = []
    for c in range(8):
        b, par = c // 2, c % 2
        xv = np.zeros((NB * 128, D), np.float32)
        if par == 0:
            xv[0:8192] = x_prompt[b]
        else:
            xv[256:NB * 128] = x_prompt[b][0:NB * 128 - 256]
        m = dict(shared)
        m.update({"xv": xv, "xs": x_sample[c], "ck": cache_k[0, c], "cv": cache_v[0, c],
                  "spool": state_pool[0, c]})
        in_maps.append(m)
    nc = _get_nc()
    res = run_bass_kernel_spmd(nc, in_maps, core_ids=list(range(8)))
    R = res.results
    B, SEQ = 4, 8192
    y_prompt = np.zeros((B, SEQ, D), np.float32)
    y_sample = np.zeros((8, 32, D), np.float32)
    nkp = np.zeros((1, B, 8, SEQ, 64), np.float32)
    nvp = np.zeros((1, B, 8, SEQ, 64), np.float32)
    npp = np.zeros((1, B, 15, 512), np.float32)
    nks = np.zeros((1, 8, 8, 32, 64), np.float32)
    nvs = np.zeros((1, 8, 8, 32, 64), np.float32)
    nps = np.zeros((1, 8, 15, 512), np.float32)
    for c in range(8):
        b, par = c // 2, c % 2
        r = R[c]
        for k, vb in enumerate(OWN):
            rb = vb - 2 * par
            if rb < 0 or rb >= 64:
                continue
            y_prompt[b, rb * 128:(rb + 1) * 128] = r["y_own"][k * 128:(k + 1) * 128]
            nkp[0, b, :, rb * 128:(rb + 1) * 128] = r["nk_own"][:, k * 128:(k + 1) * 128]
            nvp[0, b, :, rb * 128:(rb + 1) * 128] = r["nv_own"][:, k * 128:(k + 1) * 128]
        y_sample[c] = r["y_own"][DEC0:DEC0 + 32]
        nks[0, c] = r["nk_own"][:, DEC0:DEC0 + 32]
        nvs[0, c] = r["nv_own"][:, DEC0:DEC0 + 32]
        nps[0, c] = r["pool_s"]
        if par == 0:
            npp[0, b] = r["pool_p"]
    return (y_prompt, y_sample, nkp, nvp, npp, nks, nvs, nps)
```

```python
import numpy as np
import ml_dtypes
from contextlib import ExitStack
import concourse.bass as bass
import concourse.mybir as mybir
from concourse.bass_utils import run_bass_kernel_spmd

F32 = mybir.dt.float32
BF16 = mybir.dt.bfloat16
AF = mybir.ActivationFunctionType
ALU = mybir.AluOpType

D = 1024
DFF = 4096
NB = 65
OWN = [0] + [x for m in range(16) for x in (4 * m + 3, 4 * m + 4)]
SLOT = {b: i for i, b in enumerate(OWN)}
NS = len(OWN)
NQC = NS * 128 + 32
DEC0 = NS * 128
NPAGE = 8
LOOKAHEAD = NPAGE - 1
MASKV = -30000.0
WINS = (2, 4, 8, 16)


class Buf:
    __slots__ = ("w", "rc", "rd", "name")

    def __init__(self, name=""):
        self.w = None
        self.rc = {}
        self.rd = []
        self.name = name


class DSem:
    def __init__(self, h):
        self.h = h
        self.v = 0
        self.buf = Buf("dsem")


class Op:
    __slots__ = ("eng", "fn", "deps", "needed", "val", "dsem", "phase")


class Sched:
    ENG = ("pe", "act", "dve", "pool", "sp")
    COMPUTE = ("pe", "act", "dve", "pool")

    def __init__(self, nc, csem):
        self.nc = nc
        self.csem = csem
        self.q = {e: [] for e in self.ENG}
        self.cnt = {e: 0 for e in self.COMPUTE}
        self.known = {}
        self.phase = 0
        self.dsems = []

    def dsem(self, h):
        d = DSem(h)
        self.dsems.append(d)
        return d

    def op(self, eng, fn, reads=(), writes=(), dsem=None):
        o = Op()
        o.eng = eng
        o.fn = fn
        o.needed = False
        o.val = None
        o.dsem = dsem
        o.phase = self.phase
        deps = []
        if dsem is not None:
            writes = list(writes) + [dsem.buf]
        for b in reads:
            if b.w is not None:
                deps.append(b.w)
        for b in writes:
            if b.w is not None:
                deps.append(b.w)
            deps.extend(b.rc.values())
            deps.extend(b.rd)
        ph = self.phase
        deps = [d for d in deps if d[-1] == ph]
        o.deps = deps
        for d in deps:
            if d[0] == "c":
                d[1].needed = True
        if dsem is not None:
            dsem.v += 16
            tok = ("d", dsem, dsem.v, ph)
        else:
            tok = ("c", o, ph)
        for b in reads:
            if tok[0] == "c":
                b.rc[eng] = tok
            else:
                b.rd.append(tok)
        for b in writes:
            b.w = tok
            b.rc = {}
            b.rd = []
        self.q[eng].append(o)
        return tok

    def _emit(self, e, eng):
        known = self.known
        for o in self.q[eng]:
            need = {}
            for d in o.deps:
                if d[0] == "c":
                    src = d[1]
                    if src.eng == eng and eng == "pe":
                        continue
                    key = ("c", src.eng)
                    val = src.val
                    sem = self.csem[src.eng]
                else:
                    key = ("d", id(d[1]))
                    val = d[2]
                    sem = d[1].h
                if known.get((eng, key), 0) >= val:
                    continue
                if key not in need or need[key][1] < val:
                    need[key] = (sem, val)
            for key, (sem, val) in need.items():
                e.wait_ge(sem, val)
                known[(eng, key)] = val
            ins = o.fn(e)
            if o.dsem is not None:
                ins.then_inc(o.dsem.h, 16)
            elif o.needed:
                ins.then_inc(self.csem[eng], 1)

    def flush(self):
        nc = self.nc
        for eng in self.COMPUTE:
            for o in self.q[eng]:
                if o.needed:
                    self.cnt[eng] += 1
                    o.val = self.cnt[eng]
        with nc.Block() as block:
            @block.tensor
            def _(e):
                self._emit(e, "pe")

            @block.scalar
            def _(e):
                self._emit(e, "act")

            @block.vector
            def _(e):
                self._emit(e, "dve")

            @block.gpsimd
            def _(e):
                self._emit(e, "pool")

            @block.sync
            def _(e):
                self._emit(e, "sp")
                for d in self.dsems:
                    if d.v > 0 and self.known.get(("sp", ("d", id(d))), 0) < d.v:
                        e.wait_ge(d.h, d.v)
                        self.known[("sp", ("d", id(d)))] = d.v
        self.q = {e: [] for e in self.ENG}
        self.phase += 1


class Ring:
    def __init__(self, tiles):
        self.t = tiles
        self.b = [Buf() for _ in tiles]
        self.i = -1

    def next(self):
        self.i = (self.i + 1) % len(self.t)
        return self.t[self.i], self.b[self.i]

    def cur(self):
        return self.t[self.i], self.b[self.i]


class Builder:
    def __init__(self, dbg_tiles=None, phases=(1, 2, 3)):
        self.nc = bass.Bass("TRN2", target_bir_lowering=False)
        self.es = ExitStack()
        self.dbg_tiles = dbg_tiles
        self.phases = phases
        self.skip = set()
        self.dbg_stage = 99
        self.dbg_slots = [0, 1]
        self.dbg_tiles3 = [0, 8]

    def dram(self, name, shape, dt, kind):
        return self.nc.dram_tensor(name, list(shape), dt, kind=kind).ap()

    def sb(self, st, name, shape, dt):
        return st.enter_context(self.nc.sbuf_tensor(f"{name}_p{self.S.phase}", list(shape), dt))

    def ps(self, st, name, shape, dt):
        return st.enter_context(self.nc.psum_tensor(f"{name}_p{self.S.phase}", list(shape), dt))

    def sem(self, name):
        return self.es.enter_context(self.nc.semaphore(name))

    def build(self):
        nc = self.nc
        A = self.dram
        self.xv = A("xv", [NB * 128, D], F32, "ExternalInput")
        self.xs = A("xs", [32, D], F32, "ExternalInput")
        self.ck = A("ck", [8, 1024, 64], F32, "ExternalInput")
        self.cv = A("cv", [8, 1024, 64], F32, "ExternalInput")
        self.spool = A("spool", [15, 512], F32, "ExternalInput")
        self.gains = A("gains", [4, D], F32, "ExternalInput")
        self.pscale = A("pscale", [128, 4], F32, "ExternalInput")
        self.poolw = A("poolw", [4, 128, 128], F32, "ExternalInput")
        self.w_f32 = {
            "g1": A("w_g1", [D, DFF], F32, "ExternalInput"),
            "u1": A("w_u1", [D, DFF], F32, "ExternalInput"),
            "d1": A("w_d1", [DFF, D], F32, "ExternalInput"),
            "in": A("w_in", [D, DFF], F32, "ExternalInput"),
            "bp": A("w_bp", [512, D], F32, "ExternalInput"),
            "sb": A("w_sb", [512, D], F32, "ExternalInput"),
            "o": A("w_o", [D, D], F32, "ExternalInput"),
            "g2": A("w_g2", [D, DFF], F32, "ExternalInput"),
            "u2": A("w_u2", [D, DFF], F32, "ExternalInput"),
            "d2": A("w_d2", [DFF, D], F32, "ExternalInput"),
        }
        self.c_bf = A("c_bf", [128, 128 * 3 + 32], BF16, "ExternalInput")
        self.c_f32 = A("c_f32", [128, 3 * 512 + 2 * 128], F32, "ExternalInput")
        self.y_own = A("y_own", [NQC, D], F32, "ExternalOutput")
        self.nk_own = A("nk_own", [8, NQC, 64], F32, "ExternalOutput")
        self.nv_own = A("nv_own", [8, NQC, 64], F32, "ExternalOutput")
        self.pool_p = A("pool_p", [15, 512], F32, "ExternalOutput")
        self.pool_s = A("pool_s", [15, 512], F32, "ExternalOutput")
        self.w_bf = {k: A("s_" + k, v.shape, BF16, "Internal") for k, v in self.w_f32.items()}
        self.w_cbuf = {k: [] for k in self.w_f32}
        self.x1s = A("s_x1", [NQC, D], F32, "Internal")
        self.hT2s = A("s_hT2", [128, 8, NQC], BF16, "Internal")
        self.QTs = A("s_QT", [4, 128, NQC], BF16, "Internal")
        self.KTs = A("s_KT", [4, 128, NB * 128], BF16, "Internal")
        self.VRs = A("s_VR", [4, 128, NB, 128], BF16, "Internal")
        self.KTDs = A("s_KTD", [4, 128, 32], BF16, "Internal")
        self.VRDs = A("s_VRD", [32, 512], BF16, "Internal")
        self.aTs = A("s_aT", [128, 4, NQC], BF16, "Internal")
        self.OTs = A("s_OT", [4, 128, NQC], BF16, "Internal")

        csem = {e: self.sem("c_" + e) for e in Sched.COMPUTE}
        self.S = Sched(nc, csem)
        self.page_sems = [self.S.dsem(self.sem(f"pg{i}")) for i in range(NPAGE)]
        self.ld_sems = [self.S.dsem(self.sem(f"ld{i}")) for i in range(16)]
        self.st_sems = [self.S.dsem(self.sem(f"st{i}")) for i in range(8)]
        self.cast_sems2 = [self.S.dsem(self.sem(f"cs{i}")) for i in range(2)]
        self._cast_i = 0
        self._ld_i = 0
        self._st_i = 0

        if 1 in self.phases:
            self.phase1()
            self.S.flush()
        if 2 in self.phases:
            self.phase2()
            self.S.flush()
        if 3 in self.phases:
            self.phase3()
            self.S.flush()
        self.es.close()
        return nc

    def load(self, out, in_, wbufs, rbufs=(), eng="sp"):
        d = self.ld_sems[self._ld_i % len(self.ld_sems)]
        self._ld_i += 1
        return self.S.op(eng, lambda e: e.dma_start(out=out, in_=in_), reads=rbufs, writes=wbufs, dsem=d)

    def store(self, out, in_, rbufs, wbufs=(), eng="sp"):
        d = self.st_sems[self._st_i % len(self.st_sems)]
        self._st_i += 1
        return self.S.op(eng, lambda e: e.dma_start(out=out, in_=in_), reads=rbufs, writes=wbufs, dsem=d)

    def mm(self, out, lhsT, rhs, start, stop, reads, writes):
        self.S.op("pe", lambda e: e.matmul(out, lhsT=lhsT, rhs=rhs, start=start, stop=stop), reads, writes)

    def tr(self, out, in_, ident, reads, writes):
        self.S.op("pe", lambda e: e.transpose(out, in_, ident), reads, writes)

    def act(self, out, in_, func, reads, writes, scale=1.0, bias=0.0, accum_out=None):
        if accum_out is None:
            self.S.op("act", lambda e: e.activation(out=out, in_=in_, func=func, scale=scale, bias=bias),
                      reads, writes)
        else:
            self.S.op("act", lambda e: e.activation(out=out, in_=in_, func=func, scale=scale, bias=bias,
                                                    accum_out=accum_out), reads, writes)

    def copy(self, eng, out, in_, reads, writes):
        if eng == "act":
            self.S.op("act", lambda e: e.activation(out=out, in_=in_, func=AF.Copy), reads, writes)
        else:
            self.S.op(eng, lambda e: e.tensor_copy(out=out, in_=in_), reads, writes)

    def tt(self, eng, out, in0, in1, op, reads, writes):
        self.S.op(eng, lambda e: e.tensor_tensor(out=out, in0=in0, in1=in1, op=op), reads, writes)

    def stt(self, eng, out, in0, scalar, in1, op0, op1, reads, writes):
        self.S.op(eng, lambda e: e.scalar_tensor_tensor(out=out, in0=in0, scalar=scalar, in1=in1,
                                                        op0=op0, op1=op1), reads, writes)

    def wstream_init(self, st, plan):
        self.pages = [self.sb(st, f"page{i}", [128, 4096], BF16) for i in range(NPAGE)]
        self.page_bufs = [Buf(f"page{i}") for i in range(NPAGE)]
        self.wplan = plan
        self.w_issued = 0
        self.w_used = 0
        self.w_rel = 0

    def _issue(self, upto):
        while self.w_issued < min(upto, len(self.wplan), self.w_rel + NPAGE):
            i = self.w_issued
            key, src, shp, rng = self.wplan[i]
            rbufs = [b_ for (r0, r1, c0, c1, b_) in self.w_cbuf[key]
                     if r0 < rng[1] and rng[0] < r1 and c0 < rng[3] and rng[2] < c1]
            pg = i % NPAGE
            dst = self.pages[pg][:, 0:shp[1] * shp[2]].rearrange("p (a b) -> p a b", a=shp[1])
            if shp[0] < 128:
                dst = self.pages[pg][0:shp[0], 0:shp[1] * shp[2]].rearrange("p (a b) -> p a b", a=shp[1])
            self.S.op("sp", lambda e, dst=dst, src=src: e.dma_start(out=dst, in_=src),
                      reads=rbufs, writes=[self.page_bufs[pg]], dsem=self.page_sems[pg])
            self.w_issued += 1

    def wget(self, key):
        i = self.w_used
        k, src, shp, _ = self.wplan[i]
        assert k == key, (k, key, i)
        self._issue(i + 1 + LOOKAHEAD)
        self.w_used += 1
        pg = i % NPAGE
        v = self.pages[pg][0:shp[0], 0:shp[1] * shp[2]].rearrange("p (a b) -> p a b", a=shp[1])
        return v, self.page_bufs[pg]

    def wrel(self):
        self.w_rel = self.w_used

    def wsrc_cols(self, key, c0, ncol=512):
        w = self.w_bf[key]
        return (key, w[:, c0:c0 + ncol].rearrange("(kc p) n -> p kc n", p=128), (128, 8, ncol),
                (0, 1024, c0, c0 + ncol))

    def wsrc_rows(self, key, r0, nrow, c0, ncol):
        w = self.w_bf[key]
        return (key, w[r0:r0 + nrow, c0:c0 + ncol].rearrange("(kc p) n -> p kc n", p=128),
                (128, nrow // 128, ncol), (r0, r0 + nrow, c0, c0 + ncol))

    def cast_weights(self, keys):
        if "cast" in self.skip:
            return
        work = []
        for k in keys:
            if ("cast_" + k) in self.skip:
                continue
            src = self.w_f32[k]
            dst = self.w_bf[k]
            rows, cols = src.shape
            if rows == 1024 and cols == 4096:
                chunks = [(slice(0, rows), slice(c, c + 512)) for c in range(0, cols, 512)]
            else:
                step = max(1, min(256, (1 << 20) // cols))
                chunks = [(slice(r, r + step), slice(0, cols)) for r in range(0, rows, step)]
            item = []
            for rs, cs in chunks:
                b_ = Buf()
                self.w_cbuf[k].append((rs.start, rs.stop, cs.start, cs.stop, b_))
                item.append((b_, dst[rs, cs], src[rs, cs]))
            work.append(item)
        order = []
        if len(work) >= 2 and len(work[0]) == len(work[1]):
            for a, b in zip(work[0], work[1]):
                order += [a, b]
            work = work[2:]
        for w_ in work:
            order += w_
        for b_, o, s_ in order:
            d = self.cast_sems2[self._cast_i % 2]
            self._cast_i += 1
            self.S.op("pool", lambda e, o=o, s_=s_: e.dma_start(out=o, in_=s_), reads=[],
                      writes=[b_], dsem=d)

    def common_alloc(self, st, gidx, with_cf=True):
        nc = self.nc
        self.cbf = self.sb(st, "cbf", [128, 128 * 3 + 32], BF16)
        self.cbf_b = Buf("cbf")
        if with_cf:
            self.cf = self.sb(st, "cf", [128, 3 * 512 + 256], F32)
        self.cf_b = Buf("cf")
        self.gb = self.sb(st, "gb", [128, 2, D], F32)
        self.gb_b = Buf("gb")
        self.load(self.cbf[:], self.c_bf, [self.cbf_b])
        if with_cf:
            self.load(self.cf[:], self.c_f32, [self.cf_b])
        for i, gi in enumerate(gidx):
            if "gb" in self.skip:
                break
            self.load(self.gb[:, i, :], self.gains[gi:gi + 1, :].partition_broadcast(128), [self.gb_b])
        self.ident = self.cbf[:, 0:128]
        self.J = self.cbf[:, 128:256]
        self.xt = Ring([self.sb(st, f"xt{i}", [128, 4, D], F32) for i in range(2)])
        self.xt_sb = [[Buf() for _ in range(4)] for _ in range(2)]
        self.hb = Ring([self.sb(st, f"hb{i}", [128, D], BF16) for i in range(4)])
        self.sq = self.sb(st, "sq", [128, D], BF16)
        self.stat = self.sb(st, "stat", [128, 16], F32)
        self.stat_b = [Buf() for _ in range(8)]
        self.stat_i = 0
        self.hT_ring = [self.sb(st, f"hT{i}", [128, 8, 512], BF16) for i in range(2)]
        self.hT_bufs = [[Buf() for _ in range(4)] for _ in range(2)]
        self.hT_i = 0
        self.hT = self.hT_ring[0]
        self.hT_b = self.hT_bufs[0]
        self.actT = self.sb(st, "actT", [128, 32, 512], BF16)
        self.actT_b = [Buf() for _ in range(32)]
        self.sgate = Ring([self.sb(st, f"sgate{i}", [128, 512], F32) for i in range(2)])
        self.pmm = Ring([self.ps(st, f"pmm{i}", [128, 512], F32) for i in range(4)])
        self.pacc = Ring([self.ps(st, f"pacc{i}", [128, 512], F32) for i in range(2)])
        self.pmisc = Ring([self.ps(st, f"pmisc{i}", [128, 512], F32) for i in range(2)])
        self.pmisc_bf = [t[:].bitcast(BF16) for t in self.pmisc.t] if False else None

    def rmsnorm_hT(self, xap, np_, col, gi, xbuf, hTbuf, defer=False):
        si = self.stat_i % 8
        self.stat_i += 1
        ss = self.stat[0:np_, 2 * si:2 * si + 1]
        rs = self.stat[0:np_, 2 * si + 1:2 * si + 2]
        sb_ = self.stat_b[si]
        self.act(self.sq[0:np_, :], xap, AF.Square, [xbuf], [sb_], accum_out=ss)
        self.act(ss, ss, AF.Sqrt, [sb_], [sb_], scale=1.0 / D, bias=1e-6)
        self.S.op("dve", lambda e: e.reciprocal(out=rs, in_=ss), [sb_], [sb_])
        hb, hbb = self.hb.next()
        self.stt("dve", hb[0:np_, :], xap, rs, self.gb[0:np_, gi, :], ALU.mult, ALU.mult,
                 [xbuf, sb_, self.gb_b], [hbb])
        if defer:
            return (hb, hbb, np_, col, hTbuf)
        self.rmsnorm_part2((hb, hbb, np_, col, hTbuf))

    def rmsnorm_part2(self, h):
        hb, hbb, np_, col, hTbuf = h
        pm, pmb = self.pmisc.next()
        pv = pm[:].bitcast(BF16)
        for c in range(8):
            self.tr(pv[:, c * np_:(c + 1) * np_], hb[0:np_, c * 128:(c + 1) * 128], self.ident[0:np_, 0:np_],
                    [hbb, self.cbf_b], [pmb])
        self.copy("dve", self.hT[:, :, col:col + np_],
                  pv[:, 0:8 * np_].rearrange("p (c n) -> p c n", c=8), [pmb], [hTbuf])

    def hT_swap(self):
        self.hT_i ^= 1
        self.hT = self.hT_ring[self.hT_i]
        self.hT_b = self.hT_bufs[self.hT_i]

    def ffn(self, kg, ku, kd, subs, ntok, xt, xbufs, mid=None):
        hTr = self.hT_b
        for jg in range(8):
            G, Gb = self.wget(kg)
            U, Ub = self.wget(ku)
            for jj in range(4):
                j = jg * 4 + jj
                pg, pgb = self.pmm.next()
                for kc in range(8):
                    self.mm(pg[:, 0:ntok], G[:, kc, jj * 128:(jj + 1) * 128], self.hT[:, kc, 0:ntok],
                            kc == 0, kc == 7, [Gb] + hTr, [pgb])
                pu, pub = self.pmm.next()
                for kc in range(8):
                    self.mm(pu[:, 0:ntok], U[:, kc, jj * 128:(jj + 1) * 128], self.hT[:, kc, 0:ntok],
                            kc == 0, kc == 7, [Ub] + hTr, [pub])
                sg, sgb = self.sgate.next()
                self.act(sg[:, 0:ntok], pg[:, 0:ntok], AF.Silu, [pgb], [sgb])
                self.tt("dve", self.actT[:, j, 0:ntok], sg[:, 0:ntok], pu[:, 0:ntok], ALU.mult,
                        [sgb, pub], [self.actT_b[j]])
            self.wrel()
        if mid is not None:
            mid()
        for hh in range(2):
            Dp = [self.wget(kd) for _ in range(4)]
            for si, sub in enumerate(subs):
                np_, col = sub["np"], sub["col"]
                pa, pab = self.pacc.next()
                for fg in range(4):
                    Dv, Db = Dp[fg]
                    for fc in range(8):
                        f = fg * 8 + fc
                        self.mm(pa[0:np_, :], self.actT[:, f, col:col + np_], Dv[:, fc, :],
                                f == 0, f == 31, [self.actT_b[f], Db], [pab])
                xs_ = xt[0:np_, si, hh * 512:(hh + 1) * 512]
                self.stt("dve", xs_, pa[0:np_, :], 0.5, xs_, ALU.mult, ALU.add, [pab, xbufs[si]], [xbufs[si]])
            self.wrel()

    def tiles1(self):
        tl = []
        for ti in range(16):
            subs = []
            for s in range(4):
                b = 4 * ti + s
                subs.append(dict(kind="p", blk=b, np=128, col=s * 128, own=b in SLOT, slot=SLOT.get(b)))
            tl.append(dict(subs=subs, ntok=512))
        subs = [dict(kind="p", blk=64, np=128, col=0, own=True, slot=SLOT[64]),
                dict(kind="d", blk=None, np=32, col=128, own=True, slot=None)]
        tl.append(dict(subs=subs, ntok=160))
        return tl

    def phase1(self):
        nc = self.nc
        S = self.S
        tiles = self.tiles1()
        if self.dbg_tiles is not None:
            tiles = [tiles[i] for i in self.dbg_tiles]
        plan = []
        if self.dbg_stage < 4:
            tiles = [dict(t, noproj=True) for t in tiles]
        for t in tiles:
            for jg in range(8):
                plan.append(self.wsrc_cols("g1", jg * 512))
                plan.append(self.wsrc_cols("u1", jg * 512))
            for hh in range(2):
                for fg in range(4):
                    plan.append(self.wsrc_rows("d1", fg * 1024, 1024, hh * 512, 512))
            if self.dbg_stage < 2:
                plan = []
            if t.get("noproj"):
                continue
            for c0 in (0, 1024, 1536, 512):
                plan.append(self.wsrc_cols("in", c0))
        with ExitStack() as st:
            self.cast_weights(["g1", "u1", "d1", "in"])
            self.common_alloc(st, (0, 1))
            self.wstream_init(st, plan)
            uring = Ring([self.sb(st, f"u{i}", [128, 512], F32) for i in range(3)])
            ktm = Ring([self.sb(st, f"ktm{i}", [128, 512], F32) for i in range(1)])
            vtm = Ring([self.sb(st, f"vtm{i}", [128, 512], F32) for i in range(1)])
            kbf = Ring([self.sb(st, f"kbf{i}", [128, 512], BF16) for i in range(2)])
            vbf = Ring([self.sb(st, f"vbf{i}", [128, 512], BF16) for i in range(2)])
            ktr = Ring([self.sb(st, f"ktr{i}", [128, 4, 128], BF16) for i in range(2)])
            vrv = Ring([self.sb(st, f"vrv{i}", [128, 512], BF16) for i in range(2)])
            qts = Ring([self.sb(st, f"qts{i}", [128, 4, 128], BF16) for i in range(2)])
            dts = Ring([self.sb(st, f"dts{i}", [128, 4, 128], BF16) for i in range(2)])
            ats = Ring([self.sb(st, f"ats{i}", [128, 4, 128], BF16) for i in range(2)])
            pwf = self.sb(st, "pwf", [128, 4, 128], F32)
            pwb = self.sb(st, "pwb", [128, 4, 128], BF16)
            pw_b = Buf()
            psc = self.sb(st, "psc", [128, 4], F32)
            psc_b = Buf()
            spl = self.sb(st, "spl", [15, 512], F32)
            spl_b = Buf()
            self.load(pwf[:], self.poolw.rearrange("g c e -> c g e"), [pw_b])
            self.copy("dve", pwb[:], pwf[:], [pw_b], [pw_b])
            self.load(psc[:], self.pscale, [psc_b])
            self.load(spl[:], self.spool, [spl_b])
            cf = self.cf
            bc = lambda g: cf[:, g * 128:(g + 1) * 128]
            bp = lambda g: cf[:, 512 + g * 128:512 + (g + 1) * 128]
            bs = lambda g: cf[:, 1024 + g * 128:1024 + (g + 1) * 128]
            bdc = lambda g: cf[0:32, 1536 + g * 32:1536 + (g + 1) * 32]
            bdp = lambda g: cf[0:15, 1664 + g * 32:1664 + (g + 1) * 32]

            u_prev = None

            def load_x(t):
                xt, _ = self.xt.next()
                xb = self.xt_sb[self.xt.i]
                for si, sub in enumerate(t["subs"]):
                    if sub["kind"] == "p":
                        b = sub["blk"]
                        self.load(xt[:, si, :], self.xv[b * 128:(b + 1) * 128, :], [xb[si]])
                    else:
                        self.load(xt[0:32, si, :], self.xs, [xb[si]])
                return xt, xb

            def norm1_part1(t, xt, xb, hTb):
                return [self.rmsnorm_hT(xt[0:sub["np"], si, :], sub["np"], sub["col"], 0, xb[si], hTb[si], defer=True)
                        for si, sub in enumerate(t["subs"])]

            cur = load_x(tiles[0])
            for h in norm1_part1(tiles[0], cur[0], cur[1], self.hT_b):
                self.rmsnorm_part2(h)
            for ti, t in enumerate(tiles):
                subs, ntok = t["subs"], t["ntok"]
                xt, xb = cur
                nxt = None
                pend = []
                if ti + 1 < len(tiles):
                    nxt = load_x(tiles[ti + 1])
                    pend = norm1_part1(tiles[ti + 1], nxt[0], nxt[1], self.hT_bufs[self.hT_i ^ 1])

                def mid(pend=pend):
                    keep = (self.hT, self.hT_b)
                    self.hT_swap()
                    for h in pend:
                        self.rmsnorm_part2(h)
                    self.hT_swap()
                    assert keep[0] is self.hT

                self.ffn("g1", "u1", "d1", subs, ntok, xt, xb, mid=mid)
                cur = nxt
                if self.dbg_stage < 3:
                    continue
                for si, sub in enumerate(subs):
                    np_ = sub["np"]
                    if sub["own"]:
                        oc = sub["slot"] * 128 if sub["kind"] == "p" else DEC0
                        self.store(self.x1s[oc:oc + np_, :], xt[0:np_, si, :], [xb[si]])
                    self.rmsnorm_hT(xt[0:np_, si, :], np_, sub["col"], 1, xb[si], self.hT_b[si])
                    if sub["own"]:
                        self.store(self.hT2s[:, :, oc:oc + np_], self.hT[:, :, sub["col"]:sub["col"] + np_],
                                   [self.hT_b[si]])
                if self.dbg_stage < 4:
                    continue
                Uw, Uwb = self.wget("in")
                Kw, Kwb = self.wget("in")
                Vw, Vwb = self.wget("in")
                Qw, Qwb = self.wget("in")
                for si, sub in enumerate(subs):
                    np_, col = sub["np"], sub["col"]
                    dec = sub["kind"] == "d"
                    own = sub["own"]
                    oc = (sub["slot"] * 128 if not dec else DEC0) if own else None
                    hTs = [self.hT_b[si]]
                    Jn = self.J[0:np_, 128 - np_:128]
                    pa, pab = self.pacc.next()
                    for kc in range(8):
                        self.mm(pa[0:np_, :], self.hT[:, kc, col:col + np_], Uw[:, kc, :], kc == 0, kc == 7,
                                hTs + [Uwb], [pab])
                    ut, utb = uring.next()
                    self.copy("act", ut[0:np_, :], pa[0:np_, :], [pab], [utb])
                    pa, pab = self.pacc.next()
                    for kc in range(8):
                        self.mm(pa[0:np_, :], self.hT[:, kc, col:col + np_], Kw[:, kc, :], kc == 0, kc == 7,
                                hTs + [Kwb], [pab])
                    kb_, kbb = kbf.next()
                    if own and "nkst" not in self.skip:
                        km, kmb = ktm.next()
                        self.copy("act", km[0:np_, :], pa[0:np_, :], [pab], [kmb])
                        self.copy("pool", kb_[0:np_, :], km[0:np_, :], [kmb], [kbb])
                        for h in range(8):
                            self.store(self.nk_own[h, oc:oc + np_, :], km[0:np_, h * 64:(h + 1) * 64], [kmb])
                    else:
                        self.copy("act", kb_[0:np_, :], pa[0:np_, :], [pab], [kbb])
                    pm, pmb = self.pmisc.next()
                    for j in range(4):
                        self.mm(pm[:, j * np_:(j + 1) * np_], kb_[0:np_, j * 128:(j + 1) * 128], Jn, True, True,
                                [kbb, self.cbf_b], [pmb])
                    kr, krb = ktr.next()
                    self.copy("dve", kr[:, :, 0:np_], pm[:, 0:4 * np_].rearrange("p (j n) -> p j n", j=4),
                              [pmb], [krb])
                    if "ktst" in self.skip:
                        pass
                    elif not dec:
                        rb = 64 - sub["blk"]
                        self.store(self.KTs[:, :, rb * 128:(rb + 1) * 128].rearrange("j p n -> p j n"),
                                   kr[:], [krb])
                    else:
                        self.store(self.KTDs.rearrange("j p n -> p j n"), kr[:, :, 0:32], [krb])
                    pa, pab = self.pacc.next()
                    for kc in range(8):
                        self.mm(pa[0:np_, :], self.hT[:, kc, col:col + np_], Vw[:, kc, :], kc == 0, kc == 7,
                                hTs + [Vwb], [pab])
                    vb_, vbb = vbf.next()
                    if own and "nkst" not in self.skip:
                        vm, vmb = vtm.next()
                        self.copy("act", vm[0:np_, :], pa[0:np_, :], [pab], [vmb])
                        self.copy("pool", vb_[0:np_, :], vm[0:np_, :], [vmb], [vbb])
                        for h in range(8):
                            self.store(self.nv_own[h, oc:oc + np_, :], vm[0:np_, h * 64:(h + 1) * 64], [vmb])
                    else:
                        self.copy("act", vb_[0:np_, :], pa[0:np_, :], [pab], [vbb])
                    pm, pmb = self.pmisc.next()
                    self.mm(pm[0:np_, :], Jn, vb_[0:np_, :], True, True, [vbb, self.cbf_b], [pmb])
                    vr, vrb = vrv.next()
                    self.copy("dve", vr[0:np_, :], pm[0:np_, :], [pmb], [vrb])
                    if "ktst" in self.skip:
                        pass
                    elif not dec:
                        self.store(self.VRs[:, :, 64 - sub["blk"], :].rearrange("j p n -> p j n"),
                                   vr[:].rearrange("p (j n) -> p j n", j=4), [vrb])
                    else:
                        self.store(self.VRDs, vr[0:32, :], [vrb])
                    if own and "q" not in self.skip:
                        pm, pmb = self.pmisc.next()
                        for j in range(4):
                            for kc in range(8):
                                self.mm(pm[:, j * np_:(j + 1) * np_], Qw[:, kc, j * 128:(j + 1) * 128],
                                        self.hT[:, kc, col:col + np_], kc == 0, kc == 7, hTs + [Qwb], [pmb])
                        qt, qtb = qts.next()
                        self.copy("act", qt[:, :, 0:np_], pm[:, 0:4 * np_].rearrange("p (j n) -> p j n", j=4),
                                  [pmb], [qtb])
                        self.store(self.QTs[:, :, oc:oc + np_].rearrange("j p n -> p j n"), qt[:, :, 0:np_], [qtb])
                    if own and "pool" not in self.skip:
                        pm, pmb = self.pmisc.next()
                        for g in range(4):
                            ug = ut[0:np_, g * 128:(g + 1) * 128]
                            if dec:
                                self.mm(pm[:, g * np_:(g + 1) * np_], ug, bdc(g), True, False,
                                        [utb, self.cf_b], [pmb])
                                self.mm(pm[:, g * np_:(g + 1) * np_], spl[0:15, g * 128:(g + 1) * 128], bdp(g),
                                        False, True, [spl_b, self.cf_b], [pmb])
                            elif sub["blk"] == 0:
                                self.mm(pm[:, g * np_:(g + 1) * np_], ug, bs(g), True, True,
                                        [utb, self.cf_b], [pmb])
                            else:
                                upt, upb = u_prev
                                self.mm(pm[:, g * np_:(g + 1) * np_], ug, bc(g), True, False,
                                        [utb, self.cf_b], [pmb])
                                self.mm(pm[:, g * np_:(g + 1) * np_], upt[:, g * 128:(g + 1) * 128], bp(g),
                                        False, True, [upb, self.cf_b], [pmb])
                        dt_, dtb = dts.next()
                        self.copy("dve", dt_[:, :, 0:np_], pm[:, 0:4 * np_].rearrange("p (j n) -> p j n", j=4),
                                  [pmb], [dtb])
                        pm, pmb = self.pmisc.next()
                        for g in range(4):
                            self.mm(pm[:, g * np_:(g + 1) * np_], pwb[:, g, :], dt_[:, g, 0:np_], True, True,
                                    [pw_b, dtb], [pmb])
                        at, atb = ats.next()
                        for g in range(4):
                            self.S.op("dve", lambda e, o=at[:, g, 0:np_], i=pm[:, g * np_:(g + 1) * np_],
                                      s=psc[:, g:g + 1]: e.tensor_scalar(out=o, in0=i, scalar1=s, scalar2=None,
                                                                         op0=ALU.mult),
                                      [pmb, psc_b], [atb])
                        self.store(self.aTs[:, :, oc:oc + np_], at[:, :, 0:np_], [atb])
                    if not dec:
                        if sub["blk"] == 63:
                            self.store(self.pool_p, ut[113:128, :], [utb])
                        u_prev = (ut, utb)
                    else:
                        self.store(self.pool_s, ut[17:32, :], [utb])
                self.wrel()
                self.hT_swap()

    def phase2(self):
        S = self.S
        with ExitStack() as st:
            self.cast_weights(["g2", "u2", "bp", "sb", "o", "d2"])
            cbf = self.sb(st, "cbf2", [128, 128 * 3 + 32], BF16)
            cbf_b = Buf()
            self.load(cbf[:], self.c_bf, [cbf_b])
            ident = cbf[:, 0:128]
            J = cbf[:, 128:256]
            M128 = cbf[:, 256:384]
            M32 = cbf[0:32, 384:416]
            ones = self.sb(st, "ones", [128, 1024], BF16)
            ones_b = Buf()
            S.op("pool", lambda e: e.memset(ones[:], 1.0), [], [ones_b])
            ktp = Ring([self.sb(st, f"ktp{i}", [128, NB * 128], BF16) for i in range(2)])
            vrp = Ring([self.sb(st, f"vrp{i}", [128, NB, 128], BF16) for i in range(2)])
            qtp = Ring([(self.sb(st, f"qA{i}", [128, NQC], BF16), self.sb(st, f"qB{i}", [128, NQC], BF16))
                        for i in range(2)])
            for (qa_, qb_), qbuf_ in zip(qtp.t, qtp.b):
                S.op("pool", lambda e, t=qa_: e.memset(t[64:128, :], 0.0), [], [qbuf_])
                S.op("pool", lambda e, t=qb_: e.memset(t[0:64, :], 0.0), [], [qbuf_])
            ktd = Ring([self.sb(st, f"ktd{i}", [128, 32 + 1024], BF16) for i in range(2)])
            vrd = Ring([self.sb(st, f"vrd{i}", [128, 9, 128], BF16) for i in range(2)])
            ot = Ring([self.sb(st, f"ot{i}", [128, NQC], BF16) for i in range(2)])
            ckf = Ring([self.sb(st, f"ckf{i}", [128, 8, 2, 64], F32) for i in range(2)])
            ckb = Ring([self.sb(st, f"ckb{i}", [128, 8, 2, 64], BF16) for i in range(2)])
            zr = Ring([self.ps(st, f"z{i}", [128, 1024], F32) for i in range(2)])
            wtp = Ring([self.ps(st, f"wtp{i}", [128, 1024], BF16) for i in range(2)])
            oacc = Ring([self.ps(st, f"oacc{i}", [128, 512], F32) for i in range(2)])
            sg = Ring([self.sb(st, f"sg{i}", [128, 1024], F32) for i in range(3)])
            Pb = Ring([self.sb(st, f"Pb{i}", [128, 1025], F32) for i in range(3)])
            wb = Ring([self.sb(st, f"wb{i}", [128, 1024], BF16) for i in range(3)])
            wT = Ring([self.sb(st, f"wT{i}", [128, 1024], BF16) for i in range(3)])
            state = {"prevP": None, "oacc": None}
            Pbc = [Buf() for _ in range(len(Pb.t))]

            def stageA(ch):
                nq, nc_ = ch["nq"], ch["ncols"]
                z, zb = zr.next()
                for off in range(0, nc_, 512):
                    n = min(512, nc_ - off)
                    self.mm(z[0:nq, off:off + n], ch["q"], ch["kT"][:, off:off + n], True,
                            not (ch["first"] and off == 0), ch["rb"], [zb])
                if ch["first"]:
                    M = M128 if nq == 128 else M32
                    self.mm(z[0:nq, 0:nq], ident[0:nq, 0:nq], M[0:nq, 0:nq], False, True, [cbf_b], [zb])
                sgt, sgb = sg.next()
                self.act(sgt[0:nq, 0:nc_], z[0:nq, 0:nc_], AF.Sigmoid, [zb], [sgb], scale=-0.125)
                pt, pb = Pb.next()
                pbc = Pbc[Pb.i]
                if ch["first"]:
                    S.op("pool", lambda e: e.memset(pt[0:nq, 0:1], 1.0), [], [pbc])
                    init = 1.0
                    rd = []
                else:
                    ppt, ppb, pn = state["prevP"]
                    self.copy("pool", pt[0:nq, 0:1], ppt[0:nq, pn:pn + 1], [ppb], [pbc])
                    init = ppt[0:nq, pn:pn + 1]
                    rd = [ppb]
                S.op("dve", lambda e: e.tensor_tensor_scan(out=pt[0:nq, 1:nc_ + 1], data0=sgt[0:nq, 0:nc_],
                                                           data1=ones[0:nq, 0:nc_], initial=init,
                                                           op0=ALU.mult, op1=ALU.mult),
                     [sgb, ones_b] + rd, [pb])
                ch["Pc"] = pbc
                state["prevP"] = (pt, pb, nc_)
                ch["P"] = (pt, pb)

            def stageA2(ch):
                nq, nc_ = ch["nq"], ch["ncols"]
                pt, pb = ch["P"]
                wt, wbb = wb.next()
                self.tt("dve", wt[0:nq, 0:nc_], pt[0:nq, 0:nc_], pt[0:nq, 1:nc_ + 1], ALU.subtract,
                        [pb, ch["Pc"]], [wbb])
                ch["wb"] = (wt, wbb)

            def stageB(ch):
                nq = ch["nq"]
                wt, wbb = ch["wb"]
                tp, tpb = wtp.next()
                blocks = ch["blocks"]
                kb = blocks[0][1]
                off = 0
                for bi, (v_ap, kb_) in enumerate(blocks):
                    self.tr(tp[0:kb_, bi * 128:bi * 128 + nq], wt[0:nq, off:off + kb_], ident[0:nq, 0:nq],
                            [wbb, cbf_b], [tpb])
                    off += kb_
                wTt, wTb = wT.next()
                nb = len(blocks)
                self.copy("act", wTt[0:kb, 0:nb * 128].rearrange("p (b n) -> p b n", b=nb)[:, :, 0:nq],
                          tp[0:kb, 0:nb * 128].rearrange("p (b n) -> p b n", b=nb)[:, :, 0:nq], [tpb], [wTb])
                ch["wT"] = (wTt, wTb)

            def stageC(ch):
                nq = ch["nq"]
                blocks = ch["blocks"]
                nb = len(blocks)
                wTt, wTb = ch["wT"]
                if ch["first"]:
                    state["oacc"] = oacc.next()
                oa, oab = state["oacc"]
                pr0 = ch["pr0"]
                for bi, (v_ap, kb_) in enumerate(blocks):
                    self.mm(oa[:, 0:nq], v_ap, wTt[0:kb_, bi * 128:bi * 128 + nq],
                            ch["first"] and bi == 0, ch["last"] and bi == nb - 1, [wTb] + ch["vb"], [oab])
                if ch["last"]:
                    ott, otb = ch["ot"]
                    oc = ch["ocol"]
                    self.copy("act", ott[pr0:pr0 + 64, oc:oc + nq], oa[pr0:pr0 + 64, 0:nq], [oab], [otb])

            slots = list(range(NS)) if self.dbg_tiles is None else self.dbg_slots
            def prep_loads(j):
                kt, ktb = ktp.next()
                vr, vrb = vrp.next()
                qt, qtb = qtp.next()
                kd, kdb = ktd.next()
                vd, vdb = vrd.next()
                if self.dbg_tiles is None:
                    self.load(kt[:], self.KTs[j], [ktb])
                    self.load(vr[:], self.VRs[j], [vrb])
                    self.load(qt[0][0:64, :], self.QTs[j][0:64, :], [qtb])
                    self.load(qt[1][64:128, :], self.QTs[j][64:128, :], [qtb])
                else:
                    self.load(kt[:, 61 * 128:], self.KTs[j][:, 61 * 128:], [ktb])
                    self.load(vr[:, 61:, :], self.VRs[j][:, 61:, :], [vrb])
                    for hq in range(2):
                        hs = slice(hq * 64, hq * 64 + 64)
                        self.load(qt[hq][hs, 0:256], self.QTs[j][hs, 0:256], [qtb])
                        self.load(qt[hq][hs, DEC0:], self.QTs[j][hs, DEC0:], [qtb])
                self.load(kd[:, 0:32], self.KTDs[j], [kdb])
                self.load(vd[0:32, 0, :], self.VRDs[:, j * 128:(j + 1) * 128], [vdb])
                cfs = []
                for src in (self.ck, self.cv):
                    cf_, cfb = ckf.next()
                    for hh in range(2):
                        self.load(cf_[:, :, hh, :], src[2 * j + hh].rearrange("(b p) d -> p b d", p=128), [cfb])
                    cfs.append((cf_, cfb))
                return dict(kt=kt, ktb=ktb, vr=vr, vrb=vrb, qt=qt, qtb=qtb, kd=kd, kdb=kdb, vd=vd, vdb=vdb, cfs=cfs)

            def prep_compute(P):
                kd, kdb, vd, vdb = P["kd"], P["kdb"], P["vd"], P["vdb"]
                for (cf_, cfb), is_k in zip(P["cfs"], (True, False)):
                    cb_, cbb = ckb.next()
                    self.copy("pool", cb_[:], cf_[:], [cfb], [cbb])
                    z, zb = zr.next()
                    for blk in range(8):
                        rb = 7 - blk
                        cblk = cb_[:, blk, :, :].rearrange("p h d -> p (h d)")
                        if is_k:
                            self.mm(z[:, rb * 128:(rb + 1) * 128], cblk, J, True, True, [cbb, cbf_b], [zb])
                        else:
                            self.mm(z[:, rb * 128:(rb + 1) * 128], J, cblk, True, True, [cbb, cbf_b], [zb])
                    if is_k:
                        self.copy("act", kd[:, 32:1056], z[:, 0:1024], [zb], [kdb])
                    else:
                        self.copy("act", vd[:, 1:9, :], z[:, 0:1024].rearrange("p (b n) -> p b n", b=8), [zb], [vdb])


            nxtP = prep_loads(0)
            prep_compute(nxtP)
            for j in range(4):
                P = nxtP
                kt, ktb, vr, vrb, qt, qtb = P["kt"], P["ktb"], P["vr"], P["vrb"], P["qt"], P["qtb"]
                kd, kdb, vd, vdb = P["kd"], P["kdb"], P["vd"], P["vdb"]
                ott, otb = ot.next()
                if j < 3:
                    nxtP = prep_loads(j + 1)
                for hh in range(2):
                    if hh == 1 and j < 3:
                        prep_compute(nxtP)
                    pr0 = hh * 64
                    chunks = []
                    for k in slots:
                        a = OWN[k]
                        c0 = (64 - a) * 128
                        nblk = a + 1
                        for ci in range(0, nblk, 8):
                            nb = min(8, nblk - ci)
                            chunks.append(dict(
                                nq=128, ncols=nb * 128, first=ci == 0, last=ci + nb == nblk, pr0=pr0,
                                q=qt[hh][:, k * 128:(k + 1) * 128],
                                kT=kt[:, c0 + ci * 128:c0 + (ci + nb) * 128],
                                blocks=[(vr[:, 64 - a + ci + b, :], 128) for b in range(nb)],
                                rb=[qtb, ktb], vb=[vrb], ot=(ott, otb), ocol=k * 128))
                    chunks.append(dict(nq=32, ncols=32, first=True, last=False, pr0=pr0,
                                       q=qt[hh][:, DEC0:DEC0 + 32], kT=kd[:, 0:32],
                                       blocks=[(vd[0:32, 0, :], 32)],
                                       rb=[qtb, kdb], vb=[vdb], ot=(ott, otb), ocol=DEC0))
                    chunks.append(dict(nq=32, ncols=1024, first=False, last=True, pr0=pr0,
                                       q=qt[hh][:, DEC0:DEC0 + 32], kT=kd[:, 32:1056],
                                       blocks=[(vd[:, 1 + b, :], 128) for b in range(8)],
                                       rb=[qtb, kdb], vb=[vdb], ot=(ott, otb), ocol=DEC0))
                    n_ch = len(chunks)
                    for i in range(n_ch + 3):
                        if i < n_ch:
                            stageA(chunks[i])
                        if 0 <= i - 1 < n_ch:
                            stageA2(chunks[i - 1])
                        if 0 <= i - 2 < n_ch:
                            stageB(chunks[i - 2])
                        if 0 <= i - 3 < n_ch:
                            stageC(chunks[i - 3])
                if self.dbg_tiles is None:
                    self.store(self.OTs[j], ott[:], [otb])
                else:
                    self.store(self.OTs[j][:, 0:256], ott[:, 0:256], [otb])
                    self.store(self.OTs[j][:, DEC0:], ott[:, DEC0:], [otb])

    def tiles3(self):
        tl = []
        for t in range(8):
            subs = [dict(kind="p", np=128, col=s * 128, slot=4 * t + s, oc=(4 * t + s) * 128) for s in range(4)]
            tl.append(dict(subs=subs, ntok=512, oc=4 * t * 128))
        subs = [dict(kind="p", np=128, col=0, slot=32, oc=32 * 128),
                dict(kind="d", np=32, col=128, slot=None, oc=DEC0)]
        tl.append(dict(subs=subs, ntok=160, oc=32 * 128))
        return tl

    def phase3(self):
        S = self.S
        tiles = self.tiles3()
        if self.dbg_tiles is not None:
            tiles = [dict(subs=tiles[0]["subs"][0:2], ntok=256, oc=0),
                     dict(subs=[dict(kind="d", np=32, col=0, slot=None, oc=DEC0)], ntok=32, oc=DEC0)]
        plan = []
        for t in tiles:
            plan.append(("bp", self.w_bf["bp"].rearrange("(g p) n -> p g n", p=128), (128, 4, 1024),
                         (0, 512, 0, 1024)))
            plan.append(("sb", self.w_bf["sb"].rearrange("(g p) n -> p g n", p=128), (128, 4, 1024),
                         (0, 512, 0, 1024)))
            for half in range(2):
                plan.append(self.wsrc_cols("in", 2048 + half * 512))
                plan.append(self.wsrc_cols("in", 3072 + half * 512))
            for hh in range(2):
                plan.append(self.wsrc_cols("o", hh * 512))
            for jg in range(8):
                plan.append(self.wsrc_cols("g2", jg * 512))
                plan.append(self.wsrc_cols("u2", jg * 512))
            for hh in range(2):
                for fg in range(4):
                    plan.append(self.wsrc_rows("d2", fg * 1024, 1024, hh * 512, 512))
        with ExitStack() as st:
            self.common_alloc(st, (2, 3), with_cf=False)
            self.wstream_init(st, plan)
            aTr = Ring([self.sb(st, f"aT{i}", [128, 4, 512], BF16) for i in range(2)])
            oTr = Ring([self.sb(st, f"oT{i}", [128, 4, 512], BF16) for i in range(2)])
            mg = self.sb(st, "mg", [128, 8, 512], BF16)
            mg_b = [Buf() for _ in range(8)]
            m1r = Ring([self.sb(st, f"m1_{i}", [128, 512], F32) for i in range(2)])
            m2r = Ring([self.sb(st, f"m2_{i}", [128, 512], F32) for i in range(2)])
            ytr = Ring([self.sb(st, f"yt{i}", [128, D], F32) for i in range(2)])
            def p3_loads(t, hT_t, hT_bufs):
                subs, ntok, oc0 = t["subs"], t["ntok"], t["oc"]
                xt, _ = self.xt.next()
                xb = self.xt_sb[self.xt.i]
                for si, sub in enumerate(subs):
                    np_, oc = sub["np"], sub["oc"]
                    self.load(xt[0:np_, si, :], self.x1s[oc:oc + np_, :], [xb[si]])
                self.load(hT_t[:, :, 0:ntok], self.hT2s[:, :, oc0:oc0 + ntok], hT_bufs)
                aT, aT_b = aTr.next()
                oT, oT_b = oTr.next()
                self.load(aT[:, :, 0:ntok], self.aTs[:, :, oc0:oc0 + ntok], [aT_b])
                self.load(oT[:, :, 0:ntok], self.OTs[:, :, oc0:oc0 + ntok].rearrange("j p n -> p j n"), [oT_b])
                return xt, xb, aT, aT_b, oT, oT_b

            cur = p3_loads(tiles[0], self.hT, self.hT_b)
            for ti, t in enumerate(tiles):
                subs, ntok, oc0 = t["subs"], t["ntok"], t["oc"]
                xt, xb, aT, aT_b, oT, oT_b = cur
                nxt_box = [None]

                def mid(ti=ti, nxt_box=nxt_box):
                    if ti + 1 < len(tiles):
                        o = self.hT_i ^ 1
                        nxt_box[0] = p3_loads(tiles[ti + 1], self.hT_ring[o], self.hT_bufs[o])
                BP, BPb = self.wget("bp")
                SBw, SBb = self.wget("sb")
                for half in range(2):
                    GA, GAb = self.wget("in")
                    GB, GBb = self.wget("in")
                    for cc in range(4):
                        c = half * 4 + cc
                        cs = slice(cc * 128, (cc + 1) * 128)
                        cg = slice(c * 128, (c + 1) * 128)
                        pga, pgab = self.pmm.next()
                        for kc in range(8):
                            self.mm(pga[:, 0:ntok], GA[:, kc, cs], self.hT[:, kc, 0:ntok], kc == 0, kc == 7,
                                    [GAb] + self.hT_b, [pgab])
                        pbp, pbpb = self.pmm.next()
                        for g in range(4):
                            self.mm(pbp[:, 0:ntok], BP[:, g, cg], aT[:, g, 0:ntok], g == 0, g == 3,
                                    [BPb, aT_b], [pbpb])
                        sa, sab = self.sgate.next()
                        self.act(sa[:, 0:ntok], pga[:, 0:ntok], AF.Sigmoid, [pgab], [sab])
                        m1, m1b = m1r.next()
                        self.tt("dve", m1[:, 0:ntok], sa[:, 0:ntok], pbp[:, 0:ntok], ALU.mult, [sab, pbpb], [m1b])
                        pgb, pgbb = self.pmm.next()
                        for kc in range(8):
                            self.mm(pgb[:, 0:ntok], GB[:, kc, cs], self.hT[:, kc, 0:ntok], kc == 0, kc == 7,
                                    [GBb] + self.hT_b, [pgbb])
                        psb, psbb = self.pmm.next()
                        for g in range(4):
                            self.mm(psb[:, 0:ntok], SBw[:, g, cg], oT[:, g, 0:ntok], g == 0, g == 3,
                                    [SBb, oT_b], [psbb])
                        sb2, sb2b = self.sgate.next()
                        self.act(sb2[:, 0:ntok], pgb[:, 0:ntok], AF.Sigmoid, [pgbb], [sb2b])
                        m2, m2b = m2r.next()
                        self.tt("dve", m2[:, 0:ntok], sb2[:, 0:ntok], psb[:, 0:ntok], ALU.mult, [sb2b, psbb], [m2b])
                        self.tt("pool", mg[:, c, 0:ntok], m1[:, 0:ntok], m2[:, 0:ntok], ALU.add, [m1b, m2b], [mg_b[c]])
                self.wrel()
                WO = [self.wget("o") for _ in range(2)]
                for si, sub in enumerate(subs):
                    np_, col = sub["np"], sub["col"]
                    for hh in range(2):
                        Wv, Wb = WO[hh]
                        pa, pab = self.pacc.next()
                        for c in range(8):
                            self.mm(pa[0:np_, :], mg[:, c, col:col + np_], Wv[:, c, :], c == 0, c == 7,
                                    [mg_b[c], Wb], [pab])
                        xs_ = xt[0:np_, si, hh * 512:(hh + 1) * 512]
                        self.tt("dve", xs_, pa[0:np_, :], xs_, ALU.add, [pab, xb[si]], [xb[si]])
                self.wrel()
                for si, sub in enumerate(subs):
                    self.rmsnorm_hT(xt[0:sub["np"], si, :], sub["np"], sub["col"], 0, xb[si], self.hT_b[si])
                self.ffn("g2", "u2", "d2", subs, ntok, xt, xb, mid=mid)
                cur = nxt_box[0]
                self.hT_swap()
                for si, sub in enumerate(subs):
                    np_, oc = sub["np"], sub["oc"]
                    xap = xt[0:np_, si, :]
                    k = self.stat_i % 8
                    self.stat_i += 1
                    ss = self.stat[0:np_, 2 * k:2 * k + 1]
                    rs = self.stat[0:np_, 2 * k + 1:2 * k + 2]
                    sb_ = self.stat_b[k]
                    self.act(self.sq[0:np_, :], xap, AF.Square, [xb[si]], [sb_], accum_out=ss)
                    self.act(ss, ss, AF.Sqrt, [sb_], [sb_], scale=1.0 / D, bias=1e-6)
                    S.op("dve", lambda e, rs=rs, ss=ss: e.reciprocal(out=rs, in_=ss), [sb_], [sb_])
                    yt, ytb = ytr.next()
                    self.stt("dve", yt[0:np_, :], xap, rs, self.gb[0:np_, 1, :], ALU.mult, ALU.mult,
                             [xb[si], sb_, self.gb_b], [ytb])
                    self.store(self.y_own[oc:oc + np_, :], yt[0:np_, :], [ytb])

def _consts():
    ident = np.eye(128, dtype=np.float32)
    J = ident[::-1].copy()
    q = np.arange(128)[:, None]
    c = np.arange(128)[None, :]
    M128 = np.where(c <= 127 - q, MASKV, 0.0).astype(np.float32)
    M32 = np.zeros((128, 32), np.float32)
    M32[0:32] = np.where(c[:, 0:32] <= 31 - q[0:32], MASKV, 0.0)
    c_bf = np.concatenate([ident, J, M128, M32], axis=1).astype(ml_dtypes.bfloat16)
    s = np.arange(128)[:, None].astype(np.float64)
    t = np.arange(128)[None, :].astype(np.float64)
    bc, bp, bs = [], [], []
    for w in WINS:
        bc.append(np.where((s > t - w) & (s <= t), 1.0 / w, 0.0) - (s == t))
        bp.append(np.where(s > t - w + 128, 1.0 / w, 0.0))
        cnt = np.minimum(w, t + 1)
        bs.append(np.where((s > t - w) & (s <= t), 1.0 / cnt, 0.0) - (s == t))
    bdc = np.zeros((128, 128))
    bdp = np.zeros((128, 128))
    s3 = np.arange(32)[:, None]
    t3 = np.arange(32)[None, :]
    r3 = np.arange(15)[:, None]
    for g, w in enumerate(WINS):
        bdc[0:32, g * 32:(g + 1) * 32] = np.where((s3 > t3 - w) & (s3 <= t3), 1.0 / w, 0.0) - (s3 == t3)
        bdp[0:15, g * 32:(g + 1) * 32] = np.where(r3 >= 16 + t3 - w, 1.0 / w, 0.0)
    c_f32 = np.concatenate(bc + bp + bs + [bdc, bdp], axis=1).astype(np.float32)
    return np.ascontiguousarray(c_bf), np.ascontiguousarray(c_f32)


_NC_CACHE = {}


def _get_nc():
    if "nc" not in _NC_CACHE:
        _NC_CACHE["nc"] = Builder().build()
    return _NC_CACHE["nc"]


def kernel(x_prompt, x_sample, cache_k, cache_v, state_pool,
           ffn1_norm, ffn1_gate, ffn1_up, ffn1_down, mix_norm, w_in, pool_w, pool_scale,
           w_branch_pool, w_branch_sb, w_out, ffn2_norm, ffn2_gate, ffn2_up, ffn2_down, final_norm):
    f = lambda a: np.ascontiguousarray(np.asarray(a, dtype=np.float32))
    x_prompt, x_sample = f(x_prompt), f(x_sample)
    cache_k, cache_v, state_pool = f(cache_k), f(cache_v), f(state_pool)
    c_bf, c_f32 = _consts()
    gains = np.stack([f(ffn1_norm)[0], f(mix_norm)[0], f(ffn2_norm)[0], f(final_norm)], axis=0)
    pscale = np.ascontiguousarray(f(pool_scale)[0].reshape(4, 128).T)
    shared = {
        "gains": np.ascontiguousarray(gains), "pscale": pscale, "poolw": f(pool_w)[0],
        "w_g1": f(ffn1_gate)[0], "w_u1": f(ffn1_up)[0], "w_d1": f(ffn1_down)[0], "w_in": f(w_in)[0],
        "w_bp": f(w_branch_pool)[0], "w_sb": f(w_branch_sb)[0], "w_o": f(w_out)[0],
        "w_g2": f(ffn2_gate)[0], "w_u2": f(ffn2_up)[0], "w_d2": f(ffn2_down)[0],
        "c_bf": c_bf, "c_f32": c_f32,
    }
    in_maps = []
    for c in range(8):
        b, par = c // 2, c % 2
        xv = np.zeros((NB * 128, D), np.float32)
        if par == 0:
            xv[0:8192] = x_prompt[b]
        else:
            xv[256:NB * 128] = x_prompt[b][0:NB * 128 - 256]
        m = dict(shared)
        m.update({"xv": xv, "xs": x_sample[c], "ck": cache_k[0, c], "cv": cache_v[0, c],
                  "spool": state_pool[0, c]})
        in_maps.append(m)
    nc = _get_nc()
    res = run_bass_kernel_spmd(nc, in_maps, core_ids=list(range(8)))
    R = res.results
    B, SEQ = 4, 8192
    y_prompt = np.zeros((B, SEQ, D), np.float32)
    y_sample = np.zeros((8, 32, D), np.float32)
    nkp = np.zeros((1, B, 8, SEQ, 64), np.float32)
    nvp = np.zeros((1, B, 8, SEQ, 64), np.float32)
    npp = np.zeros((1, B, 15, 512), np.float32)
    nks = np.zeros((1, 8, 8, 32, 64), np.float32)
    nvs = np.zeros((1, 8, 8, 32, 64), np.float32)
    nps = np.zeros((1, 8, 15, 512), np.float32)
    for c in range(8):
        b, par = c // 2, c % 2
        r = R[c]
        for k, vb in enumerate(OWN):
            rb = vb - 2 * par
            if rb < 0 or rb >= 64:
                continue
            y_prompt[b, rb * 128:(rb + 1) * 128] = r["y_own"][k * 128:(k + 1) * 128]
            nkp[0, b, :, rb * 128:(rb + 1) * 128] = r["nk_own"][:, k * 128:(k + 1) * 128]
            nvp[0, b, :, rb * 128:(rb + 1) * 128] = r["nv_own"][:, k * 128:(k + 1) * 128]
        y_sample[c] = r["y_own"][DEC0:DEC0 + 32]
        nks[0, c] = r["nk_own"][:, DEC0:DEC0 + 32]
        nvs[0, c] = r["nv_own"][:, DEC0:DEC0 + 32]
        nps[0, c] = r["pool_s"]
        if par == 0:
            npp[0, b] = r["pool_p"]
    return (y_prompt, y_sample, nkp, nvp, npp, nks, nvs, nps)
```
